# Optimizing a Trainium2 kernel written in Bass

```python
import math
import jax, jax.numpy as jnp
from jax import lax
import numpy as np

D_MODEL = 1024
BATCH = 8
SEQ = 2048
DEPTH = 1
DEC_BATCH = 128
DEC_SEQ = 4
PAST_LEN = 16384
PAGE_SIZE = 128

D_CONV = D_MODEL // 2
D_SSM = D_MODEL - D_CONV
D_MIX = D_CONV + D_SSM
CONV_WIDTH = 31
CONV_GROUP = 64
N_CONV_GROUPS = D_CONV // CONV_GROUP
SSM_GROUP = 16
N_SSM_GROUPS = D_SSM // SSM_GROUP
SSM_STATE = 64
N_MEM = 256
N_MEM_HEADS = 4
MEM_HEAD_DIM = D_MODEL // N_MEM_HEADS
D_FF = ((8 * D_MODEL + 3 * 256 - 1) // (3 * 256)) * 256
N_NORMS = 6
RMS_EPS = 1e-6
LN_EPS = 1e-5

kernel_name = "hymba_conformer_s5_memxattn_decode_step"


def rms_norm(x, g):
    xf = x.astype(jnp.float32)
    y = xf * lax.rsqrt(jnp.mean(xf * xf, axis=-1, keepdims=True) + RMS_EPS)
    return (y * g.astype(jnp.float32)).astype(x.dtype)


def layer_norm(x, g, b):
    xf = x.astype(jnp.float32)
    xc = xf - jnp.mean(xf, axis=-1, keepdims=True)
    var = jnp.mean(xc * xc, axis=-1, keepdims=True)
    y = xc * lax.rsqrt(var + LN_EPS) * g.astype(jnp.float32) + b.astype(jnp.float32)
    return y.astype(x.dtype)


def causal_depthwise_conv(v_ext, w_dw, b_dw):
    c = v_ext.shape[-1]
    out = lax.conv_general_dilated(
        v_ext, w_dw.astype(v_ext.dtype)[:, None, :], window_strides=(1,), padding="VALID",
        dimension_numbers=("NWC", "WIO", "NWC"), feature_group_count=c)
    return out + b_dw.astype(v_ext.dtype)


def _ssm_combine(left, right):
    a_l, b_l = left
    a_r, b_r = right
    return a_r * a_l, a_r * b_l + b_r


def s5_layer(u, h0, lam_re, lam_im, log_dt, b_re, b_im, c_re, c_im, d_skip, w_glu):
    f32 = jnp.float32
    bsz, t, _ = u.shape
    lam = lax.complex(lam_re.astype(f32), lam_im.astype(f32))
    dt = jnp.exp(log_dt.astype(f32))[:, None]
    a_bar = jnp.exp(lam * dt)
    b_mat = lax.complex(b_re.astype(f32), b_im.astype(f32))
    b_bar = ((a_bar - 1.0) / lam)[:, :, None] * b_mat
    c_mat = lax.complex(c_re.astype(f32), c_im.astype(f32))
    uf = u.astype(f32).reshape(bsz, t, N_SSM_GROUPS, SSM_GROUP)
    bu = jnp.einsum("btgp,gnp->btgn", uf.astype(jnp.complex64), b_bar)
    bu = bu.at[:, 0].add(a_bar * h0)
    a_seq = jnp.broadcast_to(a_bar, bu.shape)
    _, hs = lax.associative_scan(_ssm_combine, (a_seq, bu), axis=1)
    y = jnp.einsum("btgn,gpn->btgp", hs, c_mat).real
    y = y + d_skip.astype(f32).reshape(N_SSM_GROUPS, SSM_GROUP) * uf
    y = jax.nn.gelu(y.reshape(bsz, t, D_SSM))
    out = y * jax.nn.sigmoid(y @ w_glu.astype(f32))
    return out, hs[:, -1]


def token_mixer(h, conv_buf, ssm_h0, w_in, w_dw, b_dw, ln_g, ln_b, lam_re, lam_im, log_dt,
                b_re, b_im, c_re, c_im, d_skip, w_glu, w_out):
    z = h @ w_in
    a = z[..., :D_CONV]
    g = z[..., D_CONV:2 * D_CONV]
    u = z[..., 2 * D_CONV:]
    v = a * jax.nn.sigmoid(g)
    v_ext = jnp.concatenate([conv_buf.astype(v.dtype), v], axis=1)
    conv_new = v_ext[:, v_ext.shape[1] - (CONV_WIDTH - 1):]
    c = jax.nn.silu(layer_norm(causal_depthwise_conv(v_ext, w_dw, b_dw), ln_g, ln_b))
    s, h_last = s5_layer(u, ssm_h0, lam_re, lam_im, log_dt, b_re, b_im, c_re, c_im, d_skip, w_glu)
    mix = jnp.concatenate([c, s.astype(c.dtype)], axis=-1)
    return mix @ w_out, conv_new, h_last


def mem_kv(mem, g_mem, w_k, w_v):
    bsz = mem.shape[0]
    m = rms_norm(mem, g_mem)
    k = (m @ w_k).reshape(bsz, N_MEM, N_MEM_HEADS, MEM_HEAD_DIM)
    v = (m @ w_v).reshape(bsz, N_MEM, N_MEM_HEADS, MEM_HEAD_DIM)
    return k, v


def mem_attend(h, mem_k, mem_v, w_q, w_o):
    bsz, t, _ = h.shape
    q = (h @ w_q).reshape(bsz, t, N_MEM_HEADS, MEM_HEAD_DIM).astype(jnp.float32)
    s = jnp.einsum("bthd,bmhd->bhtm", q, mem_k.astype(jnp.float32)) * (MEM_HEAD_DIM ** -0.5)
    p = jax.nn.softmax(s, axis=-1)
    o = jnp.einsum("bhtm,bmhd->bthd", p, mem_v.astype(jnp.float32))
    return o.astype(h.dtype).reshape(bsz, t, D_MODEL) @ w_o


def swiglu(h, w_gate, w_up, w_down):
    return (jax.nn.silu(h @ w_gate) * (h @ w_up)) @ w_down


def block(x, mem_k, mem_v, conv_buf, ssm_h0, norm_g, mix_w, attn_w, ffn_w):
    h = rms_norm(x, norm_g[0])
    m, conv_new, h_last = token_mixer(h, conv_buf, ssm_h0, *mix_w)
    x = x + rms_norm(m, norm_g[1])
    h = rms_norm(x, norm_g[2])
    x = x + rms_norm(mem_attend(h, mem_k, mem_v, *attn_w), norm_g[3])
    h = rms_norm(x, norm_g[4])
    x = x + rms_norm(swiglu(h, *ffn_w), norm_g[5])
    return x, conv_new, h_last


def setup_inputs(seed: int = 0) -> dict:
    key = jax.random.key(seed)
    ks = jax.random.split(key, 40)
    f32 = jnp.float32
    nrm = lambda k, shape, s: s * jax.random.normal(k, shape, f32)
    L = DEPTH
    lam_im_base = math.pi * jnp.arange(SSM_STATE, dtype=f32)
    return {
        "x_prompt": nrm(ks[0], (BATCH, SEQ, D_MODEL), 1.0),
        "x_sample": nrm(ks[1], (DEC_BATCH, DEC_SEQ, D_MODEL), 1.0),
        "mem_prompt": nrm(ks[2], (BATCH, N_MEM, D_MODEL), 1.0),
        "cache_mem_k": nrm(ks[3], (L, DEC_BATCH, N_MEM, N_MEM_HEADS, MEM_HEAD_DIM), 1.0),
        "cache_mem_v": nrm(ks[4], (L, DEC_BATCH, N_MEM, N_MEM_HEADS, MEM_HEAD_DIM), 1.0),
        "state_conv": nrm(ks[5], (L, DEC_BATCH, CONV_WIDTH - 1, D_CONV), 0.5),
        "state_ssm_re": nrm(ks[6], (L, DEC_BATCH, N_SSM_GROUPS, SSM_STATE), 0.1),
        "state_ssm_im": nrm(ks[7], (L, DEC_BATCH, N_SSM_GROUPS, SSM_STATE), 0.1),
        "norm_g": 1.0 + nrm(ks[8], (L, N_NORMS, D_MODEL), 0.05),
        "mem_norm_g": 1.0 + nrm(ks[9], (L, D_MODEL), 0.05),
        "w_in": nrm(ks[10], (L, D_MODEL, 2 * D_CONV + D_SSM), D_MODEL ** -0.5),
        "w_dw": nrm(ks[11], (L, CONV_WIDTH, D_CONV), CONV_WIDTH ** -0.5),
        "b_dw": nrm(ks[12], (L, D_CONV), 0.01),
        "ln_g": 1.0 + nrm(ks[13], (L, D_CONV), 0.05),
        "ln_b": nrm(ks[14], (L, D_CONV), 0.01),
        "lam_re": -0.5 + nrm(ks[15], (L, N_SSM_GROUPS, SSM_STATE), 0.01),
        "lam_im": lam_im_base + nrm(ks[16], (L, N_SSM_GROUPS, SSM_STATE), 0.01),
        "log_dt": jax.random.uniform(ks[17], (L, N_SSM_GROUPS), f32, math.log(1e-3), math.log(1e-1)),
        "b_re": nrm(ks[18], (L, N_SSM_GROUPS, SSM_STATE, SSM_GROUP), (2 * SSM_GROUP) ** -0.5),
        "b_im": nrm(ks[19], (L, N_SSM_GROUPS, SSM_STATE, SSM_GROUP), (2 * SSM_GROUP) ** -0.5),
        "c_re": nrm(ks[20], (L, N_SSM_GROUPS, SSM_GROUP, SSM_STATE), (2 * SSM_STATE) ** -0.5),
        "c_im": nrm(ks[21], (L, N_SSM_GROUPS, SSM_GROUP, SSM_STATE), (2 * SSM_STATE) ** -0.5),
        "d_skip": nrm(ks[22], (L, D_SSM), 1.0),
        "w_glu": nrm(ks[23], (L, D_SSM, D_SSM), D_SSM ** -0.5),
        "w_out": nrm(ks[24], (L, D_MIX, D_MODEL), D_MIX ** -0.5),
        "w_q": nrm(ks[25], (L, D_MODEL, D_MODEL), D_MODEL ** -0.5),
        "w_k": nrm(ks[26], (L, D_MODEL, D_MODEL), D_MODEL ** -0.5),
        "w_v": nrm(ks[27], (L, D_MODEL, D_MODEL), D_MODEL ** -0.5),
        "w_o": nrm(ks[28], (L, D_MODEL, D_MODEL), D_MODEL ** -0.5),
        "w_gate": nrm(ks[29], (L, D_MODEL, D_FF), D_MODEL ** -0.5),
        "w_up": nrm(ks[30], (L, D_MODEL, D_FF), D_MODEL ** -0.5),
        "w_down": nrm(ks[31], (L, D_FF, D_MODEL), D_FF ** -0.5),
    }


def reference(x_prompt, x_sample, mem_prompt, cache_mem_k, cache_mem_v, state_conv,
              state_ssm_re, state_ssm_im, norm_g, mem_norm_g, w_in, w_dw, b_dw, ln_g, ln_b,
              lam_re, lam_im, log_dt, b_re, b_im, c_re, c_im, d_skip, w_glu, w_out,
              w_q, w_k, w_v, w_o, w_gate, w_up, w_down):
    f32 = jnp.float32
    yp, ys = x_prompt, x_sample
    kps, vps, cps, hps_re, hps_im, css, hss_re, hss_im = [], [], [], [], [], [], [], []
    for l in range(DEPTH):
        mix_w = (w_in[l], w_dw[l], b_dw[l], ln_g[l], ln_b[l], lam_re[l], lam_im[l], log_dt[l],
                 b_re[l], b_im[l], c_re[l], c_im[l], d_skip[l], w_glu[l], w_out[l])
        attn_w = (w_q[l], w_o[l])
        ffn_w = (w_gate[l], w_up[l], w_down[l])
        kp, vp = mem_kv(mem_prompt, mem_norm_g[l], w_k[l], w_v[l])
        conv0 = jnp.zeros((yp.shape[0], CONV_WIDTH - 1, D_CONV), yp.dtype)
        h0 = jnp.zeros((yp.shape[0], N_SSM_GROUPS, SSM_STATE), jnp.complex64)
        yp, cp, hp = block(yp, kp, vp, conv0, h0, norm_g[l], mix_w, attn_w, ffn_w)
        hs0 = lax.complex(state_ssm_re[l].astype(f32), state_ssm_im[l].astype(f32))
        ys, cs, hs = block(ys, cache_mem_k[l], cache_mem_v[l], state_conv[l], hs0,
                           norm_g[l], mix_w, attn_w, ffn_w)
        kps.append(kp)
        vps.append(vp)
        cps.append(cp)
        hps_re.append(hp.real)
        hps_im.append(hp.imag)
        css.append(cs)
        hss_re.append(hs.real)
        hss_im.append(hs.imag)
    return (yp, ys, jnp.stack(kps), jnp.stack(vps), jnp.stack(cps), jnp.stack(hps_re),
            jnp.stack(hps_im), jnp.stack(css), jnp.stack(hss_re), jnp.stack(hss_im))
```

```python
import contextlib
import math
import numpy as np
import concourse.bass as bass
import concourse.mybir as mybir
from concourse.bass_utils import run_bass_kernel_spmd

F32 = mybir.dt.float32
BF16 = mybir.dt.bfloat16
I32 = mybir.dt.int32
ALU = mybir.AluOpType
AF = mybir.ActivationFunctionType

ENGS = ("pe", "act", "dve", "pool", "sp")
NCORES = 8
T = 2048
NS = 16
TS = 64
D = 1024
DFF = 2816
NFF = 22
LCH = 128
RMS_EPS = 1e-6
LN_EPS = 1e-5


class Prog:
    def __init__(self, nc, same_engine_sync=True):
        self.nc = nc
        self.ins = []
        self.last_w = {}
        self.readers = {}
        self.dma_keys = {}
        self.same_engine_sync = same_engine_sync

    def _add(self, eng, fn, reads, writes, dma_key=None, extra_deps=()):
        i = len(self.ins)
        deps = set(extra_deps)
        for k in reads:
            if k in self.last_w:
                deps.add(self.last_w[k])
            if k.startswith("ps") and k in self.readers:
                for e2, j in self.readers[k][0].items():
                    if e2 != eng:
                        deps.add(j)
        for k in writes:
            if k in self.last_w:
                deps.add(self.last_w[k])
            rd = self.readers.get(k)
            if rd:
                deps.update(rd[0].values())
                deps.update(rd[1])
        deps.discard(i)
        rec = dict(eng=eng, fn=fn, deps=deps, dma_key=dma_key, sig=False, cnt=None)
        if dma_key is not None:
            self.dma_keys[dma_key] = self.dma_keys.get(dma_key, 0) + 1
            rec["cnt"] = self.dma_keys[dma_key]
            rec["sig"] = True
        self.ins.append(rec)
        for k in writes:
            self.last_w[k] = i
            self.readers[k] = [{}, []]
        for k in reads:
            rd = self.readers.setdefault(k, [{}, []])
            if dma_key is None:
                rd[0][eng] = i
            else:
                rd[1].append(i)
        return i

    def op(self, eng, fn, reads=(), writes=()):
        return self._add(eng, fn, tuple(reads), tuple(writes))

    def dma(self, eng, out, in_, key=None, reads=(), writes=(), **kw):
        if key is None:
            key = "k_" + writes[0]

        def fn(e):
            return e.dma_start(out=out, in_=in_, **kw)
        return self._add(eng, fn, tuple(reads), tuple(writes), dma_key=key)

    def barrier(self):
        last = {}
        for i, rec in enumerate(self.ins):
            if rec["dma_key"] is None:
                last[("e", rec["eng"])] = i
            else:
                last[("d", rec["dma_key"])] = i
        deps = set(last.values())
        for e in ENGS:
            self._add(e, lambda eng: eng.nop(), (), (), extra_deps=deps)

    def emit(self, final_wait_eng="sp"):
        nc = self.nc
        ins = self.ins
        for rec in ins:
            for d in rec["deps"]:
                p = ins[d]
                if p["dma_key"] is None:
                    if p["eng"] != rec["eng"] or (self.same_engine_sync and p["eng"] != "pe"):
                        p["sig"] = True
        cnt = {e: 0 for e in ENGS}
        for rec in ins:
            if rec["dma_key"] is None and rec["sig"]:
                cnt[rec["eng"]] += 1
                rec["cnt"] = cnt[rec["eng"]]
        with contextlib.ExitStack() as st:
            esem = {e: st.enter_context(nc.semaphore("s_" + e)) for e in ENGS}
            dsem = {k: st.enter_context(nc.semaphore("d_" + str(k))) for k in self.dma_keys}
            block = st.enter_context(nc.Block())

            def run_engine(ename, eng):
                waited = {}
                for rec in ins:
                    if rec["eng"] != ename:
                        continue
                    need = {}
                    for d in rec["deps"]:
                        p = ins[d]
                        if p["dma_key"] is not None:
                            sem, val = dsem[p["dma_key"]], 16 * p["cnt"]
                            skey = ("d", p["dma_key"])
                        else:
                            if p["eng"] == ename and (ename == "pe" or not self.same_engine_sync):
                                continue
                            sem, val = esem[p["eng"]], p["cnt"]
                            skey = ("e", p["eng"])
                        if skey not in need or need[skey][1] < val:
                            need[skey] = (sem, val)
                    for skey in sorted(need, key=str):
                        sem, val = need[skey]
                        if waited.get(skey, 0) >= val:
                            continue
                        waited[skey] = val
                        eng.wait_ge(sem, val)
                    bi = rec["fn"](eng)
                    if rec["sig"]:
                        if rec["dma_key"] is not None:
                            bi.then_inc(dsem[rec["dma_key"]], 16)
                        else:
                            bi.then_inc(esem[ename], 1)
                if ename == final_wait_eng:
                    for k, n in self.dma_keys.items():
                        eng.wait_ge(dsem[k], 16 * n)

            @block.tensor
            def _(e):
                run_engine("pe", e)

            @block.scalar
            def _(e):
                run_engine("act", e)

            @block.vector
            def _(e):
                run_engine("dve", e)

            @block.gpsimd
            def _(e):
                run_engine("pool", e)

            @block.sync
            def _(e):
                run_engine("sp", e)


def build(debug_names=(), stop=99):
    nc = bass.Bass("TRN2", target_bir_lowering=False)

    def finish():
        print("SBUF high-water words:", hiw[0], "instructions:", len(P.ins))
        P.emit()
        nc_allow.__exit__(None, None, None)
        return nc

    def din(name, shape):
        return nc.dram_tensor(name, list(shape), F32, kind="ExternalInput").ap()

    def dout(name, shape):
        return nc.dram_tensor(name, list(shape), F32, kind="ExternalOutput").ap()

    xp = din("xp", [T, D]); xs = din("xs", [TS, D]); mem = din("mem", [256, D])
    ck = din("ck", [NS, 256, D]); cv = din("cv", [NS, 256, D])
    sconv = din("sconv", [NS, 30, 512]); sre = din("sre", [NS, 2048]); sim = din("sim", [NS, 2048])
    norm_g = din("norm_g", [6, D]); mem_g = din("mem_g", [D])
    w_in = din("w_in", [D, 1536]); w_dw = din("w_dw", [31, 512])
    b_dw = din("b_dw", [512]); ln_g = din("ln_g", [512]); ln_b = din("ln_b", [512])
    lam_re = din("lam_re", [32, 64]); lam_im = din("lam_im", [32, 64]); log_dt = din("log_dt", [32])
    b_re = din("b_re", [32, 64, 16]); b_im = din("b_im", [32, 64, 16])
    c_re = din("c_re", [32, 16, 64]); c_im = din("c_im", [32, 16, 64])
    d_skip = din("d_skip", [512]); w_glu = din("w_glu", [512, 512]); w_out = din("w_out", [D, D])
    w_q = din("w_q", [D, D]); w_k = din("w_k", [D, D]); w_v = din("w_v", [D, D]); w_o = din("w_o", [D, D])
    w_gate = din("w_gate", [D, DFF]); w_up = din("w_up", [D, DFF]); w_down = din("w_down", [DFF, D])

    yp = dout("yp", [T, D]); ys = dout("ys", [TS, D]); kp = dout("kp", [256, D]); vp = dout("vp", [256, D])
    cp = dout("cp", [30, 512]); hpr = dout("hpr", [16, 128]); hpi = dout("hpi", [16, 128])
    cs = dout("cs", [NS, 30, 512]); hsr = dout("hsr", [NS, 2048]); hsi = dout("hsi", [NS, 2048])

    AW = 53200
    A = nc.alloc_sbuf_tensor("arena", [128, AW], F32).ap()
    pos = [0]
    hiw = [0]

    def carve_at(off, n, dtype=F32):
        nb = n * (2 if dtype == BF16 else 4)
        nw = ((nb + 3) // 4 + 7) // 8 * 8
        v = A[:, off:off + nw]
        if dtype != F32:
            v = v.bitcast(dtype)
        return v[:, 0:n], nw

    def carve(n, dtype=F32):
        v, nw = carve_at(pos[0], n, dtype)
        assert pos[0] + nw <= AW, ("SBUF arena overflow", pos[0], nw)
        pos[0] += nw
        hiw[0] = max(hiw[0], pos[0])
        return v

    PS = nc.alloc_psum_tensor("ps", [128, 4096], F32).ap()
    psb = [PS[:, i * 512:(i + 1) * 512] for i in range(8)]
    pk = ["ps%d" % i for i in range(8)]
    bpool = [list(range(8))]
    bank_rr = [0]

    bank_cnt = {}

    def bank():
        pl = bpool[0]
        k = tuple(pl)
        c = bank_cnt.get(k, 0)
        bank_cnt[k] = c + 1
        return pl[c % len(pl)]

    P = Prog(nc)
    nc_allow = nc.allow_non_contiguous_dma(reason="small strided parameter loads")
    nc_allow.__enter__()

    def dbg(name, ap, shape, reads):
        if name in debug_names:
            o = dout("dbg_" + name, shape)
            P.dma("sp", o, ap, key="dbg_" + name, reads=reads)

    idi = carve(128, I32); idf = carve(128); idb = carve(128, BF16)
    onesf = carve(128); onesb = carve(128, BF16)
    P.op("pool", lambda e: e.iota(idi, [[1, 128]], base=0, channel_multiplier=-1), writes=["idi"])
    P.op("dve", lambda e: e.tensor_single_scalar(idf, idi, 0, ALU.is_equal), reads=["idi"], writes=["idf"])
    P.op("dve", lambda e: e.tensor_copy(idb, idf), reads=["idf"], writes=["idb"])
    P.op("dve", lambda e: e.memset(onesf, 1.0 / 512.0), writes=["onesf"])
    P.op("dve", lambda e: e.memset(onesb, 1.0), writes=["onesb"])
    cst = carve(8)
    P.op("dve", lambda e: e.memset(cst[:, 0:1], -0.5), writes=["cst"])
    negh = cst[:, 0:1]
    mk = carve(4)
    for r in range(4):
        P.op("dve", lambda e, r=r: e.reduce_sum(mk[:, r:r + 1], idf[:, 32 * r:32 * r + 32], axis=mybir.AxisListType.X),
             reads=["idf", "mk"], writes=["mk"])
    gcol = carve(4 * 8)
    gcolv = gcol.rearrange("p (a k) -> p a k", a=4)
    for a, src in enumerate([norm_g[0], norm_g[2], norm_g[4], mem_g]):
        P.dma("sp", gcolv[:, a, :], src.rearrange("(k p) -> p k", p=128), writes=["gcol"])
    pcol = carve(16)
    pcolv = pcol.rearrange("p (a k) -> p a k", a=4)
    for a, src in enumerate([b_dw, ln_g, ln_b, d_skip]):
        P.dma("sp", pcolv[:, a, :], src.rearrange("(k p) -> p k", p=128), writes=["pcol"])
    gbc = carve(D)
    def load_gbc(r):
        P.dma("sp", gbc, bass.AP(norm_g.tensor, r * D, [[0, 128], [1, D]]), writes=["gbc"])
    xres_off = pos[0]
    xres = carve(17 * D)
    xresv = xres.rearrange("p (t n) -> p t n", t=17)
    persist_mark = pos[0]

    def load_w(dst3, wdram, nkt, key):
        P.dma("pool", dst3, wdram.rearrange("(k p) n -> p k n", p=128), writes=[key])

    def rstd_from_ss(ss, n, scale, eps, tag):
        P.op("dve", lambda e: e.tensor_scalar(ss[:n], ss[:n], scale, eps, ALU.mult, ALU.add),
             reads=[tag], writes=[tag])
        P.op("pool", lambda e: e.tensor_tensor(ss[:n], ss[:n], negh[:n], ALU.pow),
             reads=[tag, "cst"], writes=[tag])

    def norm_to_hT(xt, n, gi, hT3, c0, wk):
        norm_pre(xt, n, wk)
        norm_T(n, gi, hT3, c0, wk)

    def norm_pre(xt, n, wk):
        ss, junk, xn = wk["ss"], wk["junk"], wk["xn"]
        P.op("act", lambda e: e.activation(junk[:n], xt, AF.Square, accum_out=ss[:n]),
             reads=wk["xkeys"] + ["xn"], writes=["xn", "ss"])
        rstd_from_ss(ss, n, 1.0 / D, RMS_EPS, "ss")
        P.op("dve", lambda e: e.tensor_scalar(xn[:n], xt, ss[:n], None, ALU.mult),
             reads=wk["xkeys"] + ["ss"], writes=["xn"])

    def norm_T(n, gi, hT3, c0, wk):
        xn = wk["xn"]
        hkeys = list(wk["hkeys"])
        b = bank()
        pt = psb[b].bitcast(BF16)
        for kt in range(8):
            P.op("pe", lambda e, kt=kt: e.transpose(pt[:, kt * 128:kt * 128 + n], xn[:n, kt * 128:(kt + 1) * 128],
                                                    idb[:n, :n]),
                 reads=["xn", "idb"], writes=[pk[b]])
        ptv = pt.rearrange("p (k t) -> p k t", k=8)
        P.op("dve", lambda e: e.tensor_tensor(hT3[:, :, c0:c0 + n], ptv[:, :, 0:n],
                                              gcolv[:, gi, :].unsqueeze(2).to_broadcast([128, 8, n]), ALU.mult),
             reads=[pk[b], "gcol"] + hkeys, writes=hkeys)

    def proj_fm(W3, nkt, ncol0, rhs3, c0, n, rkeys, wkey):
        b = bank()
        for kt in range(nkt):
            P.op("pe", lambda e, kt=kt: e.matmul(psb[b][:, 0:n], W3[:, kt, ncol0:ncol0 + 128], rhs3[:, kt, c0:c0 + n],
                                                 start=(kt == 0), stop=(kt == nkt - 1)),
                 reads=rkeys + [wkey], writes=[pk[b]])
        return b

    def proj_tm_postnorm_residual(lhs3, nkt, c0, n, W3, wkey, lkeys, gi, xt, xkey, wk, out_dram=None, okey=None, mid_hook=None):
        b0, b1 = bank(), bank()
        for h, b in enumerate((b0, b1)):
            for kt in range(nkt):
                P.op("pe", lambda e, kt=kt, b=b, h=h: e.matmul(psb[b][:n, :], lhs3[:, kt, c0:c0 + n],
                                                                W3[:, kt, h * 512:(h + 1) * 512],
                                                                start=(kt == 0), stop=(kt == nkt - 1)),
                     reads=lkeys + [wkey], writes=[pk[b]])
        if mid_hook is not None:
            mid_hook()
        z = wk["pp"] = 1 - wk.get("pp", 0)
        ss2, junk, tmp = wk["ss2"][z], wk["pjunk"][z], wk["tmp"][z]
        ks, kj, kt_ = "ss2_%d" % z, "pjunk%d" % z, "tmp%d" % z
        P.op("act", lambda e: e.activation(junk[:n, 0:512], psb[b0][:n, :], AF.Square, accum_out=ss2[:n, 0:1]),
             reads=[pk[b0], kj, ks], writes=[kj, ks])
        P.op("act", lambda e: e.activation(junk[:n, 512:1024], psb[b1][:n, :], AF.Square, accum_out=ss2[:n, 1:2]),
             reads=[pk[b1], kj, ks], writes=[kj, ks])
        P.op("dve", lambda e: e.tensor_tensor(ss2[:n, 2:3], ss2[:n, 0:1], ss2[:n, 1:2], ALU.add),
             reads=[ks], writes=[ks])
        rstd_from_ss(ss2[:, 2:3], n, 1.0 / D, RMS_EPS, ks)
        for h, b in enumerate((b0, b1)):
            P.op("dve", lambda e, h=h, b=b: e.scalar_tensor_tensor(tmp[:n, h * 512:(h + 1) * 512], psb[b][:n, :],
                                                                    ss2[:n, 2:3], gbc[:n, h * 512:(h + 1) * 512],
                                                                    ALU.mult, ALU.mult),
                 reads=[pk[b], ks, "gbc", kt_], writes=[kt_])
        P.op("dve", lambda e: e.tensor_tensor(xt, xt, tmp[:n, :], ALU.add), reads=[kt_, xkey], writes=[xkey])
        if out_dram is not None:
            P.dma("sp", out_dram, xt, key=okey, reads=[xkey])

    xo = [xres_off]

    def carve_x(n, dtype=F32):
        v, nw = carve_at(xo[0], n, dtype)
        xo[0] += nw
        assert xo[0] <= xres_off + 16 * D
        return v

    diag = carve_x(4 * 31 * 128, BF16)
    diagv = diag.rearrange("p (c k j) -> p c k j", c=4, k=31)
    LD = 64
    TC = carve_x(16 * LD); TSn = carve_x(16 * LD)
    TCv = TC.rearrange("p (t j) -> p t j", t=16); TSv = TSn.rearrange("p (t j) -> p t j", t=16)
    mix = carve(8 * (T + TS), BF16)
    mixv = mix.rearrange("p (k t) -> p k t", k=8)
    Bc = carve_x(4 * 2 * 4 * 128, BF16)
    Bcv = Bc.rearrange("p (s r k n) -> p s r k n", s=4, r=2, k=4)
    Cc = carve_x(4 * 2 * 16 * 32, BF16)
    Ccv = Cc.rearrange("p (k r t c) -> p k r t c", k=4, r=2, t=16)
    KT = carve_x(3 * 4 * 128, BF16)
    KTv = KT.rearrange("p (a k n) -> p a k n", a=3, k=4)
    pw = carve(16 * 8)
    pwv = pw.rearrange("p (a t) -> p a t", a=8)
    wdw = carve(4 * 32)
    wdwv = wdw.rearrange("p (c k) -> p c k", c=4)
    sp_ = carve(16 * 8)
    spv = sp_.rearrange("p (a t) -> p a t", a=8)
    Hst = carve(2 * 16)
    Hv = Hst.rearrange("p (r t) -> p r t", r=2)
    p1b_mark = pos[0]
    Win = carve(8 * 1536, BF16); Winv = Win.rearrange("p (k n) -> p k n", k=8)
    Wglu = carve(4 * 512, BF16); Wgluv = Wglu.rearrange("p (k n) -> p k n", k=4)
    load_w(Winv, w_in, 8, "Win")
    load_w(Wgluv, w_glu, 4, "Wglu")
    phase0_mark = pos[0]

    wdn = carve(512)
    P.dma("sp", wdn[:31, :], w_dw, writes=["wdn"])
    b = bank()
    for c in range(4):
        P.op("pe", lambda e, c=c, b=b: e.transpose(psb[b][:, c * 32:c * 32 + 31], wdn[:31, c * 128:(c + 1) * 128], idf[:31, :31]),
             reads=["wdn", "idf"], writes=[pk[b]])
    P.op("dve", lambda e, b=b: e.tensor_copy(wdwv[:, :, 0:31], psb[b][:, 0:128].rearrange("p (c k) -> p c k", c=4)[:, :, 0:31]),
         reads=[pk[b]], writes=["wdw"])
    for c in range(4):
        for k in range(31):
            if (c * 31 + k) % 2 == 0:
                P.op("dve", lambda e, c=c, k=k: e.tensor_scalar(diagv[:, c, k, :], idf, wdwv[:, c, k:k + 1], None, ALU.mult),
                     reads=["idf", "wdw"], writes=["diagA%d" % (k % 4)])
            else:
                P.op("act", lambda e, c=c, k=k: e.activation(diagv[:, c, k, :], idf, AF.Copy, scale=wdwv[:, c, k:k + 1]),
                     reads=["idf", "wdw"], writes=["diagB%d" % (k % 4)])

    t16 = carve(16 * 16)
    tv = t16.rearrange("p (a t) -> p a t", a=16)
    LR, LI, LDT = tv[:, 0, :], tv[:, 1, :], tv[:, 2, :]
    P.dma("sp", LR, lam_re.rearrange("(t g) n -> (g n) t", g=2), writes=["t16"])
    P.dma("sp", LI, lam_im.rearrange("(t g) n -> (g n) t", g=2), writes=["t16"])
    for g2 in range(2):
        P.dma("sp", tv[g2 * 64:(g2 + 1) * 64, 2, :], bass.AP(log_dt.tensor, g2, [[0, 64], [2, 16]]), writes=["t16"])
    K16 = ["t16"]

    def tt(out, a, b_, op, eng="dve", r=K16, w=K16):
        P.op(eng, lambda e: e.tensor_tensor(out, a, b_, op), reads=r, writes=w)

    def ts(out, a, s1, s2, op0, op1=None, eng="dve", r=K16, w=K16):
        if op1 is None:
            P.op(eng, lambda e: e.tensor_scalar(out, a, s1, None, op0), reads=r, writes=w)
        else:
            P.op(eng, lambda e: e.tensor_scalar(out, a, s1, s2, op0, op1), reads=r, writes=w)

    dt_, ere, th, sh, ch_, sn, x_, den = (tv[:, i, :] for i in range(3, 11))
    KS = ["t16", "sp"]
    P.op("act", lambda e: e.activation(dt_, LDT, AF.Exp), reads=K16, writes=K16)
    tt(ere, LR, dt_, ALU.mult)
    P.op("act", lambda e: e.activation(spv[:, 0, :], ere, AF.Exp), reads=KS, writes=KS)
    tt(th, LI, dt_, ALU.mult)
    NDBL = 4
    P.op("act", lambda e: e.activation(sh, th, AF.Sin, scale=1.0 / (2 ** (NDBL + 1))), reads=K16, writes=K16)
    P.op("act", lambda e: e.activation(sn, th, AF.Sin, scale=1.0 / (2 ** NDBL)), reads=K16, writes=K16)
    tt(ch_, sh, sh, ALU.mult)
    ts(ch_, ch_, -2.0, 1.0, ALU.mult, ALU.add)
    for _ in range(NDBL):
        tt(x_, ch_, sn, ALU.mult)
        tt(den, sn, sn, ALU.mult)
        ts(sn, x_, 2.0, None, ALU.mult)
        ts(ch_, den, -2.0, 1.0, ALU.mult, ALU.add)
    P.op("dve", lambda e: e.tensor_copy(spv[:, 5, :], ch_), reads=KS, writes=KS)
    P.op("dve", lambda e: e.tensor_copy(spv[:, 6, :], sn), reads=KS, writes=KS)
    tt(spv[:, 1, :], spv[:, 0, :], ch_, ALU.mult, r=KS, w=KS)
    tt(spv[:, 2, :], spv[:, 0, :], sn, ALU.mult, r=KS, w=KS)
    ts(x_, spv[:, 1, :], -1.0, None, ALU.add, r=KS)
    y_ = spv[:, 2, :]
    tt(den, LR, LR, ALU.mult)
    tt(sh, LI, LI, ALU.mult)
    tt(den, den, sh, ALU.add)
    P.op("dve", lambda e: e.reciprocal(den, den), reads=K16, writes=K16)
    tt(sh, x_, LR, ALU.mult)
    tt(th, y_, LI, ALU.mult, r=KS)
    tt(sh, sh, th, ALU.add)
    tt(spv[:, 3, :], sh, den, ALU.mult, r=KS, w=KS)
    tt(sh, y_, LR, ALU.mult, r=KS)
    tt(th, x_, LI, ALU.mult)
    tt(sh, sh, th, ALU.subtract)
    tt(spv[:, 4, :], sh, den, ALU.mult, r=KS, w=KS)

    KP = ["sp", "pw"]
    def ptt(out, a_, b_, op):
        P.op("dve", lambda e: e.tensor_tensor(out, a_, b_, op), reads=KP, writes=KP)
    def pts(out, a_, s1, s2, op0, op1):
        P.op("dve", lambda e: e.tensor_scalar(out, a_, s1, s2, op0, op1), reads=KP, writes=KP)
    ar_, ai_, tmp_ = spv[:, 1, :], spv[:, 2, :], pwv[:, 7, :]
    ptt(pwv[:, 0, :], ar_, ar_, ALU.mult); ptt(tmp_, ai_, ai_, ALU.mult); ptt(pwv[:, 0, :], pwv[:, 0, :], tmp_, ALU.subtract)
    ptt(pwv[:, 1, :], ar_, ai_, ALU.mult); pts(pwv[:, 1, :], pwv[:, 1, :], 2.0, 0.0, ALU.mult, ALU.add)
    ptt(pwv[:, 2, :], pwv[:, 0, :], ar_, ALU.mult); ptt(tmp_, pwv[:, 1, :], ai_, ALU.mult); ptt(pwv[:, 2, :], pwv[:, 2, :], tmp_, ALU.subtract)
    ptt(pwv[:, 3, :], pwv[:, 0, :], ai_, ALU.mult); ptt(tmp_, pwv[:, 1, :], ar_, ALU.mult); ptt(pwv[:, 3, :], pwv[:, 3, :], tmp_, ALU.add)
    ptt(pwv[:, 4, :], spv[:, 0, :], spv[:, 0, :], ALU.mult); ptt(pwv[:, 4, :], pwv[:, 4, :], pwv[:, 4, :], ALU.mult)
    ptt(pwv[:, 5, :], spv[:, 6, :], spv[:, 6, :], ALU.mult); pts(pwv[:, 5, :], pwv[:, 5, :], -2.0, 1.0, ALU.mult, ALU.add)
    ptt(pwv[:, 6, :], spv[:, 5, :], spv[:, 6, :], ALU.mult); pts(pwv[:, 6, :], pwv[:, 6, :], 2.0, 0.0, ALU.mult, ALU.add)
    ptt(tmp_, pwv[:, 5, :], pwv[:, 6, :], ALU.mult)
    ptt(pwv[:, 5, :], pwv[:, 6, :], pwv[:, 6, :], ALU.mult); pts(pwv[:, 5, :], pwv[:, 5, :], -2.0, 1.0, ALU.mult, ALU.add)
    pts(pwv[:, 6, :], tmp_, 2.0, 0.0, ALU.mult, ALU.add)
    tmpA = carve(16 * (LD // 2)); tmpB = carve(16 * (LD // 2))
    tAv = tmpA.rearrange("p (t j) -> p t j", t=16); tBv = tmpB.rearrange("p (t j) -> p t j", t=16)
    KTB = ["TC", "TS", "tmpAB"]
    P.op("dve", lambda e: e.tensor_copy(TCv[:, :, 0], pwv[:, 5, :]), reads=["pw"] + KTB, writes=KTB)
    P.op("dve", lambda e: e.tensor_copy(TSv[:, :, 0], pwv[:, 6, :]), reads=["pw"] + KTB, writes=KTB)
    n_ = 1
    while n_ < LD:
        cn = TCv[:, :, n_ - 1:n_].to_broadcast([128, 16, n_])
        snn = TSv[:, :, n_ - 1:n_].to_broadcast([128, 16, n_])
        c0_, s0_ = TCv[:, :, 0:n_], TSv[:, :, 0:n_]
        a_, b__ = tAv[:, :, 0:n_], tBv[:, :, 0:n_]
        P.op("dve", lambda e, a_=a_, c0_=c0_, cn=cn: e.tensor_tensor(a_, c0_, cn, ALU.mult), reads=KTB, writes=KTB)
        P.op("dve", lambda e, b__=b__, s0_=s0_, snn=snn: e.tensor_tensor(b__, s0_, snn, ALU.mult), reads=KTB, writes=KTB)
        P.op("dve", lambda e, n_=n_, a_=a_, b__=b__: e.tensor_tensor(TCv[:, :, n_:2 * n_], a_, b__, ALU.subtract),
             reads=KTB, writes=KTB)
        P.op("dve", lambda e, a_=a_, c0_=c0_, snn=snn: e.tensor_tensor(a_, c0_, snn, ALU.mult), reads=KTB, writes=KTB)
        P.op("dve", lambda e, b__=b__, s0_=s0_, cn=cn: e.tensor_tensor(b__, s0_, cn, ALU.mult), reads=KTB, writes=KTB)
        P.op("dve", lambda e, n_=n_, a_=a_, b__=b__: e.tensor_tensor(TSv[:, :, n_:2 * n_], a_, b__, ALU.add),
             reads=KTB, writes=KTB)
        n_ *= 2

    Bp = [carve(16 * 128), carve(16 * 128)]
    Bb = [carve(16 * 128), carve(16 * 128)]
    Bpv = [x.rearrange("p (t s) -> p t s", t=16) for x in Bp]
    Bbv = [x.rearrange("p (t s) -> p t s", t=16) for x in Bb]
    KB = ["Bp"]
    for ri, src in enumerate([b_re, b_im]):
        P.op("pool", lambda e, ri=ri: e.memset(Bp[ri], 0.0), reads=KB, writes=KB)
    for ri, src in enumerate([b_re, b_im]):
        for g2 in range(2):
            for i in range(4):
                col = (2 * i + g2) * 16
                dst = Bpv[ri][g2 * 64:(g2 + 1) * 64, :, col:col + 16].rearrange("p (q i) c -> p q i c", i=4)[:, :, i, :]
                s_ = bass.AP(src.tensor, (2 * i + g2) * 1024, [[16, 64], [8 * 1024, 4], [1, 16]])
                P.dma("sp", dst, s_, reads=KB, writes=KB)
    crb = spv[:, 3, :].unsqueeze(2).to_broadcast([128, 16, 128])
    cib = spv[:, 4, :].unsqueeze(2).to_broadcast([128, 16, 128])
    tb1 = carve(16 * 128); tb1v = tb1.rearrange("p (t s) -> p t s", t=16)
    KBS = ["Bp", "Bb", "sp", "tb1"]
    P.op("dve", lambda e: e.tensor_tensor(Bbv[0], Bpv[0], crb, ALU.mult), reads=KBS, writes=["Bb"])
    P.op("dve", lambda e: e.tensor_tensor(tb1v, Bpv[1], cib, ALU.mult), reads=KBS, writes=["tb1"])
    P.op("dve", lambda e: e.tensor_tensor(Bbv[0], Bbv[0], tb1v, ALU.subtract), reads=KBS, writes=["Bb"])
    P.op("dve", lambda e: e.tensor_tensor(Bbv[1], Bpv[1], crb, ALU.mult), reads=KBS, writes=["Bb"])
    P.op("dve", lambda e: e.tensor_tensor(tb1v, Bpv[0], cib, ALU.mult), reads=KBS, writes=["tb1"])
    P.op("dve", lambda e: e.tensor_tensor(Bbv[1], Bbv[1], tb1v, ALU.add), reads=KBS, writes=["Bb"])
    CTf = [carve(16 * 128), carve(16 * 128)]
    CTfv = [x.rearrange("p (t s) -> p t s", t=16) for x in CTf]
    Wt_alt = [(Bpv, "Bp"), (CTfv, "CTf")]
    pows = {1: (spv[:, 1, :], spv[:, 2, :]), 2: (pwv[:, 0, :], pwv[:, 1, :]), 3: (pwv[:, 2, :], pwv[:, 3, :])}
    KW = ["Bp", "Bb", "sp", "pw", "tb1"]
    for s in range(4):
        if s == 3:
            srcv, skey = Bbv, "Bb"
        else:
            pr = pows[3 - s][0].unsqueeze(2).to_broadcast([128, 16, 128])
            pi = pows[3 - s][1].unsqueeze(2).to_broadcast([128, 16, 128])
            Wtv, wkey_ = Wt_alt[s % 2]
            KW_ = ["Bb", "sp", "pw", "tb1", wkey_]
            P.op("dve", lambda e, pr=pr, Wtv=Wtv: e.tensor_tensor(Wtv[0], Bbv[0], pr, ALU.mult), reads=KW_, writes=[wkey_])
            P.op("dve", lambda e, pi=pi: e.tensor_tensor(tb1v, Bbv[1], pi, ALU.mult), reads=KW_, writes=["tb1"])
            P.op("dve", lambda e, Wtv=Wtv: e.tensor_tensor(Wtv[0], Wtv[0], tb1v, ALU.subtract), reads=KW_, writes=[wkey_])
            P.op("dve", lambda e, pi=pi, Wtv=Wtv: e.tensor_tensor(Wtv[1], Bbv[0], pi, ALU.mult), reads=KW_, writes=[wkey_])
            P.op("dve", lambda e, pr=pr: e.tensor_tensor(tb1v, Bbv[1], pr, ALU.mult), reads=KW_, writes=["tb1"])
            P.op("dve", lambda e, Wtv=Wtv: e.tensor_tensor(Wtv[1], Wtv[1], tb1v, ALU.add), reads=KW_, writes=[wkey_])
            srcv, skey = Wtv, wkey_
        for ri in range(2):
            b = bank()
            for q in range(4):
                for j in range(4):
                    ti = q * 4 + j
                    P.op("pe", lambda e, ri=ri, ti=ti, q=q, j=j, b=b, srcv=srcv: e.matmul(psb[b][:, q * 128:(q + 1) * 128], srcv[ri][:, ti, :], idf,
                                                                                          start=(j == 0), stop=(j == 3)),
                         reads=[skey, "idf"], writes=[pk[b]])
            P.op("act", lambda e, s=s, ri=ri, b=b: e.activation(Bcv[:, s, ri, :, :], psb[b].rearrange("p (k n) -> p k n", k=4), AF.Copy),
                 reads=[pk[b], "Bc"], writes=["Bc"])
    if stop == 30:
        P.barrier()
        return finish()
    Cp = [Bp[0], Bp[1]]
    Cpv = Bpv
    for ri, src in enumerate([c_re, c_im]):
        P.op("pool", lambda e, ri=ri: e.memset(Cp[ri], 0.0), reads=KB, writes=KB)
    for ri, src in enumerate([c_re, c_im]):
        for g2 in range(2):
            for i in range(4):
                p0 = 32 * i + 16 * g2
                dst = Cpv[ri][p0:p0 + 16, :, g2 * 64:(g2 + 1) * 64].rearrange("p (q i) c -> p q i c", i=4)[:, :, i, :]
                s_ = bass.AP(src.tensor, (2 * i + g2) * 1024, [[64, 16], [8 * 1024, 4], [1, 64]])
                P.dma("sp", dst, s_, key="k_Cp", reads=KB, writes=KB)
    for ri in range(2):
        for q in range(4):
            b = bank()
            for j in range(4):
                ti = q * 4 + j
                P.op("pe", lambda e, ri=ri, ti=ti, j=j, b=b: e.transpose(psb[b][:, j * 128:(j + 1) * 128], Cpv[ri][:, ti, :], idf),
                     reads=["Bp", "idf"], writes=[pk[b]])
            P.op("act", lambda e, ri=ri, q=q, b=b: e.activation(CTfv[ri][:, q * 4:(q + 1) * 4, :],
                                                                 psb[b].rearrange("p (j s) -> p j s", j=4), AF.Copy),
                 reads=[pk[b], "CTf"], writes=["CTf"])
    if stop == 31:
        P.barrier()
        return finish()
    Ck = [Bp[0], Bp[1]]; Ckv = Bpv
    KC = ["Bp", "CTf", "sp", "pw", "tb1"]
    for k in range(4):
        if k == 0:
            P.op("dve", lambda e: e.tensor_copy(Ckv[0], CTfv[0]), reads=KC, writes=["Bp"])
            P.op("dve", lambda e: e.tensor_scalar(Ckv[1], CTfv[1], -1.0, None, ALU.mult), reads=KC, writes=["Bp"])
        else:
            pr = pows[k][0].unsqueeze(2).to_broadcast([128, 16, 128])
            pi = pows[k][1].unsqueeze(2).to_broadcast([128, 16, 128])
            P.op("dve", lambda e, pr=pr: e.tensor_tensor(Ckv[0], CTfv[0], pr, ALU.mult), reads=KC, writes=["Bp"])
            P.op("dve", lambda e, pi=pi: e.tensor_tensor(tb1v, CTfv[1], pi, ALU.mult), reads=KC, writes=["tb1"])
            P.op("dve", lambda e: e.tensor_tensor(Ckv[0], Ckv[0], tb1v, ALU.subtract), reads=KC, writes=["Bp"])
            P.op("dve", lambda e, pi=pi: e.tensor_tensor(Ckv[1], CTfv[0], pi, ALU.mult), reads=KC, writes=["Bp"])
            P.op("dve", lambda e, pr=pr: e.tensor_tensor(tb1v, CTfv[1], pr, ALU.mult), reads=KC, writes=["tb1"])
            P.op("dve", lambda e: e.tensor_tensor(Ckv[1], Ckv[1], tb1v, ALU.add), reads=KC, writes=["Bp"])
            P.op("dve", lambda e: e.tensor_scalar(Ckv[1], Ckv[1], -1.0, None, ALU.mult), reads=KC, writes=["Bp"])
        for ri in range(2):
            for r in range(4):
                P.op("act", lambda e, k=k, ri=ri, r=r: e.activation(
                    Ccv[:, k, ri, :, :].rearrange("p (q r) c -> p q r c", r=4)[:, :, r, :],
                    Ckv[ri].rearrange("p (q r) c -> p q r c", r=4)[:, :, r, 32 * r:32 * r + 32], AF.Copy),
                    reads=["Bp", "Cc"], writes=["Cc"])
        if stop == 32:
            P.barrier()
            return finish()
        if k < 3:
            b = bank()
            for kc in range(4):
                first = True
                for j in range(4):
                    ti = kc * 4 + j
                    for ri in range(2):
                        P.op("pe", lambda e, ri=ri, ti=ti, kc=kc, b=b, first=first, last=(j == 3 and ri == 1):
                             e.matmul(psb[b][:, kc * 128:(kc + 1) * 128], Bbv[ri][:, ti, :], Ckv[ri][:, ti, :], start=first, stop=last),
                             reads=["Bb", "Bp"], writes=[pk[b]])
                        first = False
            if stop == 33:
                P.barrier()
                return finish()
            P.op("act", lambda e, k=k, b=b: e.activation(KTv[:, k, :, :], psb[b].rearrange("p (k n) -> p k n", k=4), AF.Copy),
                 reads=[pk[b], "KTw"], writes=["KTw"])
            if stop == 34:
                P.barrier()
                return finish()
    P.op("pool", lambda e: e.tensor_tensor(tb1[:, 0:512], tb1[:, 512:1024], tb1[:, 1024:1536], ALU.mult), reads=["tb1"], writes=["tb1"])
    P.op("pool", lambda e: e.tensor_tensor(tb1[:, 0:512], tb1[:, 512:1024], tb1[:, 1024:1536], ALU.subtract), reads=["tb1"], writes=["tb1"])
    P.op("dve", lambda e: e.memset(Hst, 0.0), writes=["H"])
    dbg("sp", sp_, [128, 128], ["sp"])
    dbg("TC", TC, [128, 16 * LD], KTB)
    dbg("TS", TSn, [128, 16 * LD], KTB)
    P.barrier()
    pos[0] = phase0_mark
    if stop == 0:
        return finish()

    NB = 256
    wk = dict(ss=carve(2), xn=carve(2 * D, BF16))
    wk["junk"] = wk["xn"]
    hT = carve(8 * NB, BF16); hTv = hT.rearrange("p (k t) -> p k t", k=8)
    VW = NB + 32
    vext = carve(4 * VW, BF16); vextv = vext.rearrange("p (c t) -> p c t", c=4)
    ubf2 = [carve(4 * NB, BF16), carve(4 * NB, BF16)]
    uf2 = [carve(4 * NB), carve(4 * NB)]
    ubfv2 = [x.rearrange("p (c t) -> p c t", c=4) for x in ubf2]
    ubm2 = [carve(16 * NB, BF16), carve(16 * NB, BF16)]
    ubmv2 = [x.rearrange("p (r c t) -> p r c t", r=4, c=4) for x in ubm2]
    ufv2 = [x.rearrange("p (c t) -> p c t", c=4) for x in uf2]
    cpre = carve(4 * NB); cprev = cpre.rearrange("p (c t) -> p c t", c=4)
    yf = carve(4 * NB); yfv = yf.rearrange("p (c t) -> p c t", c=4)
    sqv = yfv
    lnreg = carve(2 * NB)
    lnm = lnreg[:, 0:NB]; lnr = lnreg[:, NB:2 * NB]
    ygb = lnreg.bitcast(BF16); ygbv = ygb.rearrange("p (c t) -> p c t", c=4)
    sig = carve(NB)
    p1_shared_mark = pos[0]
    xin = [carve(D), gbc]
    vtail = carve(4 * 32); vtailv = vtail.rearrange("p (c t) -> p c t", c=4)
    G8 = 8 * LD
    rt = [carve(G8) for _ in range(2)]
    bpr = carve(G8); bpi = carve(G8)
    bprv = bpr.rearrange("p (t j) -> p t j", t=8); bpiv = bpi.rearrange("p (t j) -> p t j", t=8)
    rt = rt + [rt[0], rt[1]]
    rtv = [x.rearrange("p (t j) -> p t j", t=8) for x in rt]
    gre = carve(G8); gim = carve(G8)
    grev = gre.rearrange("p (t j) -> p t j", t=8); gimv = gim.rearrange("p (t j) -> p t j", t=8)
    Hb_ = carve_x(2 * 16 * (LD + 1), BF16)
    Hb = [Hb_, Hb_]
    Hbv = [x.rearrange("p (r t j) -> p r t j", r=2, t=16) for x in Hb]
    gl = carve(2 * 16); glv = gl.rearrange("p (r t) -> p r t", r=2)
    hl = carve(4 * 16); hlv = hl.rearrange("p (a t) -> p a t", a=4)
    hout = rt[0][:, 0:256]; cpo = rt[1]

    P.op("dve", lambda e: e.memset(vext, 0.0), writes=["vext"])
    P.op("pool", lambda e: e.memset(Hb[0].bitcast(F32), 0.0), writes=["Hb0"])

    DIAGK = ["diagA%d" % i for i in range(4)] + ["diagB%d" % i for i in range(4)]
    YFC = ["yf%d" % i for i in range(4)]; CPC = ["cpre%d" % i for i in range(4)]; YGC = ["ygb%d" % i for i in range(4)]

    def front_steps(n, c0, sample, last, par, xtiles):
        ubfv, ufv = ubfv2[par], ufv2[par]
        hTl = hTv
        ukb, ukf = "ubf%d" % par, "uf%d" % par
        st_ = {}
        ss, xn = wk["ss"], wk["xn"]
        xnv = xn.rearrange("p (t n) -> p t n", t=2)

        def n_pre():
            for i, (xt, nt, xkeys, load_fn) in enumerate(xtiles):
                if load_fn is not None:
                    load_fn()
                P.op("act", lambda e, i=i, xt=xt, nt=nt: e.activation(xnv[:nt, i, :], xt, AF.Square, accum_out=ss[:nt, i:i + 1]),
                     reads=xkeys + ["xn", "ss"], writes=["xn", "ss"])

        def n_rstd():
            nt = xtiles[0][1]
            k = len(xtiles)
            P.op("dve", lambda e: e.tensor_scalar(ss[:nt, 0:k], ss[:nt, 0:k], 1.0 / D, RMS_EPS, ALU.mult, ALU.add), reads=["ss"], writes=["ss"])
            P.op("pool", lambda e: e.tensor_tensor(ss[:nt, 0:k], ss[:nt, 0:k], negh[:nt].to_broadcast([nt, k]), ALU.pow),
                 reads=["ss", "cst"], writes=["ss"])

        def n_scale_T():
            st_["tb"] = []
            for i, (xt, nt, xkeys, load_fn) in enumerate(xtiles):
                P.op("dve", lambda e, i=i, xt=xt, nt=nt: e.tensor_scalar(xnv[:nt, i, :], xt, ss[:nt, i:i + 1], None, ALU.mult),
                     reads=xkeys + ["ss", "xn"], writes=["xn"])
                b = bank()
                pt = psb[b].bitcast(BF16)
                for kt in range(8):
                    P.op("pe", lambda e, kt=kt, i=i, nt=nt, pt=pt: e.transpose(pt[:, kt * 128:kt * 128 + nt], xnv[:nt, i, kt * 128:(kt + 1) * 128], idb[:nt, :nt]),
                         reads=["xn", "idb"], writes=[pk[b]])
                st_["tb"].append(b)

        def n_evac():
            for i, (xt, nt, xkeys, load_fn) in enumerate(xtiles):
                b = st_["tb"][i]
                ptv = psb[b].bitcast(BF16).rearrange("p (k t) -> p k t", k=8)
                P.op("dve", lambda e, i=i, nt=nt, ptv=ptv: e.tensor_tensor(hTl[:, :, i * 128:i * 128 + nt], ptv[:, :, 0:nt],
                                                                          gcolv[:, 0, :].unsqueeze(2).to_broadcast([128, 8, nt]), ALU.mult),
                     reads=[pk[b], "gcol", "hT"], writes=["hT"])

        def glu_pe(cs_):
            def f():
                st_["glu"] = []
                for c in cs_:
                    ba = proj_fm(Winv, 8, c * 128, hTl, 0, n, ["hT"], "Win")
                    bg = proj_fm(Winv, 8, 512 + c * 128, hTl, 0, n, ["hT"], "Win")
                    st_["glu"].append((c, ba, bg))
            return f

        def glu_post():
            for (c, ba, bg) in st_["glu"]:
                P.op("act", lambda e, bg=bg: e.activation(sig[:, 0:n], psb[bg][:, 0:n], AF.Sigmoid), reads=[pk[bg], "sig"], writes=["sig"])
                if not sample:
                    P.op("dve", lambda e, c=c, ba=ba: e.tensor_tensor(vextv[:, c, 30:30 + n], psb[ba][:, 0:n], sig[:, 0:n], ALU.mult),
                         reads=[pk[ba], "sig", "vext"], writes=["vext"])
                    if last:
                        P.op("dve", lambda e, c=c, ba=ba: e.tensor_tensor(vtailv[:, c, :], psb[ba][:, n - 32:n], sig[:, n - 32:n], ALU.mult),
                             reads=[pk[ba], "sig", "vtail"], writes=["vtail"])
                else:
                    P.op("dve", lambda e, c=c, ba=ba: e.tensor_tensor(vsfv[:, c, :], psb[ba][:, 0:n], sig[:, 0:n], ALU.mult),
                         reads=[pk[ba], "sig", "vsf"], writes=["vsf"])
                    P.op("dve", lambda e, c=c: e.tensor_copy(vextsv[:, c, :, 30:34], vsfv[:, c, :].rearrange("p (s t) -> p s t", t=4)),
                         reads=["vsf", "vexts"], writes=["vexts"])

        def u_pe():
            st_["u"] = [proj_fm(Winv, 8, 1024 + c * 128, hTl, 0, n, ["hT"], "Win") for c in range(4)]

        def u_post():
            for c, bu in enumerate(st_["u"]):
                P.op("act", lambda e, c=c, bu=bu: e.activation(ufv[:, c, 0:n], psb[bu][:, 0:n], AF.Copy), reads=[pk[bu], ukf], writes=[ukf])
                P.op("dve", lambda e, c=c, bu=bu: e.tensor_copy(ubfv[:, c, 0:n], psb[bu][:, 0:n]), reads=[pk[bu], ukb], writes=[ukb])
                for r in range(4):
                    P.op("act", lambda e, c=c, r=r: e.activation(ubmv2[par][:, r, c, 0:n], ubfv[:, c, 0:n], AF.Copy, scale=mk[:, r:r + 1]),
                         reads=[ukb, "mk", "ubm%d" % par], writes=["ubm%d" % par])

        def conv_pe():
            st_["cv"] = []
            for c in range(4):
                b = bank()
                for k in range(31):
                    rhs = vextsv[:, c, :, k:k + 4] if sample else vextv[:, c, k:k + n]
                    P.op("pe", lambda e, c=c, k=k, b=b, rhs=rhs: e.matmul(psb[b][:, 0:n], diagv[:, c, k, :], rhs,
                                                                          start=(k == 0), stop=(k == 30)),
                         reads=DIAGK + ["vexts" if sample else "vext"], writes=[pk[b]])
                st_["cv"].append(b)

        def conv_post():
            for c, b in enumerate(st_["cv"]):
                P.op("act", lambda e, c=c, b=b: e.activation(cprev[:, c, 0:n], psb[b][:, 0:n], AF.Identity, bias=pcolv[:, 0, c:c + 1]),
                     reads=[pk[b], "pcol", "cpre"] + CPC, writes=["cpre"])
                P.op("act", lambda e, c=c: e.activation(sqv[:, c, 0:n], cprev[:, c, 0:n], AF.Square), reads=["cpre", "yf"] + YFC, writes=["yf"])
            if not sample:
                P.op("dve", lambda e: e.tensor_copy(vextv[:, :, 0:30], vextv[:, :, n:n + 30]), reads=["vext"], writes=["vext"])

        def ln_pe():
            bm, bq = bank(), bank()
            st_["ln"] = (bm, bq)
            for c in range(4):
                P.op("pe", lambda e, c=c: e.matmul(psb[bm][:, 0:n], onesf, cprev[:, c, 0:n], start=(c == 0), stop=(c == 3)),
                     reads=["onesf", "cpre"], writes=[pk[bm]])
            for c in range(4):
                P.op("pe", lambda e, c=c: e.matmul(psb[bq][:, 0:n], onesf, sqv[:, c, 0:n], start=(c == 0), stop=(c == 3)),
                     reads=["onesf", "yf"], writes=[pk[bq]])

        def ln_post():
            bm, bq = st_["ln"]
            P.op("act", lambda e: e.activation(lnm[:, 0:n], psb[bm][:, 0:n], AF.Copy), reads=[pk[bm], "ygb"] + YGC, writes=["ygb"])
            P.op("dve", lambda e: e.tensor_tensor(lnr[:, 0:n], lnm[:, 0:n], lnm[:, 0:n], ALU.mult), reads=["ygb"], writes=["ygb"])
            P.op("dve", lambda e: e.tensor_tensor(lnr[:, 0:n], psb[bq][:, 0:n], lnr[:, 0:n], ALU.subtract), reads=[pk[bq], "ygb"], writes=["ygb"])
            P.op("dve", lambda e: e.tensor_scalar(lnr[:, 0:n], lnr[:, 0:n], LN_EPS, None, ALU.add), reads=["ygb"], writes=["ygb"])
            P.op("act", lambda e: e.activation(lnr[:, 0:n], lnr[:, 0:n], AF.Sqrt), reads=["ygb"], writes=["ygb"])
            P.op("dve", lambda e: e.reciprocal(lnr[:, 0:n], lnr[:, 0:n]), reads=["ygb"], writes=["ygb"])
            xcs = [sqv[:, c, 0:n] for c in range(4)]
            for c in range(4):
                P.op("dve", lambda e, c=c: e.tensor_tensor(xcs[c], cprev[:, c, 0:n], lnm[:, 0:n], ALU.subtract),
                     reads=["cpre", "ygb", "yf", YFC[c]], writes=["yf", YFC[c]])
                P.op("dve", lambda e, c=c: e.tensor_tensor(xcs[c], xcs[c], lnr[:, 0:n], ALU.mult), reads=[YFC[c], "ygb"], writes=[YFC[c]])
            for c in range(4):
                P.op("act", lambda e, c=c: e.activation(sig[:, 0:n], xcs[c], AF.Sigmoid, scale=pcolv[:, 1, c:c + 1], bias=pcolv[:, 2, c:c + 1]),
                     reads=[YFC[c], "pcol", "sig"], writes=["sig"])
                P.op("dve", lambda e, c=c: e.tensor_scalar(xcs[c], xcs[c], pcolv[:, 1, c:c + 1], pcolv[:, 2, c:c + 1], ALU.mult, ALU.add),
                     reads=[YFC[c], "pcol"], writes=[YFC[c]])
                P.op("dve", lambda e, c=c: e.tensor_tensor(mixv[:, c, c0:c0 + n], xcs[c], sig[:, 0:n], ALU.mult),
                     reads=[YFC[c], "sig", "mix"], writes=["mix"])

        nop_ = lambda: None
        return [(n_pre, nop_), (n_rstd, nop_), (n_scale_T, n_evac), (glu_pe([0, 1]), glu_post), (glu_pe([2, 3]), glu_post),
                (u_pe, u_post), (conv_pe, conv_post), (ln_pe, ln_post)]

    def run_front(steps):
        for pe_, post_ in steps:
            pe_()
            post_()

    def s5_back(n, c0, yloc_fn, par, deint=False):
        ufv = ufv2[par]
        ukf = "uf%d" % par
        ss_ = [cprev[:, kc, 0:n] for kc in range(4)]
        for kc in range(4):
            b, off = yloc_fn(kc)
            if deint:
                o_ = yfv[:, kc, 0:n].rearrange("p (i s) -> p i s", s=4)
                u_ = ufv[:, kc, 0:n].rearrange("p (i s) -> p i s", s=4)
                y_ = psb[b][:, off:off + n].rearrange("p (s i) -> p i s", s=4)
            else:
                o_, u_, y_ = yfv[:, kc, 0:n], ufv[:, kc, 0:n], psb[b][:, off:off + n]
            P.op("dve", lambda e, kc=kc, o_=o_, u_=u_, y_=y_: e.scalar_tensor_tensor(o_, u_, pcolv[:, 3, kc:kc + 1], y_, ALU.mult, ALU.add),
                 reads=[ukf, "pcol", pk[b], "yf", YFC[kc]], writes=["yf", YFC[kc]])
        for kc in range(4):
            P.op("act", lambda e, kc=kc: e.activation(ss_[kc], yfv[:, kc, 0:n], AF.Square), reads=[YFC[kc], "cpre", CPC[kc]], writes=["cpre", CPC[kc]])
        for kc in range(4):
            P.op("dve", lambda e, kc=kc: e.tensor_scalar(ss_[kc], ss_[kc], 0.044715, 1.0, ALU.mult, ALU.add), reads=[CPC[kc]], writes=[CPC[kc]])
            P.op("dve", lambda e, kc=kc: e.tensor_tensor(ss_[kc], ss_[kc], yfv[:, kc, 0:n], ALU.mult), reads=[CPC[kc], YFC[kc]], writes=[CPC[kc]])
        for kc in range(4):
            P.op("act", lambda e, kc=kc: e.activation(ss_[kc], ss_[kc], AF.Sigmoid, scale=1.5957691216057308), reads=[CPC[kc]], writes=[CPC[kc]])
        for kc in range(4):
            P.op("dve", lambda e, kc=kc: e.tensor_tensor(yfv[:, kc, 0:n], yfv[:, kc, 0:n], ss_[kc], ALU.mult), reads=[CPC[kc], YFC[kc]], writes=[YFC[kc]])
        for kc in range(4):
            P.op("act", lambda e, kc=kc: e.activation(ygbv[:, kc, 0:n], yfv[:, kc, 0:n], AF.Copy), reads=[YFC[kc], "ygb", YGC[kc]], writes=["ygb", YGC[kc]])
        for c in range(4):
            b = proj_fm(Wgluv, 4, c * 128, ygbv, 0, n, ["ygb"] + YGC, "Wglu")
            P.op("act", lambda e, b=b: e.activation(sig[:, 0:n], psb[b][:, 0:n], AF.Sigmoid), reads=[pk[b], "sig"], writes=["sig"])
            P.op("dve", lambda e, c=c: e.tensor_tensor(mixv[:, 4 + c, c0:c0 + n], yfv[:, c, 0:n], sig[:, 0:n], ALU.mult),
                 reads=[YFC[c], "sig", "mix"], writes=["mix"])

    NBLK = T // NB
    TPB = NB // 128
    POOL_BU, POOL_FR, POOL_BACK = [0, 1], [2, 3, 6, 7], [2, 3, 6, 7]

    def prompt_front(blk):
        xt = []
        for tl in range(TPB):
            t_ = blk * TPB + tl
            z = t_ % 2
            xt.append((xin[z], 128, [("xin0" if z == 0 else "gbc")],
                       (lambda t_=t_, z=z: P.dma("sp", xin[z], xp[t_ * 128:(t_ + 1) * 128, :], writes=[("xin0" if z == 0 else "gbc")]))))
        return front_steps(NB, blk * NB, False, blk == NBLK - 1, blk % 2, xt)

    bpool[0] = list(range(8))
    carry_bu = None
    run_front(prompt_front(0)[:3] if stop == 20 else prompt_front(0))
    if stop == 21:
        return finish()
    if stop == 20:
        dbg("xn", wk["xn"].bitcast(F32), [128, D], ["xn"])
        dbg("ss", wk["ss"], [128, 2], ["ss"])
        dbg("hT", hT.bitcast(F32), [128, 4 * NB], ["hT"])
        return finish()
    for blk in range(NBLK):
        par = blk % 2
        ubfv = ubfv2[par]
        c0 = blk * NB
        steps = prompt_front(blk + 1) if blk + 1 < NBLK else []
        yloc = lambda kc: (4 + kc // 2, (kc % 2) * NB)
        zb = blk % 2
        Hcur = Hbv[zb]
        hbk = "Hb0"
        ubi = [ubfv[:, kc, :].rearrange("p (i s) -> p i s", s=4) for kc in range(4)]
        ubmv = ubmv2[par]

        def emit_bu(half, par=par):
            ubmv = ubmv2[par]
            bpool[0] = POOL_BU
            bA, bB = bank(), bank()
            for j in range(8):
                ti = half * 8 + j
                kc, r = ti // 4, ti % 4
                for ri, bb in enumerate((bA, bB)):
                    for s in range(4):
                        P.op("pe", lambda e, ri=ri, kc=kc, r=r, bb=bb, j=j, s=s, ubmv=ubmv: e.matmul(
                            psb[bb][:, j * LD:(j + 1) * LD], Bcv[:, s, ri, kc, :],
                            ubmv[:, r, kc, :].rearrange("p (i s) -> p i s", s=4)[:, :, s], start=(s == 0), stop=(s == 3)),
                            reads=["Bc", "ubm%d" % par], writes=[pk[bb]])
            return bA, bB

        def front_slot(k):
            if steps and k < len(steps):
                bpool[0] = POOL_FR
                if k >= 1:
                    steps[k - 1][1]()
                steps[k][0]()

        slot = 0
        nxt = carry_bu if carry_bu is not None else emit_bu(0)
        if stop == 40:
            P.barrier(); return finish()
        for half in range(2):
            bA, bB = nxt
            tsl = slice(half * 8, (half + 1) * 8)
            pre = psb[bA].rearrange("p (t j) -> p t j", t=8)
            pim = psb[bB].rearrange("p (t j) -> p t j", t=8)
            cosT, sinT = TCv[:, tsl, :], TSv[:, tsl, :]
            P.op("dve", lambda e, pre=pre, cosT=cosT: e.tensor_tensor(rtv[0], pre, cosT, ALU.mult), reads=[pk[bA], "TC", "rt0"], writes=["rt0"])
            P.op("dve", lambda e, pim=pim, sinT=sinT: e.tensor_tensor(rtv[1], pim, sinT, ALU.mult), reads=[pk[bB], "TS", "rt1"], writes=["rt1"])
            P.op("dve", lambda e: e.tensor_tensor(bprv, rtv[0], rtv[1], ALU.add), reads=["rt0", "rt1"], writes=["bpr"])
            P.op("dve", lambda e, pim=pim, cosT=cosT: e.tensor_tensor(rtv[2], pim, cosT, ALU.mult), reads=[pk[bB], "TC", "rt0"], writes=["rt0"])
            P.op("dve", lambda e, pre=pre, sinT=sinT: e.tensor_tensor(rtv[3], pre, sinT, ALU.mult), reads=[pk[bA], "TS", "rt1"], writes=["rt1"])
            P.op("dve", lambda e: e.tensor_tensor(bpiv, rtv[2], rtv[3], ALU.subtract), reads=["rt0", "rt1"], writes=["bpi"])
            if stop == 41:
                P.barrier(); return finish()
            if half == 0:
                nxt = emit_bu(1)
            front_slot(slot); slot += 1
            front_slot(slot); slot += 1
            if stop == 42:
                P.barrier(); return finish()
            for j in range(8):
                ti = half * 8 + j
                rb = pwv[:, 4, ti:ti + 1].to_broadcast([128, LD])
                P.op("dve", lambda e, ti=ti, j=j, rb=rb: e.tensor_tensor_scan(grev[:, j, :], rb, bprv[:, j, :], Hv[:, 0, ti:ti + 1], ALU.mult, ALU.add),
                     reads=["pw", "bpr", "H", "gre"], writes=["gre"])
                P.op("dve", lambda e, ti=ti, j=j, rb=rb: e.tensor_tensor_scan(gimv[:, j, :], rb, bpiv[:, j, :], Hv[:, 1, ti:ti + 1], ALU.mult, ALU.add),
                     reads=["pw", "bpi", "H", "gim"], writes=["gim"])
            P.op("act", lambda e, tsl=tsl: e.activation(glv[:, 0, tsl], grev[:, :, LD - 1], AF.Copy), reads=["gre", "gl"], writes=["gl"])
            P.op("act", lambda e, tsl=tsl: e.activation(glv[:, 1, tsl], gimv[:, :, LD - 1], AF.Copy), reads=["gim", "gl"], writes=["gl"])
            P.op("dve", lambda e, cosT=cosT: e.tensor_tensor(rtv[0], grev, cosT, ALU.mult), reads=["gre", "TC"], writes=["rt0"])
            P.op("dve", lambda e, sinT=sinT: e.tensor_tensor(rtv[1], gimv, sinT, ALU.mult), reads=["gim", "TS"], writes=["rt1"])
            P.op("dve", lambda e, tsl=tsl, Hcur=Hcur: e.tensor_tensor(Hcur[:, 0, tsl, 1:LD + 1], rtv[0], rtv[1], ALU.subtract), reads=["rt0", "rt1", hbk], writes=[hbk])
            P.op("dve", lambda e, sinT=sinT: e.tensor_tensor(rtv[2], grev, sinT, ALU.mult), reads=["gre", "TS"], writes=["rt0"])
            P.op("dve", lambda e, cosT=cosT: e.tensor_tensor(rtv[3], gimv, cosT, ALU.mult), reads=["gim", "TC"], writes=["rt1"])
            P.op("dve", lambda e, tsl=tsl, Hcur=Hcur: e.tensor_tensor(Hcur[:, 1, tsl, 1:LD + 1], rtv[2], rtv[3], ALU.add), reads=["rt0", "rt1", hbk], writes=[hbk])
            front_slot(slot); slot += 1
            front_slot(slot); slot += 1
        if stop == 43:
            P.barrier(); return finish()
        cl, sl = TCv[:, :, LD - 1], TSv[:, :, LD - 1]
        P.op("dve", lambda e: e.tensor_tensor(hlv[:, 0, :], glv[:, 0, :], cl, ALU.mult), reads=["gl", "TC", "hl"], writes=["hl"])
        P.op("dve", lambda e: e.tensor_tensor(hlv[:, 1, :], glv[:, 1, :], sl, ALU.mult), reads=["gl", "TS", "hl"], writes=["hl"])
        P.op("dve", lambda e: e.tensor_tensor(hlv[:, 2, :], glv[:, 0, :], sl, ALU.mult), reads=["gl", "TS", "hl"], writes=["hl"])
        P.op("dve", lambda e: e.tensor_tensor(hlv[:, 3, :], glv[:, 1, :], cl, ALU.mult), reads=["gl", "TC", "hl"], writes=["hl"])
        P.op("dve", lambda e: e.tensor_tensor(Hv[:, 0, :], hlv[:, 0, :], hlv[:, 1, :], ALU.subtract), reads=["hl", "H"], writes=["H"])
        P.op("dve", lambda e: e.tensor_tensor(Hv[:, 1, :], hlv[:, 2, :], hlv[:, 3, :], ALU.add), reads=["hl", "H"], writes=["H"])
        if stop == 44:
            P.barrier(); return finish()
        for kc in range(4):
            b, off = yloc(kc)
            for sp in range(4):
                reg = slice(off + sp * LD, off + (sp + 1) * LD)
                if sp < 3:
                    for s in range(sp + 1):
                        P.op("pe", lambda e, b=b, reg=reg, sp=sp, s=s, kc=kc, ubi=ubi: e.matmul(psb[b][:, reg], KTv[:, sp - s, kc, :], ubi[kc][:, :, s],
                                                                                              start=(s == 0), stop=False),
                             reads=["KTw", "ubf%d" % par], writes=[pk[b]])
                for r in range(4):
                    ti = kc * 4 + r
                    for ri in range(2):
                        if sp < 3:
                            lhs_, rhs_, st_f = Ccv[:, sp + 1, ri, ti, :], Hcur[:, ri, ti, 0:LD], False
                        else:
                            lhs_, rhs_, st_f = Ccv[:, 0, ri, ti, :], Hcur[:, ri, ti, 1:LD + 1], (ri == 0)
                        P.op("pe", lambda e, b=b, reg=reg, r=r, lhs_=lhs_, rhs_=rhs_, st_f=st_f, ri=ri: e.matmul(
                            psb[b][32 * r:32 * r + 32, reg], lhs_, rhs_, start=st_f, stop=(ri == 1), tile_position=(0, 32 * r)),
                            reads=["Cc", hbk], writes=[pk[b]])
        P.op("act", lambda e, zb=zb: e.activation(Hbv[1 - zb][:, :, :, 0], Hv, AF.Copy), reads=["H", "Hb0"], writes=["Hb0"])
        while slot < len(steps):
            front_slot(slot); slot += 1
        carry_bu = emit_bu(0, par=(blk + 1) % 2) if blk + 1 < NBLK else None
        if steps:
            bpool[0] = POOL_FR
            steps[len(steps) - 1][1]()
        if stop == 22:
            return finish()
        bpool[0] = POOL_BACK
        s5_back(NB, c0, yloc, par, deint=True)
        if stop == 23:
            return finish()
    bpool[0] = list(range(8))

    b = bank()
    for ri in range(2):
        P.op("pe", lambda e, ri=ri, b=b: e.transpose(psb[b][:16, ri * 128:(ri + 1) * 128], Hv[:, ri, :], idf), reads=["H", "idf"], writes=[pk[b]])
    P.op("act", lambda e, b=b: e.activation(hout[:16, :], psb[b][:16, 0:256], AF.Copy), reads=[pk[b], "rt0"], writes=["rt0"])
    P.dma("sp", hpr, hout[:16, 0:128], key="o_hp", reads=["rt0"])
    P.dma("sp", hpi, hout[:16, 128:256], key="o_hp", reads=["rt0"])
    b = bank()
    for c in range(4):
        P.op("pe", lambda e, c=c, b=b: e.transpose(psb[b][:32, c * 128:(c + 1) * 128], vtailv[:, c, :], idf), reads=["vtail", "idf"], writes=[pk[b]])
    P.op("act", lambda e, b=b: e.activation(cpo[:32, :], psb[b][:32, :], AF.Copy), reads=[pk[b], "rt1"], writes=["rt1"])
    P.dma("sp", cp, cpo[2:32, :], key="o_cp", reads=["rt1"])
    dbg("mixp", mix, [128, 8 * (T + TS)], ["mix"])
    P.barrier()
    pos[0] = p1_shared_mark
    if stop == 1:
        return finish()

    vsf = carve(4 * TS); vsfv = vsf.rearrange("p (c t) -> p c t", c=4)
    vexts = carve(4 * NS * 34, BF16); vextsv = vexts.rearrange("p (c s t) -> p c s t", c=4, s=NS)
    scq = [carve(512), carve(512)]
    cso = carve(512)
    h0q = scq
    hc = [carve(16 * 32), carve(16 * 32)]
    hcv = [x.rearrange("p (t r s) -> p t r s", t=16, r=2) for x in hc]
    bus = ubm2[1].bitcast(F32)[:, 0:2 * 16 * TS]; busv = bus.rearrange("p (r t s k) -> p r t s k", r=2, t=16, s=NS)
    st = [ubf2[1].bitcast(F32)[:, 0:16 * NS], ubf2[1].bitcast(F32)[:, 16 * NS:32 * NS]]
    stv = [x.rearrange("p (t s) -> p t s", t=16) for x in st]
    hsb = uf2[1].bitcast(BF16)[:, 0:2 * 16 * TS]; hsbv = hsb.rearrange("p (r t s k) -> p r t s k", r=2, t=16, s=NS)
    hsoq = scq
    for q in range(4):
        z = q % 2
        P.dma("sp", scq[z][:120, :], sconv[q * 4:(q + 1) * 4, :, :].rearrange("s j c -> (s j) c"), writes=["scq%d" % z])
        b = bank()
        for c in range(4):
            P.op("pe", lambda e, z=z, c=c, b=b: e.transpose(psb[b][:, c * 120:(c + 1) * 120], scq[z][:120, c * 128:(c + 1) * 128], idf[:120, :120]),
                 reads=["scq%d" % z, "idf"], writes=[pk[b]])
        for c in range(4):
            P.op("act", lambda e, q=q, c=c, b=b: e.activation(vextsv[:, c, q * 4:(q + 1) * 4, 0:30],
                                                             psb[b][:, c * 120:(c + 1) * 120].rearrange("p (s j) -> p s j", s=4), AF.Copy),
                 reads=[pk[b], "vexts"], writes=["vexts"])
    P.dma("sp", cs[:, 0:26, :], sconv[:, 4:30, :], key="o_cs0")
    P.dma("sp", xresv[:TS, 16, :], xs, writes=["xr16"])
    run_front(front_steps(TS, T, True, False, 0, [(xresv[:TS, 16, :], TS, ["xr16"], None)]))
    b = bank()
    for c in range(4):
        P.op("pe", lambda e, c=c, b=b: e.transpose(psb[b][:TS, c * 128:(c + 1) * 128], vsfv[:, c, :], idf), reads=["vsf", "idf"], writes=[pk[b]])
    P.op("act", lambda e, b=b: e.activation(cso[:TS, :], psb[b][:TS, :], AF.Copy), reads=[pk[b]], writes=["cso"])
    for s in range(NS):
        P.dma("sp", cs[s, 26:30, :], cso[s * 4:(s + 1) * 4, :], key="o_cs1", reads=["cso"])
    for q in range(4):
        z = q % 2
        P.dma("sp", h0q[z][0:NS, :], sre[:, q * 512:(q + 1) * 512], writes=["scq%d" % z])
        P.dma("sp", h0q[z][NS:2 * NS, :], sim[:, q * 512:(q + 1) * 512], reads=["scq%d" % z], writes=["scq%d" % z])
        b = bank()
        for j in range(4):
            P.op("pe", lambda e, z=z, j=j, b=b: e.transpose(psb[b][:, j * 32:(j + 1) * 32], h0q[z][:32, j * 128:(j + 1) * 128], idf[:32, :32]),
                 reads=["scq%d" % z, "idf"], writes=[pk[b]])
        P.op("act", lambda e, q=q, b=b: e.activation(hcv[0][:, q * 4:(q + 1) * 4, :, :].rearrange("p t r s -> p t (r s)"),
                                                     psb[b][:, 0:128].rearrange("p (t x) -> p t x", t=4), AF.Copy),
             reads=[pk[b], "hc0"], writes=["hc0"])
    for q in range(4):
        bb2 = [bank(), bank()]
        for j in range(4):
            ti = q * 4 + j
            b = bb2[j % 2]
            for ri in range(2):
                c_ = ((j // 2) * 2 + ri) * TS
                P.op("pe", lambda e, ri=ri, ti=ti, b=b, c_=c_: e.matmul(psb[b][:, c_:c_ + TS],
                                                                      Bcv[:, 3, ri, ti // 4, :],
                                                                      ubmv2[0][:, ti % 4, ti // 4, 0:TS], start=True, stop=True),
                     reads=["Bc", "ubm0"], writes=[pk[b]])
        for par_ in range(2):
            b = bb2[par_]
            for ri in range(2):
                P.op("act", lambda e, ri=ri, q=q, b=b, par_=par_: e.activation(
                    busv[:, ri, q * 4 + par_:q * 4 + 4:2, :, :].rearrange("p t s k -> p t (s k)"),
                    psb[b][:, 0:4 * TS].rearrange("p (j r x) -> p j r x", j=2, r=2)[:, :, ri, :], AF.Copy),
                    reads=[pk[b], "bus"], writes=["bus"])
    arb = spv[:, 1, :].unsqueeze(2).to_broadcast([128, 16, NS])
    aib = spv[:, 2, :].unsqueeze(2).to_broadcast([128, 16, NS])
    for k in range(4):
        zc, zn = k % 2, (k + 1) % 2
        hr, hi = hcv[zc][:, :, 0, :], hcv[zc][:, :, 1, :]
        KH = ["hc0", "hc1", "st", "sp", "bus"]
        P.op("dve", lambda e, hr=hr: e.tensor_tensor(stv[0], hr, arb, ALU.mult), reads=KH, writes=["st"])
        P.op("dve", lambda e, hi=hi: e.tensor_tensor(stv[1], hi, aib, ALU.mult), reads=KH, writes=["st"])
        P.op("dve", lambda e: e.tensor_tensor(stv[0], stv[0], stv[1], ALU.subtract), reads=KH, writes=["st"])
        P.op("dve", lambda e, k=k, zn=zn: e.tensor_tensor(hcv[zn][:, :, 0, :], stv[0], busv[:, 0, :, :, k], ALU.add), reads=KH, writes=["hc%d" % zn])
        P.op("dve", lambda e, hr=hr: e.tensor_tensor(stv[0], hr, aib, ALU.mult), reads=KH, writes=["st"])
        P.op("dve", lambda e, hi=hi: e.tensor_tensor(stv[1], hi, arb, ALU.mult), reads=KH, writes=["st"])
        P.op("dve", lambda e: e.tensor_tensor(stv[0], stv[0], stv[1], ALU.add), reads=KH, writes=["st"])
        P.op("dve", lambda e, k=k, zn=zn: e.tensor_tensor(hcv[zn][:, :, 1, :], stv[0], busv[:, 1, :, :, k], ALU.add), reads=KH, writes=["hc%d" % zn])
        for ri in range(2):
            P.op("act", lambda e, ri=ri, k=k, zn=zn: e.activation(hsbv[:, ri, :, :, k], hcv[zn][:, :, ri, :], AF.Copy),
                 reads=["hc%d" % zn, "hsb"], writes=["hsb"])
    ysb = {}
    for kc in range(4):
        b = bank(); ysb[kc] = b
        for r in range(4):
            ti = kc * 4 + r
            for ri in range(2):
                P.op("pe", lambda e, ri=ri, ti=ti, r=r, b=b: e.matmul(psb[b][32 * r:32 * r + 32, 0:TS], Ccv[:, 0, ri, ti, :],
                                                                      hsbv[:, ri, ti, :, :].rearrange("p s k -> p (s k)"),
                                                                      start=(ri == 0), stop=(ri == 1), tile_position=(0, 32 * r)),
                     reads=["Cc", "hsb"], writes=[pk[b]])
    s5_back(TS, T, lambda kc: (ysb[kc], 0), 0)
    for q in range(4):
        z = q % 2
        b = bank()
        for j in range(4):
            ti = q * 4 + j
            P.op("pe", lambda e, ti=ti, j=j, b=b: e.transpose(psb[b][:32, j * 128:(j + 1) * 128], hcv[0][:, ti, :, :].rearrange("p r s -> p (r s)"), idf),
                 reads=["hc0", "idf"], writes=[pk[b]])
        P.op("act", lambda e, z=z, b=b: e.activation(hsoq[z][:32, :], psb[b][:32, :], AF.Copy), reads=[pk[b], "scq%d" % z], writes=["scq%d" % z])
        P.dma("sp", hsr[:, q * 512:(q + 1) * 512], hsoq[z][0:NS, :], key="o_hs%d" % z, reads=["scq%d" % z])
        P.dma("sp", hsi[:, q * 512:(q + 1) * 512], hsoq[z][NS:2 * NS, :], key="o_hs%d" % z, reads=["scq%d" % z])
    dbg("mix", mix, [128, 8 * (T + TS)], ["mix"])
    P.barrier()
    if stop == 2:
        return finish()
    pos[0] = p1b_mark
    load_gbc(1)
    Wo1 = carve(8 * D, BF16); Wo1v = Wo1.rearrange("p (k n) -> p k n", k=8)
    load_w(Wo1v, w_out, 8, "Wout")
    PRE0 = 45000
    Wq, _ = carve_at(PRE0, 8 * D, BF16); Wqv = Wq.rearrange("p (k n) -> p k n", k=8)
    Wkv, _ = carve_at(PRE0 + 4096, 8 * D, BF16); Wkvv = Wkv.rearrange("p (k n) -> p k n", k=8)
    load_w(Wqv, w_q, 8, "Wq")
    load_w(Wkvv, w_k, 8, "Wkv")
    wk = dict(ss=carve(1), xn=carve(D, BF16), ss2=[carve(4), carve(4)], tmp=[carve(D), carve(D)], pjunk=[carve(D, BF16), carve(D, BF16)]); wk["junk"] = wk["xn"]
    for q in range(4):
        P.dma("sp", xresv[:, q * 4:(q + 1) * 4, :], xp[q * 512:(q + 1) * 512, :].rearrange("(t p) n -> p t n", p=128),
              writes=["xr%d" % (q * 4 + i) for i in range(4)], key="k_xrq%d" % q)
    for t_ in range(17):
        n = 128 if t_ < 16 else TS
        proj_tm_postnorm_residual(mixv, 8, t_ * 128, n, Wo1v, "Wout", ["mix"], 0, xresv[:n, t_, :], "xr%d" % t_, wk)
    assert pos[0] <= PRE0, pos[0]
    dbg("x1", xres, [128, 17 * D], ["xr%d" % i for i in range(17)])
    P.barrier()
    pos[0] = persist_mark
    if stop == 3:
        return finish()

    load_gbc(3)
    Wo = carve(8 * D, BF16); Wov = Wo.rearrange("p (k n) -> p k n", k=8)
    load_w(Wov, w_v, 8, "Wo")
    wk = dict(ss=carve(1), xn=carve(D, BF16), ss2=[carve(4), carve(4)], tmp=[carve(D), carve(D)]); wk["junk"] = wk["xn"]
    pj2_ = carve(D, BF16); wk["pjunk"] = [pj2_, pj2_]
    kT = carve(8 * 256, BF16); kTv = kT.rearrange("p (k t) -> p k t", k=8)
    vb = carve(2 * D, BF16); vbv = vb.rearrange("p (m n) -> p m n", m=2)
    hT = carve(8 * 512, BF16); hTv = hT.rearrange("p (k t) -> p k t", k=8)
    hT2 = [hT, carve(8 * 512, BF16)]
    hTv2 = [x.rearrange("p (k t) -> p k t", k=8) for x in hT2]
    qT = carve(8 * 512, BF16); qTv = qT.rearrange("p (k t) -> p k t", k=8)
    hTs = carve(8 * TS, BF16); hTsv = hTs.rearrange("p (k t) -> p k t", k=8)
    qTs = carve(8 * TS, BF16); qTsv = qTs.rearrange("p (k t) -> p k t", k=8)
    p2_mark = pos[0]

    def norm_tile(blk, tl):
        t_ = blk * 4 + tl
        wk["xkeys"] = ["xr%d" % t_]; wk["hkeys"] = ["hTa%d" % (blk % 2)]
        norm_to_hT(xresv[:, t_, :], 128, 1, hTv2[blk % 2], tl * 128, wk)

    def q_proj2(n, hv, hk):
        for c in range(8):
            b = proj_fm(Wqv, 8, c * 128, hv, 0, n, [hk], "Wq")
            P.op("act", lambda e, c=c, b=b: e.activation(qTv[:, c, 0:n], psb[b][:, 0:n], AF.Copy, scale=1.0 / 16.0),
                 reads=[pk[b], "qT"], writes=["qT"])

    bpool[0] = [0, 1, 2, 3, 4, 5]
    wk["xkeys"] = ["xr16"]; wk["hkeys"] = ["hTs"]
    norm_to_hT(xresv[:TS, 16, :], TS, 1, hTsv, 0, wk)
    for c in range(8):
        b_ = proj_fm(Wqv, 8, c * 128, hTsv, 0, TS, ["hTs"], "Wq")
        P.op("act", lambda e, c=c, b_=b_: e.activation(qTsv[:, c, :], psb[b_][:, 0:TS], AF.Copy, scale=1.0 / 16.0),
             reads=[pk[b_], "qTs"], writes=["qTs"])
    for tl in range(4):
        norm_tile(0, tl)
    q_proj2(512, hTv2[0], "hTa0")
    memt = [carve(D), carve(D)]
    mT = carve(8 * 256, BF16); mTv = mT.rearrange("p (k t) -> p k t", k=8)
    kvo = carve(D)
    assert pos[0] <= PRE0, pos[0]
    for mt in range(2):
        P.dma("sp", memt[mt], mem[mt * 128:(mt + 1) * 128, :], writes=["memt%d" % mt])
        wk["xkeys"] = ["memt%d" % mt]; wk["hkeys"] = ["mT"]
        norm_to_hT(memt[mt], 128, 3, mTv, mt * 128, wk)
    for c in range(8):
        b = proj_fm(Wkvv, 8, c * 128, mTv, 0, 256, ["mT"], "Wkv")
        P.op("act", lambda e, c=c, b=b: e.activation(kTv[:, c, :], psb[b][:, 0:256], AF.Copy), reads=[pk[b], "kT"], writes=["kT"])

    def mem_tm(out_dram, okey, also_bf=None, Wv_=None, wkey_="Wkv"):
        Wv_ = Wkvv if Wv_ is None else Wv_
        for mt in range(2):
            for h in range(2):
                b = bank()
                for kt in range(8):
                    P.op("pe", lambda e, kt=kt, b=b, h=h, mt=mt: e.matmul(psb[b], mTv[:, kt, mt * 128:(mt + 1) * 128],
                                                                          Wv_[:, kt, h * 512:(h + 1) * 512], start=(kt == 0), stop=(kt == 7)),
                         reads=["mT", wkey_], writes=[pk[b]])
                P.op("act", lambda e, b=b, h=h: e.activation(kvo[:, h * 512:(h + 1) * 512], psb[b], AF.Copy), reads=[pk[b], "kvo"], writes=["kvo"])
                if also_bf is not None:
                    P.op("dve", lambda e, b=b, h=h, mt=mt: e.tensor_copy(also_bf[:, mt, h * 512:(h + 1) * 512], psb[b]), reads=[pk[b], "vb"], writes=["vb"])
            P.dma("sp", out_dram[mt * 128:(mt + 1) * 128, :], kvo, key=okey, reads=["kvo"])

    mem_tm(kp, "o_kv")
    mem_tm(vp, "o_kv", also_bf=vbv, Wv_=Wov, wkey_="Wo")
    load_w(Wov, w_o, 8, "Wo")
    P.barrier()
    pos[0] = p2_mark
    if stop == 4:
        return finish()
    oT = carve(8 * 512, BF16); oTv = oT.rearrange("p (k t) -> p k t", k=8)
    pT = carve(2 * 512, BF16); pTv = pT.rearrange("p (m t) -> p m t", m=2)
    rec = carve(512)

    def attn_tail(tiles):
        for i, (t_, n) in enumerate(tiles):
            proj_tm_postnorm_residual(oTv, 8, i * 128, n, Wov, "Wo", ["oT"], 1, xresv[:n, t_, :], "xr%d" % t_, wk)

    def q_proj(n):
        for c in range(8):
            b = proj_fm(Wqv, 8, c * 128, hTv, 0, n, ["hT"], "Wq")
            P.op("act", lambda e, c=c, b=b: e.activation(qTv[:, c, 0:n], psb[b][:, 0:n], AF.Copy, scale=1.0 / 16.0),
                 reads=[pk[b], "qT"], writes=["qT"])

    pT2 = [pT, carve(2 * 512, BF16)]
    pTv2 = [x.rearrange("p (m t) -> p m t", m=2) for x in pT2]
    rec2 = [rec, rec]

    def head_scores(h, z):
        bs = [bank(), bank()]
        for mc in range(2):
            for dc in range(2):
                P.op("pe", lambda e, h=h, mc=mc, dc=dc, b=bs[mc]: e.matmul(psb[b], kTv[:, 2 * h + dc, mc * 128:(mc + 1) * 128],
                                                                           qTv[:, 2 * h + dc, :], start=(dc == 0), stop=(dc == 1)),
                     reads=["kT", "qT"], writes=[pk[bs[mc]]])
            P.op("act", lambda e, mc=mc, b=bs[mc], z=z: e.activation(pTv2[z][:, mc, :], psb[b], AF.Exp),
                 reads=[pk[bs[mc]], "pT%d_%d" % (z, mc)], writes=["pT%d_%d" % (z, mc)])

    def head_finish(h, z):
        bc = bank()
        for mc in range(2):
            P.op("pe", lambda e, mc=mc, bc=bc, z=z: e.matmul(psb[bc], onesb, pTv2[z][:, mc, :], start=(mc == 0), stop=(mc == 1)),
                 reads=["onesb", "pT%d_%d" % (z, mc)], writes=[pk[bc]])
        P.op("act", lambda e, bc=bc, z=z: e.activation(rec2[z], psb[bc], AF.Ln), reads=[pk[bc], "rec"], writes=["rec"])
        P.op("act", lambda e, z=z: e.activation(rec2[z], rec2[z], AF.Exp, scale=-1.0), reads=["rec"], writes=["rec"])
        for dc in range(2):
            bo = bank()
            for mc in range(2):
                P.op("pe", lambda e, h=h, mc=mc, dc=dc, bo=bo, z=z: e.matmul(psb[bo], vbv[:, mc, (2 * h + dc) * 128:(2 * h + dc + 1) * 128],
                                                                             pTv2[z][:, mc, :], start=(mc == 0), stop=(mc == 1)),
                     reads=["vb", "pT%d_%d" % (z, mc)], writes=[pk[bo]])
            P.op("dve", lambda e, h=h, dc=dc, bo=bo, z=z: e.tensor_tensor(oTv[:, 2 * h + dc, :], psb[bo], rec2[z], ALU.mult),
                 reads=[pk[bo], "rec", "oT"], writes=["oT"])

    kcb = [carve(2 * D, BF16), carve(2 * D, BF16)]
    vcb = [carve(2 * D, BF16), carve(2 * D, BF16)]
    kcbv = [x.rearrange("p (m n) -> p m n", m=2) for x in kcb]
    vcbv = [x.rearrange("p (m n) -> p m n", m=2) for x in vcb]
    kTs = [carve(8 * 256, BF16), carve(8 * 256, BF16)]
    kTsv = [x.rearrange("p (k t) -> p k t", k=8) for x in kTs]
    pTs = carve(512, BF16); pTsv = pTs.rearrange("p (s m h k) -> p s m h k", s=NS, m=2, h=4)
    assert pos[0] <= PRE0, pos[0]
    bsc, bos = 6, 7
    bpool[0] = [0, 1, 2, 3, 4, 5]
    def load_seq(s):
        z = s % 2
        P.dma("pool", kcbv[z], ck[s].rearrange("(m p) n -> p m n", p=128), writes=["kcb%d" % z])
        P.dma("pool", vcbv[z], cv[s].rearrange("(m p) n -> p m n", p=128), writes=["vcb%d" % z])

    def sample_seq(s):
        z = s % 2
        for half in range(2):
            b = bank()
            pt = psb[b].bitcast(BF16)
            for j in range(8):
                idx = half * 8 + j
                dcn, mc = idx // 2, idx % 2
                P.op("pe", lambda e, z=z, dcn=dcn, mc=mc, j=j, pt=pt: e.transpose(pt[:, j * 128:(j + 1) * 128], kcbv[z][:, mc, dcn * 128:(dcn + 1) * 128], idb),
                     reads=["kcb%d" % z, "idb"], writes=[pk[b]])
            if half == 0:
                P.op("act", lambda e, z=z, half=half, pt=pt: e.activation(kTsv[z][:, half * 4:(half + 1) * 4, :], pt.rearrange("p (d x) -> p d x", d=4), AF.Copy),
                     reads=[pk[b], "kTs%d" % z], writes=["kTs%d" % z])
            else:
                P.op("dve", lambda e, z=z, half=half, pt=pt: e.tensor_copy(kTsv[z][:, half * 4:(half + 1) * 4, :], pt.rearrange("p (d x) -> p d x", d=4)),
                     reads=[pk[b], "kTs%d" % z], writes=["kTs%d" % z])
        for mc in range(2):
            for h in range(4):
                col = ((s * 2 + mc) * 4 + h) * 4
                for dc in range(2):
                    P.op("pe", lambda e, z=z, h=h, mc=mc, dc=dc, col=col, s=s: e.matmul(psb[bsc][:, col:col + 4], kTsv[z][:, 2 * h + dc, mc * 128:(mc + 1) * 128],
                                                                                      qTsv[:, 2 * h + dc, s * 4:(s + 1) * 4], start=(dc == 0), stop=(dc == 1)),
                         reads=["kTs%d" % z, "qTs"], writes=[pk[bsc]])
        P.op("act", lambda e, s=s: e.activation(pTs[:, s * 32:(s + 1) * 32], psb[bsc][:, s * 32:(s + 1) * 32], AF.Exp),
             reads=[pk[bsc], "pTs"], writes=["pTs"])
        for h in range(4):
            for dc in range(2):
                col = (2 * h + dc) * TS + s * 4
                for mc in range(2):
                    P.op("pe", lambda e, z=z, h=h, mc=mc, dc=dc, col=col, s=s: e.matmul(psb[bos][:, col:col + 4], vcbv[z][:, mc, (2 * h + dc) * 128:(2 * h + dc + 1) * 128],
                                                                                      pTsv[:, s, mc, h, :], start=(mc == 0), stop=(mc == 1)),
                         reads=["vcb%d" % z, "pTs"], writes=[pk[bos]])

    load_seq(0)
    load_seq(1)
    for blk in range(4):
        if blk > 0:
            q_proj2(512, hTv2[blk % 2], "hTa%d" % (blk % 2))
        pend = None
        for h in range(4):
            head_scores(h, h % 2)
            if pend is not None:
                head_finish(*pend)
            pend = (h, h % 2)
            if blk + 1 < 4:
                norm_tile(blk + 1, h)
            s_ = blk * 4 + h
            sample_seq(s_)
            if s_ + 2 < NS:
                load_seq(s_ + 2)
        head_finish(*pend)
        attn_tail([(blk * 4 + tl, 128) for tl in range(4)])

    bc = bank()
    for mc in range(2):
        P.op("pe", lambda e, mc=mc, bc=bc: e.matmul(psb[bc][:, 0:256], onesb, pTsv[:, :, mc, :, :], start=(mc == 0), stop=(mc == 1)),
             reads=["onesb", "pTs"], writes=[pk[bc]])
    P.op("dve", lambda e, bc=bc: e.reciprocal(rec[:, 0:256], psb[bc][:, 0:256]), reads=[pk[bc]], writes=["rec"])
    recv = rec[:, 0:256].rearrange("p (s h k) -> p s h k", s=NS, h=4)
    osv = psb[bos].rearrange("p (c s k) -> p c s k", c=8, s=NS)
    for h in range(4):
        for dc in range(2):
            P.op("dve", lambda e, h=h, dc=dc: e.tensor_tensor(oTv[:, 2 * h + dc, 0:TS].rearrange("p (s k) -> p s k", s=NS),
                                                              osv[:, 2 * h + dc, :, :], recv[:, :, h, :], ALU.mult),
                 reads=[pk[bos], "rec", "oT"], writes=["oT"])
    bpool[0] = list(range(8))
    attn_tail([(16, TS)])
    dbg("x2", xres, [128, 17 * D], ["xr%d" % i for i in range(17)])
    P.barrier()
    pos[0] = persist_mark
    if stop == 5:
        return finish()

    load_gbc(5)
    Wd = carve(NFF * D, BF16); Wdv = Wd.rearrange("p (k n) -> p k n", k=NFF)
    wk = dict(ss=carve(1), xn=carve(D, BF16), ss2=[carve(4), carve(4)], tmp=[carve(D), carve(D)]); wk["junk"] = wk["xn"]
    pj_ = carve(D, BF16); wk["pjunk"] = [pj_, pj_]
    NH = 768
    JW = 256
    hT2 = [carve(8 * NH, BF16), carve(8 * NH, BF16)]
    hTv2 = [x.rearrange("p (k t) -> p k t", k=8) for x in hT2]
    act = carve(NFF * NH, BF16); actv = act.rearrange("p (k t) -> p k t", k=NFF)
    wg = [carve(8 * JW, BF16), carve(8 * JW, BF16)]
    wu = [carve(8 * JW, BF16), carve(8 * JW, BF16)]
    wgv = [x.rearrange("p (k n) -> p k n", k=8) for x in wg]
    wuv = [x.rearrange("p (k n) -> p k n", k=8) for x in wu]
    sgt = carve(512); gt = carve(512)
    groups = [[(i, 128) for i in range(0, 6)], [(i, 128) for i in range(6, 12)], [(i, 128) for i in range(12, 16)] + [(16, TS)]]
    wstep = 0
    NJJ = NFF * 128 // JW
    loaded = set()

    def load_step(s):
        if s in loaded or s >= NJJ * len(groups):
            return
        loaded.add(s)
        z_, jj_ = s % 2, s % NJJ
        P.dma("pool", wgv[z_], w_gate[:, jj_ * JW:(jj_ + 1) * JW].rearrange("(k p) n -> p k n", p=128), writes=["wg%d" % z_])
        P.dma("pool", wuv[z_], w_up[:, jj_ * JW:(jj_ + 1) * JW].rearrange("(k p) n -> p k n", p=128), writes=["wu%d" % z_])
        if 1 <= s <= 8:
            k0 = (s - 1) * 3
            k1 = min(k0 + 3, NFF)
            P.dma("pool", Wdv[:, k0:k1, :], w_down[k0 * 128:k1 * 128, :].rearrange("(k p) n -> p k n", p=128),
                  reads=["Wd"], writes=["Wd"])

    load_step(0)
    for i, (t_, n) in enumerate(groups[0]):
        wk["xkeys"] = ["xr%d" % t_]; wk["hkeys"] = ["hTf0"]
        norm_to_hT(xresv[:n, t_, :], n, 2, hTv2[0], i * 128, wk)
    for gi_, tiles in enumerate(groups):
        zg = gi_ % 2
        hTv_, hk_ = hTv2[zg], "hTf%d" % zg
        ntok = sum(n for _, n in tiles)
        blocks = [(c0, min(512, ntok - c0)) for c0 in range(0, ntok, 512)]
        for jj in range(NFF * 128 // JW):
            z = wstep % 2
            load_step(wstep)
            load_step(wstep + 1)
            wstep += 1
            for sub in range(JW // 128):
                j = jj * (JW // 128) + sub
                for (c0, n) in blocks:
                    bg = proj_fm(wgv[z], 8, sub * 128, hTv_, c0, n, [hk_], "wg%d" % z)
                    bu = proj_fm(wuv[z], 8, sub * 128, hTv_, c0, n, [hk_], "wu%d" % z)
                    P.op("act", lambda e, bg=bg, n=n: e.activation(sgt[:, 0:n], psb[bg][:, 0:n], AF.Sigmoid), reads=[pk[bg], "sgt"], writes=["sgt"])
                    P.op("dve", lambda e, bg=bg, n=n: e.tensor_tensor(gt[:, 0:n], psb[bg][:, 0:n], sgt[:, 0:n], ALU.mult), reads=[pk[bg], "sgt", "gt"], writes=["gt"])
                    P.op("dve", lambda e, bu=bu, n=n, j=j, c0=c0: e.tensor_tensor(actv[:, j, c0:c0 + n], gt[:, 0:n], psb[bu][:, 0:n], ALU.mult),
                         reads=[pk[bu], "gt", "act"], writes=["act"])
        nxt_tiles = groups[gi_ + 1] if gi_ + 1 < len(groups) else []
        zn = (gi_ + 1) % 2
        for i, (t_, n) in enumerate(tiles):
            od = yp[t_ * 128:(t_ + 1) * 128, :] if t_ < 16 else ys
            hook = None
            if i < len(nxt_tiles):
                t2, n2 = nxt_tiles[i]
                wk["xkeys"] = ["xr%d" % t2]; wk["hkeys"] = ["hTf%d" % zn]
                norm_pre(xresv[:n2, t2, :], n2, wk)
                hook = (lambda i=i, n2=n2, zn=zn: norm_T(n2, 2, hTv2[zn], i * 128, wk))
            proj_tm_postnorm_residual(actv, NFF, i * 128, n, Wdv, "Wd", ["act"], 2, xresv[:n, t_, :], "xr%d" % t_, wk,
                                      out_dram=od, okey="o_y%d" % (t_ % 4), mid_hook=hook)
        for i in range(len(tiles), len(nxt_tiles)):
            t2, n2 = nxt_tiles[i]
            wk["xkeys"] = ["xr%d" % t2]; wk["hkeys"] = ["hTf%d" % zn]
            norm_to_hT(xresv[:n2, t2, :], n2, 2, hTv2[zn], i * 128, wk)
    return finish()


_NC_CACHE = {}


def _shard_inputs(inp):
    f = lambda a: np.ascontiguousarray(a, dtype=np.float32)
    shared = {
        "norm_g": f(inp["norm_g"][0]), "mem_g": f(inp["mem_norm_g"][0]), "w_in": f(inp["w_in"][0]),
        "w_dw": f(inp["w_dw"][0]), "b_dw": f(inp["b_dw"][0]), "ln_g": f(inp["ln_g"][0]), "ln_b": f(inp["ln_b"][0]),
        "lam_re": f(inp["lam_re"][0]), "lam_im": f(inp["lam_im"][0]), "log_dt": f(inp["log_dt"][0]),
        "b_re": f(inp["b_re"][0]), "b_im": f(inp["b_im"][0]), "c_re": f(inp["c_re"][0]), "c_im": f(inp["c_im"][0]),
        "d_skip": f(inp["d_skip"][0]), "w_glu": f(inp["w_glu"][0]), "w_out": f(inp["w_out"][0]),
        "w_q": f(inp["w_q"][0]), "w_k": f(inp["w_k"][0]), "w_v": f(inp["w_v"][0]), "w_o": f(inp["w_o"][0]),
        "w_gate": f(inp["w_gate"][0]), "w_up": f(inp["w_up"][0]), "w_down": f(inp["w_down"][0]),
    }
    maps = []
    for c in range(NCORES):
        s0, s1 = c * NS, (c + 1) * NS
        m = dict(shared)
        m["xp"] = f(inp["x_prompt"][c])
        m["xs"] = f(inp["x_sample"][s0:s1]).reshape(TS, D)
        m["mem"] = f(inp["mem_prompt"][c])
        m["ck"] = f(inp["cache_mem_k"][0, s0:s1]).reshape(NS, 256, D)
        m["cv"] = f(inp["cache_mem_v"][0, s0:s1]).reshape(NS, 256, D)
        m["sconv"] = f(inp["state_conv"][0, s0:s1])
        m["sre"] = f(inp["state_ssm_re"][0, s0:s1]).reshape(NS, 2048)
        m["sim"] = f(inp["state_ssm_im"][0, s0:s1]).reshape(NS, 2048)
        maps.append(m)
    return maps


def kernel(**inputs):
    inp = {k: np.asarray(v) for k, v in inputs.items()}
    if "nc" not in _NC_CACHE:
        import os
        _NC_CACHE["nc"] = build(stop=int(os.environ.get("KSTOP", "99")))
    nc = _NC_CACHE["nc"]
    maps = _shard_inputs(inp)
    res = run_bass_kernel_spmd(nc, maps, core_ids=list(range(NCORES)))
    R = res.results
    cat = lambda k: np.stack([np.asarray(R[c][k], dtype=np.float32) for c in range(NCORES)], axis=0)
    y_p = cat("yp")
    y_s = cat("ys").reshape(128, 4, D)
    k_p = cat("kp").reshape(1, 8, 256, 4, 256)
    v_p = cat("vp").reshape(1, 8, 256, 4, 256)
    c_p = cat("cp").reshape(1, 8, 30, 512)
    h_pr = cat("hpr").reshape(1, 8, 32, 64)
    h_pi = cat("hpi").reshape(1, 8, 32, 64)
    c_s = cat("cs").reshape(1, 128, 30, 512)
    h_sr = cat("hsr").reshape(1, 128, 32, 64)
    h_si = cat("hsi").reshape(1, 128, 32, 64)
    return (y_p, y_s, k_p, v_p, c_p, h_pr, h_pi, c_s, h_sr, h_si)
```

```python
import contextlib
import math
import numpy as np
import concourse.bass as bass
import concourse.mybir as mybir
from concourse.bass_utils import run_bass_kernel_spmd

F32 = mybir.dt.float32
BF16 = mybir.dt.bfloat16
I32 = mybir.dt.int32
ALU = mybir.AluOpType
AF = mybir.ActivationFunctionType

ENGS = ("pe", "act", "dve", "pool", "sp")
NCORES = 8
T = 2048
NS = 16
TS = 64
D = 1024
DFF = 2816
NFF = 22
LCH = 128
RMS_EPS = 1e-6
LN_EPS = 1e-5


class Prog:
    def __init__(self, nc, same_engine_sync=True):
        self.nc = nc
        self.ins = []
        self.last_w = {}
        self.readers = {}
        self.dma_keys = {}
        self.same_engine_sync = same_engine_sync

    def _add(self, eng, fn, reads, writes, dma_key=None, extra_deps=()):
        i = len(self.ins)
        deps = set(extra_deps)
        for k in reads:
            if k in self.last_w:
                deps.add(self.last_w[k])
            if k.startswith("ps") and k in self.readers:
                for e2, j in self.readers[k][0].items():
                    if e2 != eng:
                        deps.add(j)
        for k in writes:
            if k in self.last_w:
                deps.add(self.last_w[k])
            rd = self.readers.get(k)
            if rd:
                deps.update(rd[0].values())
                deps.update(rd[1])
        deps.discard(i)
        rec = dict(eng=eng, fn=fn, deps=deps, dma_key=dma_key, sig=False, cnt=None)
        if dma_key is not None:
            self.dma_keys[dma_key] = self.dma_keys.get(dma_key, 0) + 1
            rec["cnt"] = self.dma_keys[dma_key]
            rec["sig"] = True
        self.ins.append(rec)
        for k in writes:
            self.last_w[k] = i
            self.readers[k] = [{}, []]
        for k in reads:
            rd = self.readers.setdefault(k, [{}, []])
            if dma_key is None:
                rd[0][eng] = i
            else:
                rd[1].append(i)
        return i

    def op(self, eng, fn, reads=(), writes=()):
        return self._add(eng, fn, tuple(reads), tuple(writes))

    def dma(self, eng, out, in_, key=None, reads=(), writes=(), **kw):
        if key is None:
            key = "k_" + writes[0]

        def fn(e):
            return e.dma_start(out=out, in_=in_, **kw)
        return self._add(eng, fn, tuple(reads), tuple(writes), dma_key=key)

    def barrier(self):
        last = {}
        for i, rec in enumerate(self.ins):
            if rec["dma_key"] is None:
                last[("e", rec["eng"])] = i
            else:
                last[("d", rec["dma_key"])] = i
        deps = set(last.values())
        for e in ENGS:
            self._add(e, lambda eng: eng.nop(), (), (), extra_deps=deps)

    def emit(self, final_wait_eng="sp"):
        nc = self.nc
        ins = self.ins
        for rec in ins:
            for d in rec["deps"]:
                p = ins[d]
                if p["dma_key"] is None:
                    if p["eng"] != rec["eng"] or (self.same_engine_sync and p["eng"] != "pe"):
                        p["sig"] = True
        cnt = {e: 0 for e in ENGS}
        for rec in ins:
            if rec["dma_key"] is None and rec["sig"]:
                cnt[rec["eng"]] += 1
                rec["cnt"] = cnt[rec["eng"]]
        with contextlib.ExitStack() as st:
            esem = {e: st.enter_context(nc.semaphore("s_" + e)) for e in ENGS}
            dsem = {k: st.enter_context(nc.semaphore("d_" + str(k))) for k in self.dma_keys}
            block = st.enter_context(nc.Block())

            def run_engine(ename, eng):
                waited = {}
                for rec in ins:
                    if rec["eng"] != ename:
                        continue
                    need = {}
                    for d in rec["deps"]:
                        p = ins[d]
                        if p["dma_key"] is not None:
                            sem, val = dsem[p["dma_key"]], 16 * p["cnt"]
                            skey = ("d", p["dma_key"])
                        else:
                            if p["eng"] == ename and (ename == "pe" or not self.same_engine_sync):
                                continue
                            sem, val = esem[p["eng"]], p["cnt"]
                            skey = ("e", p["eng"])
                        if skey not in need or need[skey][1] < val:
                            need[skey] = (sem, val)
                    for skey in sorted(need, key=str):
                        sem, val = need[skey]
                        if waited.get(skey, 0) >= val:
                            continue
                        waited[skey] = val
                        eng.wait_ge(sem, val)
                    bi = rec["fn"](eng)
                    if rec["sig"]:
                        if rec["dma_key"] is not None:
                            bi.then_inc(dsem[rec["dma_key"]], 16)
                        else:
                            bi.then_inc(esem[ename], 1)
                if ename == final_wait_eng:
                    for k, n in self.dma_keys.items():
                        eng.wait_ge(dsem[k], 16 * n)

            @block.tensor
            def _(e):
                run_engine("pe", e)

            @block.scalar
            def _(e):
                run_engine("act", e)

            @block.vector
            def _(e):
                run_engine("dve", e)

            @block.gpsimd
            def _(e):
                run_engine("pool", e)

            @block.sync
            def _(e):
                run_engine("sp", e)


def build(debug_names=(), stop=99):
    nc = bass.Bass("TRN2", target_bir_lowering=False)

    def finish():
        print("SBUF high-water words:", hiw[0], "instructions:", len(P.ins))
        P.emit()
        nc_allow.__exit__(None, None, None)
        return nc

    def din(name, shape):
        return nc.dram_tensor(name, list(shape), F32, kind="ExternalInput").ap()

    def dout(name, shape):
        return nc.dram_tensor(name, list(shape), F32, kind="ExternalOutput").ap()

    xp = din("xp", [T, D]); xs = din("xs", [TS, D]); mem = din("mem", [256, D])
    ck = din("ck", [NS, 256, D]); cv = din("cv", [NS, 256, D])
    sconv = din("sconv", [NS, 30, 512]); sre = din("sre", [NS, 2048]); sim = din("sim", [NS, 2048])
    norm_g = din("norm_g", [6, D]); mem_g = din("mem_g", [D])
    w_in = din("w_in", [D, 1536]); w_dw = din("w_dw", [31, 512])
    b_dw = din("b_dw", [512]); ln_g = din("ln_g", [512]); ln_b = din("ln_b", [512])
    lam_re = din("lam_re", [32, 64]); lam_im = din("lam_im", [32, 64]); log_dt = din("log_dt", [32])
    b_re = din("b_re", [32, 64, 16]); b_im = din("b_im", [32, 64, 16])
    c_re = din("c_re", [32, 16, 64]); c_im = din("c_im", [32, 16, 64])
    d_skip = din("d_skip", [512]); w_glu = din("w_glu", [512, 512]); w_out = din("w_out", [D, D])
    w_q = din("w_q", [D, D]); w_k = din("w_k", [D, D]); w_v = din("w_v", [D, D]); w_o = din("w_o", [D, D])
    w_gate = din("w_gate", [D, DFF]); w_up = din("w_up", [D, DFF]); w_down = din("w_down", [DFF, D])

    yp = dout("yp", [T, D]); ys = dout("ys", [TS, D]); kp = dout("kp", [256, D]); vp = dout("vp", [256, D])
    cp = dout("cp", [30, 512]); hpr = dout("hpr", [16, 128]); hpi = dout("hpi", [16, 128])
    cs = dout("cs", [NS, 30, 512]); hsr = dout("hsr", [NS, 2048]); hsi = dout("hsi", [NS, 2048])

    AW = 53200
    A = nc.alloc_sbuf_tensor("arena", [128, AW], F32).ap()
    pos = [0]
    hiw = [0]

    def carve_at(off, n, dtype=F32):
        nb = n * (2 if dtype == BF16 else 4)
        nw = ((nb + 3) // 4 + 7) // 8 * 8
        v = A[:, off:off + nw]
        if dtype != F32:
            v = v.bitcast(dtype)
        return v[:, 0:n], nw

    def carve(n, dtype=F32):
        v, nw = carve_at(pos[0], n, dtype)
        assert pos[0] + nw <= AW, ("SBUF arena overflow", pos[0], nw)
        pos[0] += nw
        hiw[0] = max(hiw[0], pos[0])
        return v

    PS = nc.alloc_psum_tensor("ps", [128, 4096], F32).ap()
    psb = [PS[:, i * 512:(i + 1) * 512] for i in range(8)]
    pk = ["ps%d" % i for i in range(8)]
    bpool = [list(range(8))]
    bank_rr = [0]

    bank_cnt = {}

    def bank():
        pl = bpool[0]
        k = tuple(pl)
        c = bank_cnt.get(k, 0)
        bank_cnt[k] = c + 1
        return pl[c % len(pl)]

    P = Prog(nc)
    nc_allow = nc.allow_non_contiguous_dma(reason="small strided parameter loads")
    nc_allow.__enter__()

    def dbg(name, ap, shape, reads):
        if name in debug_names:
            o = dout("dbg_" + name, shape)
            P.dma("sp", o, ap, key="dbg_" + name, reads=reads)

    idi = carve(128, I32); idf = carve(128); idb = carve(128, BF16)
    onesf = carve(128); onesb = carve(128, BF16)
    P.op("pool", lambda e: e.iota(idi, [[1, 128]], base=0, channel_multiplier=-1), writes=["idi"])
    P.op("dve", lambda e: e.tensor_single_scalar(idf, idi, 0, ALU.is_equal), reads=["idi"], writes=["idf"])
    P.op("dve", lambda e: e.tensor_copy(idb, idf), reads=["idf"], writes=["idb"])
    P.op("dve", lambda e: e.memset(onesf, 1.0 / 512.0), writes=["onesf"])
    P.op("dve", lambda e: e.memset(onesb, 1.0), writes=["onesb"])
    cst = carve(8)
    P.op("dve", lambda e: e.memset(cst[:, 0:1], -0.5), writes=["cst"])
    negh = cst[:, 0:1]
    mk = carve(4)
    for r in range(4):
        P.op("dve", lambda e, r=r: e.reduce_sum(mk[:, r:r + 1], idf[:, 32 * r:32 * r + 32], axis=mybir.AxisListType.X),
             reads=["idf", "mk"], writes=["mk"])
    gcol = carve(4 * 8)
    gcolv = gcol.rearrange("p (a k) -> p a k", a=4)
    for a, src in enumerate([norm_g[0], norm_g[2], norm_g[4], mem_g]):
        P.dma("sp", gcolv[:, a, :], src.rearrange("(k p) -> p k", p=128), writes=["gcol"])
    pcol = carve(16)
    pcolv = pcol.rearrange("p (a k) -> p a k", a=4)
    for a, src in enumerate([b_dw, ln_g, ln_b, d_skip]):
        P.dma("sp", pcolv[:, a, :], src.rearrange("(k p) -> p k", p=128), writes=["pcol"])
    gbc = carve(D)
    def load_gbc(r):
        P.dma("sp", gbc, bass.AP(norm_g.tensor, r * D, [[0, 128], [1, D]]), writes=["gbc"])
    xres_off = pos[0]
    xres = carve(17 * D)
    xresv = xres.rearrange("p (t n) -> p t n", t=17)
    persist_mark = pos[0]

    def load_w(dst3, wdram, nkt, key):
        P.dma("pool", dst3, wdram.rearrange("(k p) n -> p k n", p=128), writes=[key])

    def rstd_from_ss(ss, n, scale, eps, tag):
        P.op("dve", lambda e: e.tensor_scalar(ss[:n], ss[:n], scale, eps, ALU.mult, ALU.add),
             reads=[tag], writes=[tag])
        P.op("pool", lambda e: e.tensor_tensor(ss[:n], ss[:n], negh[:n], ALU.pow),
             reads=[tag, "cst"], writes=[tag])

    def norm_to_hT(xt, n, gi, hT3, c0, wk):
        norm_pre(xt, n, wk)
        norm_T(n, gi, hT3, c0, wk)

    def norm_pre(xt, n, wk):
        ss, junk, xn = wk["ss"], wk["junk"], wk["xn"]
        P.op("act", lambda e: e.activation(junk[:n], xt, AF.Square, accum_out=ss[:n]),
             reads=wk["xkeys"] + ["xn"], writes=["xn", "ss"])
        rstd_from_ss(ss, n, 1.0 / D, RMS_EPS, "ss")
        P.op("dve", lambda e: e.tensor_scalar(xn[:n], xt, ss[:n], None, ALU.mult),
             reads=wk["xkeys"] + ["ss"], writes=["xn"])

    def norm_T(n, gi, hT3, c0, wk):
        xn = wk["xn"]
        hkeys = list(wk["hkeys"])
        b = bank()
        pt = psb[b].bitcast(BF16)
        for kt in range(8):
            P.op("pe", lambda e, kt=kt: e.transpose(pt[:, kt * 128:kt * 128 + n], xn[:n, kt * 128:(kt + 1) * 128],
                                                    idb[:n, :n]),
                 reads=["xn", "idb"], writes=[pk[b]])
        ptv = pt.rearrange("p (k t) -> p k t", k=8)
        P.op("dve", lambda e: e.tensor_tensor(hT3[:, :, c0:c0 + n], ptv[:, :, 0:n],
                                              gcolv[:, gi, :].unsqueeze(2).to_broadcast([128, 8, n]), ALU.mult),
             reads=[pk[b], "gcol"] + hkeys, writes=hkeys)

    def proj_fm(W3, nkt, ncol0, rhs3, c0, n, rkeys, wkey):
        b = bank()
        for kt in range(nkt):
            P.op("pe", lambda e, kt=kt: e.matmul(psb[b][:, 0:n], W3[:, kt, ncol0:ncol0 + 128], rhs3[:, kt, c0:c0 + n],
                                                 start=(kt == 0), stop=(kt == nkt - 1)),
                 reads=rkeys + [wkey], writes=[pk[b]])
        return b

    def proj_tm_postnorm_residual(lhs3, nkt, c0, n, W3, wkey, lkeys, gi, xt, xkey, wk, out_dram=None, okey=None, mid_hook=None):
        b0, b1 = bank(), bank()
        for h, b in enumerate((b0, b1)):
            for kt in range(nkt):
                P.op("pe", lambda e, kt=kt, b=b, h=h: e.matmul(psb[b][:n, :], lhs3[:, kt, c0:c0 + n],
                                                                W3[:, kt, h * 512:(h + 1) * 512],
                                                                start=(kt == 0), stop=(kt == nkt - 1)),
                     reads=lkeys + [wkey], writes=[pk[b]])
        if mid_hook is not None:
            mid_hook()
        z = wk["pp"] = 1 - wk.get("pp", 0)
        ss2, junk, tmp = wk["ss2"][z], wk["pjunk"][z], wk["tmp"][z]
        ks, kj, kt_ = "ss2_%d" % z, "pjunk%d" % z, "tmp%d" % z
        P.op("act", lambda e: e.activation(junk[:n, 0:512], psb[b0][:n, :], AF.Square, accum_out=ss2[:n, 0:1]),
             reads=[pk[b0], kj, ks], writes=[kj, ks])
        P.op("act", lambda e: e.activation(junk[:n, 512:1024], psb[b1][:n, :], AF.Square, accum_out=ss2[:n, 1:2]),
             reads=[pk[b1], kj, ks], writes=[kj, ks])
        P.op("dve", lambda e: e.tensor_tensor(ss2[:n, 2:3], ss2[:n, 0:1], ss2[:n, 1:2], ALU.add),
             reads=[ks], writes=[ks])
        rstd_from_ss(ss2[:, 2:3], n, 1.0 / D, RMS_EPS, ks)
        for h, b in enumerate((b0, b1)):
            P.op("dve", lambda e, h=h, b=b: e.scalar_tensor_tensor(tmp[:n, h * 512:(h + 1) * 512], psb[b][:n, :],
                                                                    ss2[:n, 2:3], gbc[:n, h * 512:(h + 1) * 512],
                                                                    ALU.mult, ALU.mult),
                 reads=[pk[b], ks, "gbc", kt_], writes=[kt_])
        P.op("dve", lambda e: e.tensor_tensor(xt, xt, tmp[:n, :], ALU.add), reads=[kt_, xkey], writes=[xkey])
        if out_dram is not None:
            P.dma("sp", out_dram, xt, key=okey, reads=[xkey])

    xo = [xres_off]

    def carve_x(n, dtype=F32):
        v, nw = carve_at(xo[0], n, dtype)
        xo[0] += nw
        assert xo[0] <= xres_off + 16 * D
        return v

    diag = carve_x(4 * 31 * 128, BF16)
    diagv = diag.rearrange("p (c k j) -> p c k j", c=4, k=31)
    LD = 64
    TC = carve_x(16 * LD); TSn = carve_x(16 * LD)
    TCv = TC.rearrange("p (t j) -> p t j", t=16); TSv = TSn.rearrange("p (t j) -> p t j", t=16)
    mix = carve(8 * (T + TS), BF16)
    mixv = mix.rearrange("p (k t) -> p k t", k=8)
    Bc = carve_x(4 * 2 * 4 * 128, BF16)
    Bcv = Bc.rearrange("p (s r k n) -> p s r k n", s=4, r=2, k=4)
    Cc = carve_x(4 * 2 * 16 * 32, BF16)
    Ccv = Cc.rearrange("p (k r t c) -> p k r t c", k=4, r=2, t=16)
    KT = carve_x(3 * 4 * 128, BF16)
    KTv = KT.rearrange("p (a k n) -> p a k n", a=3, k=4)
    pw = carve(16 * 8)
    pwv = pw.rearrange("p (a t) -> p a t", a=8)
    wdw = carve(4 * 32)
    wdwv = wdw.rearrange("p (c k) -> p c k", c=4)
    sp_ = carve(16 * 8)
    spv = sp_.rearrange("p (a t) -> p a t", a=8)
    Hst = carve(2 * 16)
    Hv = Hst.rearrange("p (r t) -> p r t", r=2)
    p1b_mark = pos[0]
    Win = carve(8 * 1536, BF16); Winv = Win.rearrange("p (k n) -> p k n", k=8)
    Wglu = carve(4 * 512, BF16); Wgluv = Wglu.rearrange("p (k n) -> p k n", k=4)
    load_w(Winv, w_in, 8, "Win")
    load_w(Wgluv, w_glu, 4, "Wglu")
    phase0_mark = pos[0]

    wdn = carve(512)
    P.dma("sp", wdn[:31, :], w_dw, writes=["wdn"])
    b = bank()
    for c in range(4):
        P.op("pe", lambda e, c=c, b=b: e.transpose(psb[b][:, c * 32:c * 32 + 31], wdn[:31, c * 128:(c + 1) * 128], idf[:31, :31]),
             reads=["wdn", "idf"], writes=[pk[b]])
    P.op("dve", lambda e, b=b: e.tensor_copy(wdwv[:, :, 0:31], psb[b][:, 0:128].rearrange("p (c k) -> p c k", c=4)[:, :, 0:31]),
         reads=[pk[b]], writes=["wdw"])
    for c in range(4):
        for k in range(31):
            if (c * 31 + k) % 2 == 0:
                P.op("dve", lambda e, c=c, k=k: e.tensor_scalar(diagv[:, c, k, :], idf, wdwv[:, c, k:k + 1], None, ALU.mult),
                     reads=["idf", "wdw"], writes=["diagA%d" % (k % 4)])
            else:
                P.op("act", lambda e, c=c, k=k: e.activation(diagv[:, c, k, :], idf, AF.Copy, scale=wdwv[:, c, k:k + 1]),
                     reads=["idf", "wdw"], writes=["diagB%d" % (k % 4)])

    t16 = carve(16 * 16)
    tv = t16.rearrange("p (a t) -> p a t", a=16)
    LR, LI, LDT = tv[:, 0, :], tv[:, 1, :], tv[:, 2, :]
    P.dma("sp", LR, lam_re.rearrange("(t g) n -> (g n) t", g=2), writes=["t16"])
    P.dma("sp", LI, lam_im.rearrange("(t g) n -> (g n) t", g=2), writes=["t16"])
    for g2 in range(2):
        P.dma("sp", tv[g2 * 64:(g2 + 1) * 64, 2, :], bass.AP(log_dt.tensor, g2, [[0, 64], [2, 16]]), writes=["t16"])
    K16 = ["t16"]

    def tt(out, a, b_, op, eng="dve", r=K16, w=K16):
        P.op(eng, lambda e: e.tensor_tensor(out, a, b_, op), reads=r, writes=w)

    def ts(out, a, s1, s2, op0, op1=None, eng="dve", r=K16, w=K16):
        if op1 is None:
            P.op(eng, lambda e: e.tensor_scalar(out, a, s1, None, op0), reads=r, writes=w)
        else:
            P.op(eng, lambda e: e.tensor_scalar(out, a, s1, s2, op0, op1), reads=r, writes=w)

    dt_, ere, th, sh, ch_, sn, x_, den = (tv[:, i, :] for i in range(3, 11))
    KS = ["t16", "sp"]
    P.op("act", lambda e: e.activation(dt_, LDT, AF.Exp), reads=K16, writes=K16)
    tt(ere, LR, dt_, ALU.mult)
    P.op("act", lambda e: e.activation(spv[:, 0, :], ere, AF.Exp), reads=KS, writes=KS)
    tt(th, LI, dt_, ALU.mult)
    NDBL = 4
    P.op("act", lambda e: e.activation(sh, th, AF.Sin, scale=1.0 / (2 ** (NDBL + 1))), reads=K16, writes=K16)
    P.op("act", lambda e: e.activation(sn, th, AF.Sin, scale=1.0 / (2 ** NDBL)), reads=K16, writes=K16)
    tt(ch_, sh, sh, ALU.mult)
    ts(ch_, ch_, -2.0, 1.0, ALU.mult, ALU.add)
    for _ in range(NDBL):
        tt(x_, ch_, sn, ALU.mult)
        tt(den, sn, sn, ALU.mult)
        ts(sn, x_, 2.0, None, ALU.mult)
        ts(ch_, den, -2.0, 1.0, ALU.mult, ALU.add)
    P.op("dve", lambda e: e.tensor_copy(spv[:, 5, :], ch_), reads=KS, writes=KS)
    P.op("dve", lambda e: e.tensor_copy(spv[:, 6, :], sn), reads=KS, writes=KS)
    tt(spv[:, 1, :], spv[:, 0, :], ch_, ALU.mult, r=KS, w=KS)
    tt(spv[:, 2, :], spv[:, 0, :], sn, ALU.mult, r=KS, w=KS)
    ts(x_, spv[:, 1, :], -1.0, None, ALU.add, r=KS)
    y_ = spv[:, 2, :]
    tt(den, LR, LR, ALU.mult)
    tt(sh, LI, LI, ALU.mult)
    tt(den, den, sh, ALU.add)
    P.op("dve", lambda e: e.reciprocal(den, den), reads=K16, writes=K16)
    tt(sh, x_, LR, ALU.mult)
    tt(th, y_, LI, ALU.mult, r=KS)
    tt(sh, sh, th, ALU.add)
    tt(spv[:, 3, :], sh, den, ALU.mult, r=KS, w=KS)
    tt(sh, y_, LR, ALU.mult, r=KS)
    tt(th, x_, LI, ALU.mult)
    tt(sh, sh, th, ALU.subtract)
    tt(spv[:, 4, :], sh, den, ALU.mult, r=KS, w=KS)

    KP = ["sp", "pw"]
    def ptt(out, a_, b_, op):
        P.op("dve", lambda e: e.tensor_tensor(out, a_, b_, op), reads=KP, writes=KP)
    def pts(out, a_, s1, s2, op0, op1):
        P.op("dve", lambda e: e.tensor_scalar(out, a_, s1, s2, op0, op1), reads=KP, writes=KP)
    ar_, ai_, tmp_ = spv[:, 1, :], spv[:, 2, :], pwv[:, 7, :]
    ptt(pwv[:, 0, :], ar_, ar_, ALU.mult); ptt(tmp_, ai_, ai_, ALU.mult); ptt(pwv[:, 0, :], pwv[:, 0, :], tmp_, ALU.subtract)
    ptt(pwv[:, 1, :], ar_, ai_, ALU.mult); pts(pwv[:, 1, :], pwv[:, 1, :], 2.0, 0.0, ALU.mult, ALU.add)
    ptt(pwv[:, 2, :], pwv[:, 0, :], ar_, ALU.mult); ptt(tmp_, pwv[:, 1, :], ai_, ALU.mult); ptt(pwv[:, 2, :], pwv[:, 2, :], tmp_, ALU.subtract)
    ptt(pwv[:, 3, :], pwv[:, 0, :], ai_, ALU.mult); ptt(tmp_, pwv[:, 1, :], ar_, ALU.mult); ptt(pwv[:, 3, :], pwv[:, 3, :], tmp_, ALU.add)
    ptt(pwv[:, 4, :], spv[:, 0, :], spv[:, 0, :], ALU.mult); ptt(pwv[:, 4, :], pwv[:, 4, :], pwv[:, 4, :], ALU.mult)
    ptt(pwv[:, 5, :], spv[:, 6, :], spv[:, 6, :], ALU.mult); pts(pwv[:, 5, :], pwv[:, 5, :], -2.0, 1.0, ALU.mult, ALU.add)
    ptt(pwv[:, 6, :], spv[:, 5, :], spv[:, 6, :], ALU.mult); pts(pwv[:, 6, :], pwv[:, 6, :], 2.0, 0.0, ALU.mult, ALU.add)
    ptt(tmp_, pwv[:, 5, :], pwv[:, 6, :], ALU.mult)
    ptt(pwv[:, 5, :], pwv[:, 6, :], pwv[:, 6, :], ALU.mult); pts(pwv[:, 5, :], pwv[:, 5, :], -2.0, 1.0, ALU.mult, ALU.add)
    pts(pwv[:, 6, :], tmp_, 2.0, 0.0, ALU.mult, ALU.add)
    tmpA = carve(16 * (LD // 2)); tmpB = carve(16 * (LD // 2))
    tAv = tmpA.rearrange("p (t j) -> p t j", t=16); tBv = tmpB.rearrange("p (t j) -> p t j", t=16)
    KTB = ["TC", "TS", "tmpAB"]
    P.op("dve", lambda e: e.tensor_copy(TCv[:, :, 0], pwv[:, 5, :]), reads=["pw"] + KTB, writes=KTB)
    P.op("dve", lambda e: e.tensor_copy(TSv[:, :, 0], pwv[:, 6, :]), reads=["pw"] + KTB, writes=KTB)
    n_ = 1
    while n_ < LD:
        cn = TCv[:, :, n_ - 1:n_].to_broadcast([128, 16, n_])
        snn = TSv[:, :, n_ - 1:n_].to_broadcast([128, 16, n_])
        c0_, s0_ = TCv[:, :, 0:n_], TSv[:, :, 0:n_]
        a_, b__ = tAv[:, :, 0:n_], tBv[:, :, 0:n_]
        P.op("dve", lambda e, a_=a_, c0_=c0_, cn=cn: e.tensor_tensor(a_, c0_, cn, ALU.mult), reads=KTB, writes=KTB)
        P.op("dve", lambda e, b__=b__, s0_=s0_, snn=snn: e.tensor_tensor(b__, s0_, snn, ALU.mult), reads=KTB, writes=KTB)
        P.op("dve", lambda e, n_=n_, a_=a_, b__=b__: e.tensor_tensor(TCv[:, :, n_:2 * n_], a_, b__, ALU.subtract),
             reads=KTB, writes=KTB)
        P.op("dve", lambda e, a_=a_, c0_=c0_, snn=snn: e.tensor_tensor(a_, c0_, snn, ALU.mult), reads=KTB, writes=KTB)
        P.op("dve", lambda e, b__=b__, s0_=s0_, cn=cn: e.tensor_tensor(b__, s0_, cn, ALU.mult), reads=KTB, writes=KTB)
        P.op("dve", lambda e, n_=n_, a_=a_, b__=b__: e.tensor_tensor(TSv[:, :, n_:2 * n_], a_, b__, ALU.add),
             reads=KTB, writes=KTB)
        n_ *= 2

    Bp = [carve(16 * 128), carve(16 * 128)]
    Bb = [carve(16 * 128), carve(16 * 128)]
    Bpv = [x.rearrange("p (t s) -> p t s", t=16) for x in Bp]
    Bbv = [x.rearrange("p (t s) -> p t s", t=16) for x in Bb]
    KB = ["Bp"]
    for ri, src in enumerate([b_re, b_im]):
        P.op("pool", lambda e, ri=ri: e.memset(Bp[ri], 0.0), reads=KB, writes=KB)
    for ri, src in enumerate([b_re, b_im]):
        for g2 in range(2):
            for i in range(4):
                col = (2 * i + g2) * 16
                dst = Bpv[ri][g2 * 64:(g2 + 1) * 64, :, col:col + 16].rearrange("p (q i) c -> p q i c", i=4)[:, :, i, :]
                s_ = bass.AP(src.tensor, (2 * i + g2) * 1024, [[16, 64], [8 * 1024, 4], [1, 16]])
                P.dma("sp", dst, s_, reads=KB, writes=KB)
    crb = spv[:, 3, :].unsqueeze(2).to_broadcast([128, 16, 128])
    cib = spv[:, 4, :].unsqueeze(2).to_broadcast([128, 16, 128])
    tb1 = carve(16 * 128); tb1v = tb1.rearrange("p (t s) -> p t s", t=16)
    KBS = ["Bp", "Bb", "sp", "tb1"]
    P.op("dve", lambda e: e.tensor_tensor(Bbv[0], Bpv[0], crb, ALU.mult), reads=KBS, writes=["Bb"])
    P.op("dve", lambda e: e.tensor_tensor(tb1v, Bpv[1], cib, ALU.mult), reads=KBS, writes=["tb1"])
    P.op("dve", lambda e: e.tensor_tensor(Bbv[0], Bbv[0], tb1v, ALU.subtract), reads=KBS, writes=["Bb"])
    P.op("dve", lambda e: e.tensor_tensor(Bbv[1], Bpv[1], crb, ALU.mult), reads=KBS, writes=["Bb"])
    P.op("dve", lambda e: e.tensor_tensor(tb1v, Bpv[0], cib, ALU.mult), reads=KBS, writes=["tb1"])
    P.op("dve", lambda e: e.tensor_tensor(Bbv[1], Bbv[1], tb1v, ALU.add), reads=KBS, writes=["Bb"])
    CTf = [carve(16 * 128), carve(16 * 128)]
    CTfv = [x.rearrange("p (t s) -> p t s", t=16) for x in CTf]
    Wt_alt = [(Bpv, "Bp"), (CTfv, "CTf")]
    pows = {1: (spv[:, 1, :], spv[:, 2, :]), 2: (pwv[:, 0, :], pwv[:, 1, :]), 3: (pwv[:, 2, :], pwv[:, 3, :])}
    KW = ["Bp", "Bb", "sp", "pw", "tb1"]
    for s in range(4):
        if s == 3:
            srcv, skey = Bbv, "Bb"
        else:
            pr = pows[3 - s][0].unsqueeze(2).to_broadcast([128, 16, 128])
            pi = pows[3 - s][1].unsqueeze(2).to_broadcast([128, 16, 128])
            Wtv, wkey_ = Wt_alt[s % 2]
            KW_ = ["Bb", "sp", "pw", "tb1", wkey_]
            P.op("dve", lambda e, pr=pr, Wtv=Wtv: e.tensor_tensor(Wtv[0], Bbv[0], pr, ALU.mult), reads=KW_, writes=[wkey_])
            P.op("dve", lambda e, pi=pi: e.tensor_tensor(tb1v, Bbv[1], pi, ALU.mult), reads=KW_, writes=["tb1"])
            P.op("dve", lambda e, Wtv=Wtv: e.tensor_tensor(Wtv[0], Wtv[0], tb1v, ALU.subtract), reads=KW_, writes=[wkey_])
            P.op("dve", lambda e, pi=pi, Wtv=Wtv: e.tensor_tensor(Wtv[1], Bbv[0], pi, ALU.mult), reads=KW_, writes=[wkey_])
            P.op("dve", lambda e, pr=pr: e.tensor_tensor(tb1v, Bbv[1], pr, ALU.mult), reads=KW_, writes=["tb1"])
            P.op("dve", lambda e, Wtv=Wtv: e.tensor_tensor(Wtv[1], Wtv[1], tb1v, ALU.add), reads=KW_, writes=[wkey_])
            srcv, skey = Wtv, wkey_
        for ri in range(2):
            b = bank()
            for q in range(4):
                for j in range(4):
                    ti = q * 4 + j
                    P.op("pe", lambda e, ri=ri, ti=ti, q=q, j=j, b=b, srcv=srcv: e.matmul(psb[b][:, q * 128:(q + 1) * 128], srcv[ri][:, ti, :], idf,
                                                                                          start=(j == 0), stop=(j == 3)),
                         reads=[skey, "idf"], writes=[pk[b]])
            P.op("act", lambda e, s=s, ri=ri, b=b: e.activation(Bcv[:, s, ri, :, :], psb[b].rearrange("p (k n) -> p k n", k=4), AF.Copy),
                 reads=[pk[b], "Bc"], writes=["Bc"])
    if stop == 30:
        P.barrier()
        return finish()
    Cp = [Bp[0], Bp[1]]
    Cpv = Bpv
    for ri, src in enumerate([c_re, c_im]):
        P.op("pool", lambda e, ri=ri: e.memset(Cp[ri], 0.0), reads=KB, writes=KB)
    for ri, src in enumerate([c_re, c_im]):
        for g2 in range(2):
            for i in range(4):
                p0 = 32 * i + 16 * g2
                dst = Cpv[ri][p0:p0 + 16, :, g2 * 64:(g2 + 1) * 64].rearrange("p (q i) c -> p q i c", i=4)[:, :, i, :]
                s_ = bass.AP(src.tensor, (2 * i + g2) * 1024, [[64, 16], [8 * 1024, 4], [1, 64]])
                P.dma("sp", dst, s_, key="k_Cp", reads=KB, writes=KB)
    for ri in range(2):
        for q in range(4):
            b = bank()
            for j in range(4):
                ti = q * 4 + j
                P.op("pe", lambda e, ri=ri, ti=ti, j=j, b=b: e.transpose(psb[b][:, j * 128:(j + 1) * 128], Cpv[ri][:, ti, :], idf),
                     reads=["Bp", "idf"], writes=[pk[b]])
            P.op("act", lambda e, ri=ri, q=q, b=b: e.activation(CTfv[ri][:, q * 4:(q + 1) * 4, :],
                                                                 psb[b].rearrange("p (j s) -> p j s", j=4), AF.Copy),
                 reads=[pk[b], "CTf"], writes=["CTf"])
    if stop == 31:
        P.barrier()
        return finish()
    Ck = [Bp[0], Bp[1]]; Ckv = Bpv
    KC = ["Bp", "CTf", "sp", "pw", "tb1"]
    for k in range(4):
        if k == 0:
            P.op("dve", lambda e: e.tensor_copy(Ckv[0], CTfv[0]), reads=KC, writes=["Bp"])
            P.op("dve", lambda e: e.tensor_scalar(Ckv[1], CTfv[1], -1.0, None, ALU.mult), reads=KC, writes=["Bp"])
        else:
            pr = pows[k][0].unsqueeze(2).to_broadcast([128, 16, 128])
            pi = pows[k][1].unsqueeze(2).to_broadcast([128, 16, 128])
            P.op("dve", lambda e, pr=pr: e.tensor_tensor(Ckv[0], CTfv[0], pr, ALU.mult), reads=KC, writes=["Bp"])
            P.op("dve", lambda e, pi=pi: e.tensor_tensor(tb1v, CTfv[1], pi, ALU.mult), reads=KC, writes=["tb1"])
            P.op("dve", lambda e: e.tensor_tensor(Ckv[0], Ckv[0], tb1v, ALU.subtract), reads=KC, writes=["Bp"])
            P.op("dve", lambda e, pi=pi: e.tensor_tensor(Ckv[1], CTfv[0], pi, ALU.mult), reads=KC, writes=["Bp"])
            P.op("dve", lambda e, pr=pr: e.tensor_tensor(tb1v, CTfv[1], pr, ALU.mult), reads=KC, writes=["tb1"])
            P.op("dve", lambda e: e.tensor_tensor(Ckv[1], Ckv[1], tb1v, ALU.add), reads=KC, writes=["Bp"])
            P.op("dve", lambda e: e.tensor_scalar(Ckv[1], Ckv[1], -1.0, None, ALU.mult), reads=KC, writes=["Bp"])
        for ri in range(2):
            for r in range(4):
                P.op("act", lambda e, k=k, ri=ri, r=r: e.activation(
                    Ccv[:, k, ri, :, :].rearrange("p (q r) c -> p q r c", r=4)[:, :, r, :],
                    Ckv[ri].rearrange("p (q r) c -> p q r c", r=4)[:, :, r, 32 * r:32 * r + 32], AF.Copy),
                    reads=["Bp", "Cc"], writes=["Cc"])
        if stop == 32:
            P.barrier()
            return finish()
        if k < 3:
            b = bank()
            for kc in range(4):
                first = True
                for j in range(4):
                    ti = kc * 4 + j
                    for ri in range(2):
                        P.op("pe", lambda e, ri=ri, ti=ti, kc=kc, b=b, first=first, last=(j == 3 and ri == 1):
                             e.matmul(psb[b][:, kc * 128:(kc + 1) * 128], Bbv[ri][:, ti, :], Ckv[ri][:, ti, :], start=first, stop=last),
                             reads=["Bb", "Bp"], writes=[pk[b]])
                        first = False
            if stop == 33:
                P.barrier()
                return finish()
            P.op("act", lambda e, k=k, b=b: e.activation(KTv[:, k, :, :], psb[b].rearrange("p (k n) -> p k n", k=4), AF.Copy),
                 reads=[pk[b], "KTw"], writes=["KTw"])
            if stop == 34:
                P.barrier()
                return finish()
    P.op("pool", lambda e: e.tensor_tensor(tb1[:, 0:512], tb1[:, 512:1024], tb1[:, 1024:1536], ALU.mult), reads=["tb1"], writes=["tb1"])
    P.op("pool", lambda e: e.tensor_tensor(tb1[:, 0:512], tb1[:, 512:1024], tb1[:, 1024:1536], ALU.subtract), reads=["tb1"], writes=["tb1"])
    P.op("dve", lambda e: e.memset(Hst, 0.0), writes=["H"])
    dbg("sp", sp_, [128, 128], ["sp"])
    dbg("TC", TC, [128, 16 * LD], KTB)
    dbg("TS", TSn, [128, 16 * LD], KTB)
    P.barrier()
    pos[0] = phase0_mark
    if stop == 0:
        return finish()

    NB = 256
    wk = dict(ss=carve(2), xn=carve(2 * D, BF16))
    wk["junk"] = wk["xn"]
    hT = carve(8 * NB, BF16); hTv = hT.rearrange("p (k t) -> p k t", k=8)
    VW = NB + 32
    vext = carve(4 * VW, BF16); vextv = vext.rearrange("p (c t) -> p c t", c=4)
    ubf2 = [carve(4 * NB, BF16), carve(4 * NB, BF16)]
    uf2 = [carve(4 * NB), carve(4 * NB)]
    ubfv2 = [x.rearrange("p (c t) -> p c t", c=4) for x in ubf2]
    ubm2 = [carve(16 * NB, BF16), carve(16 * NB, BF16)]
    ubmv2 = [x.rearrange("p (r c t) -> p r c t", r=4, c=4) for x in ubm2]
    ufv2 = [x.rearrange("p (c t) -> p c t", c=4) for x in uf2]
    cpre = carve(4 * NB); cprev = cpre.rearrange("p (c t) -> p c t", c=4)
    yf = carve(4 * NB); yfv = yf.rearrange("p (c t) -> p c t", c=4)
    sqv = yfv
    lnreg = carve(2 * NB)
    lnm = lnreg[:, 0:NB]; lnr = lnreg[:, NB:2 * NB]
    ygb = lnreg.bitcast(BF16); ygbv = ygb.rearrange("p (c t) -> p c t", c=4)
    sig = carve(NB)
    p1_shared_mark = pos[0]
    xin = [carve(D), gbc]
    vtail = carve(4 * 32); vtailv = vtail.rearrange("p (c t) -> p c t", c=4)
    G8 = 8 * LD
    rt = [carve(G8) for _ in range(2)]
    bpr = carve(G8); bpi = carve(G8)
    bprv = bpr.rearrange("p (t j) -> p t j", t=8); bpiv = bpi.rearrange("p (t j) -> p t j", t=8)
    rt = rt + [rt[0], rt[1]]
    rtv = [x.rearrange("p (t j) -> p t j", t=8) for x in rt]
    gre = carve(G8); gim = carve(G8)
    grev = gre.rearrange("p (t j) -> p t j", t=8); gimv = gim.rearrange("p (t j) -> p t j", t=8)
    Hb_ = carve_x(2 * 16 * (LD + 1), BF16)
    Hb = [Hb_, Hb_]
    Hbv = [x.rearrange("p (r t j) -> p r t j", r=2, t=16) for x in Hb]
    gl = carve(2 * 16); glv = gl.rearrange("p (r t) -> p r t", r=2)
    hl = carve(4 * 16); hlv = hl.rearrange("p (a t) -> p a t", a=4)
    hout = rt[0][:, 0:256]; cpo = rt[1]

    P.op("dve", lambda e: e.memset(vext, 0.0), writes=["vext"])
    P.op("pool", lambda e: e.memset(Hb[0].bitcast(F32), 0.0), writes=["Hb0"])

    DIAGK = ["diagA%d" % i for i in range(4)] + ["diagB%d" % i for i in range(4)]
    YFC = ["yf%d" % i for i in range(4)]; CPC = ["cpre%d" % i for i in range(4)]; YGC = ["ygb%d" % i for i in range(4)]

    def front_steps(n, c0, sample, last, par, xtiles):
        ubfv, ufv = ubfv2[par], ufv2[par]
        hTl = hTv
        ukb, ukf = "ubf%d" % par, "uf%d" % par
        st_ = {}
        ss, xn = wk["ss"], wk["xn"]
        xnv = xn.rearrange("p (t n) -> p t n", t=2)

        def n_pre():
            for i, (xt, nt, xkeys, load_fn) in enumerate(xtiles):
                if load_fn is not None:
                    load_fn()
                P.op("act", lambda e, i=i, xt=xt, nt=nt: e.activation(xnv[:nt, i, :], xt, AF.Square, accum_out=ss[:nt, i:i + 1]),
                     reads=xkeys + ["xn", "ss"], writes=["xn", "ss"])

        def n_rstd():
            nt = xtiles[0][1]
            k = len(xtiles)
            P.op("dve", lambda e: e.tensor_scalar(ss[:nt, 0:k], ss[:nt, 0:k], 1.0 / D, RMS_EPS, ALU.mult, ALU.add), reads=["ss"], writes=["ss"])
            P.op("pool", lambda e: e.tensor_tensor(ss[:nt, 0:k], ss[:nt, 0:k], negh[:nt].to_broadcast([nt, k]), ALU.pow),
                 reads=["ss", "cst"], writes=["ss"])

        def n_scale_T():
            st_["tb"] = []
            old_pool = bpool[0]
            bpool[0] = [6, 7]
            for i, (xt, nt, xkeys, load_fn) in enumerate(xtiles):
                P.op("dve", lambda e, i=i, xt=xt, nt=nt: e.tensor_scalar(xnv[:nt, i, :], xt, ss[:nt, i:i + 1], None, ALU.mult),
                     reads=xkeys + ["ss", "xn"], writes=["xn"])
                b = bank()
                pt = psb[b].bitcast(BF16)
                for kt in range(8):
                    P.op("pe", lambda e, kt=kt, i=i, nt=nt, pt=pt: e.transpose(pt[:, kt * 128:kt * 128 + nt], xnv[:nt, i, kt * 128:(kt + 1) * 128], idb[:nt, :nt]),
                         reads=["xn", "idb"], writes=[pk[b]])
                st_["tb"].append(b)
            bpool[0] = old_pool

        def n_evac():
            for i, (xt, nt, xkeys, load_fn) in enumerate(xtiles):
                b = st_["tb"][i]
                ptv = psb[b].bitcast(BF16).rearrange("p (k t) -> p k t", k=8)
                P.op("dve", lambda e, i=i, nt=nt, ptv=ptv: e.tensor_tensor(hTl[:, :, i * 128:i * 128 + nt], ptv[:, :, 0:nt],
                                                                          gcolv[:, 0, :].unsqueeze(2).to_broadcast([128, 8, nt]), ALU.mult),
                     reads=[pk[b], "gcol", "hT"], writes=["hT"])

        def glu_pe(cs_):
            def f():
                st_["glu"] = []
                for c in cs_:
                    ba = proj_fm(Winv, 8, c * 128, hTl, 0, n, ["hT"], "Win")
                    bg = proj_fm(Winv, 8, 512 + c * 128, hTl, 0, n, ["hT"], "Win")
                    st_["glu"].append((c, ba, bg))
            return f

        def glu_post():
            for (c, ba, bg) in st_["glu"]:
                P.op("act", lambda e, bg=bg: e.activation(sig[:, 0:n], psb[bg][:, 0:n], AF.Sigmoid), reads=[pk[bg], "sig"], writes=["sig"])
                if not sample:
                    P.op("dve", lambda e, c=c, ba=ba: e.tensor_tensor(vextv[:, c, 30:30 + n], psb[ba][:, 0:n], sig[:, 0:n], ALU.mult),
                         reads=[pk[ba], "sig", "vext"], writes=["vext"])
                    if last:
                        P.op("dve", lambda e, c=c, ba=ba: e.tensor_tensor(vtailv[:, c, :], psb[ba][:, n - 32:n], sig[:, n - 32:n], ALU.mult),
                             reads=[pk[ba], "sig", "vtail"], writes=["vtail"])
                else:
                    P.op("dve", lambda e, c=c, ba=ba: e.tensor_tensor(vsfv[:, c, :], psb[ba][:, 0:n], sig[:, 0:n], ALU.mult),
                         reads=[pk[ba], "sig", "vsf"], writes=["vsf"])
                    P.op("dve", lambda e, c=c: e.tensor_copy(vextsv[:, c, :, 30:34], vsfv[:, c, :].rearrange("p (s t) -> p s t", t=4)),
                         reads=["vsf", "vexts"], writes=["vexts"])

        def u_pe():
            st_["u"] = [proj_fm(Winv, 8, 1024 + c * 128, hTl, 0, n, ["hT"], "Win") for c in range(4)]

        def u_post():
            for c, bu in enumerate(st_["u"]):
                P.op("act", lambda e, c=c, bu=bu: e.activation(ufv[:, c, 0:n], psb[bu][:, 0:n], AF.Copy), reads=[pk[bu], ukf], writes=[ukf])
                P.op("dve", lambda e, c=c, bu=bu: e.tensor_copy(ubfv[:, c, 0:n], psb[bu][:, 0:n]), reads=[pk[bu], ukb], writes=[ukb])
                for r in range(4):
                    P.op("act", lambda e, c=c, r=r: e.activation(ubmv2[par][:, r, c, 0:n], ubfv[:, c, 0:n], AF.Copy, scale=mk[:, r:r + 1]),
                         reads=[ukb, "mk", "ubm%d" % par], writes=["ubm%d" % par])

        def conv_pe():
            st_["cv"] = []
            for c in range(4):
                b = bank()
                for k in range(31):
                    rhs = vextsv[:, c, :, k:k + 4] if sample else vextv[:, c, k:k + n]
                    P.op("pe", lambda e, c=c, k=k, b=b, rhs=rhs: e.matmul(psb[b][:, 0:n], diagv[:, c, k, :], rhs,
                                                                          start=(k == 0), stop=(k == 30)),
                         reads=DIAGK + ["vexts" if sample else "vext"], writes=[pk[b]])
                st_["cv"].append(b)

        def conv_post():
            for c, b in enumerate(st_["cv"]):
                P.op("act", lambda e, c=c, b=b: e.activation(cprev[:, c, 0:n], psb[b][:, 0:n], AF.Identity, bias=pcolv[:, 0, c:c + 1]),
                     reads=[pk[b], "pcol", "cpre"] + CPC, writes=["cpre"])
                P.op("act", lambda e, c=c: e.activation(sqv[:, c, 0:n], cprev[:, c, 0:n], AF.Square), reads=["cpre", "yf"] + YFC, writes=["yf"])
            if not sample:
                P.op("dve", lambda e: e.tensor_copy(vextv[:, :, 0:30], vextv[:, :, n:n + 30]), reads=["vext"], writes=["vext"])

        def ln_pe():
            bm, bq = bank(), bank()
            st_["ln"] = (bm, bq)
            for c in range(4):
                P.op("pe", lambda e, c=c: e.matmul(psb[bm][:, 0:n], onesf, cprev[:, c, 0:n], start=(c == 0), stop=(c == 3)),
                     reads=["onesf", "cpre"], writes=[pk[bm]])
            for c in range(4):
                P.op("pe", lambda e, c=c: e.matmul(psb[bq][:, 0:n], onesf, sqv[:, c, 0:n], start=(c == 0), stop=(c == 3)),
                     reads=["onesf", "yf"], writes=[pk[bq]])

        def ln_post():
            bm, bq = st_["ln"]
            P.op("act", lambda e: e.activation(lnm[:, 0:n], psb[bm][:, 0:n], AF.Copy), reads=[pk[bm], "ygb"] + YGC, writes=["ygb"])
            P.op("dve", lambda e: e.tensor_tensor(lnr[:, 0:n], lnm[:, 0:n], lnm[:, 0:n], ALU.mult), reads=["ygb"], writes=["ygb"])
            P.op("dve", lambda e: e.tensor_tensor(lnr[:, 0:n], psb[bq][:, 0:n], lnr[:, 0:n], ALU.subtract), reads=[pk[bq], "ygb"], writes=["ygb"])
            P.op("dve", lambda e: e.tensor_scalar(lnr[:, 0:n], lnr[:, 0:n], LN_EPS, None, ALU.add), reads=["ygb"], writes=["ygb"])
            P.op("act", lambda e: e.activation(lnr[:, 0:n], lnr[:, 0:n], AF.Sqrt), reads=["ygb"], writes=["ygb"])
            P.op("dve", lambda e: e.reciprocal(lnr[:, 0:n], lnr[:, 0:n]), reads=["ygb"], writes=["ygb"])
            xcs = [sqv[:, c, 0:n] for c in range(4)]
            for c in range(4):
                P.op("dve", lambda e, c=c: e.tensor_tensor(xcs[c], cprev[:, c, 0:n], lnm[:, 0:n], ALU.subtract),
                     reads=["cpre", "ygb", "yf", YFC[c]], writes=["yf", YFC[c]])
                P.op("dve", lambda e, c=c: e.tensor_tensor(xcs[c], xcs[c], lnr[:, 0:n], ALU.mult), reads=[YFC[c], "ygb"], writes=[YFC[c]])
            for c in range(4):
                P.op("act", lambda e, c=c: e.activation(sig[:, 0:n], xcs[c], AF.Sigmoid, scale=pcolv[:, 1, c:c + 1], bias=pcolv[:, 2, c:c + 1]),
                     reads=[YFC[c], "pcol", "sig"], writes=["sig"])
                P.op("dve", lambda e, c=c: e.tensor_scalar(xcs[c], xcs[c], pcolv[:, 1, c:c + 1], pcolv[:, 2, c:c + 1], ALU.mult, ALU.add),
                     reads=[YFC[c], "pcol"], writes=[YFC[c]])
                P.op("dve", lambda e, c=c: e.tensor_tensor(mixv[:, c, c0:c0 + n], xcs[c], sig[:, 0:n], ALU.mult),
                     reads=[YFC[c], "sig", "mix"], writes=["mix"])

        nop_ = lambda: None
        return [(n_pre, nop_), (n_rstd, nop_), (n_scale_T, n_evac), (glu_pe([0, 1]), glu_post), (glu_pe([2, 3]), glu_post),
                (u_pe, u_post), (conv_pe, conv_post), (ln_pe, ln_post)]

    def run_front(steps):
        for pe_, post_ in steps:
            pe_()
            post_()

    def s5_back(n, c0, yloc_fn, par, deint=False):
        ufv = ufv2[par]
        ukf = "uf%d" % par
        ss_ = [cprev[:, kc, 0:n] for kc in range(4)]
        for kc in range(4):
            b, off = yloc_fn(kc)
            if deint:
                o_ = yfv[:, kc, 0:n].rearrange("p (i s) -> p i s", s=4)
                u_ = ufv[:, kc, 0:n].rearrange("p (i s) -> p i s", s=4)
                y_ = psb[b][:, off:off + n].rearrange("p (s i) -> p i s", s=4)
            else:
                o_, u_, y_ = yfv[:, kc, 0:n], ufv[:, kc, 0:n], psb[b][:, off:off + n]
            P.op("dve", lambda e, kc=kc, o_=o_, u_=u_, y_=y_: e.scalar_tensor_tensor(o_, u_, pcolv[:, 3, kc:kc + 1], y_, ALU.mult, ALU.add),
                 reads=[ukf, "pcol", pk[b], "yf", YFC[kc]], writes=["yf", YFC[kc]])
        for kc in range(4):
            P.op("act", lambda e, kc=kc: e.activation(ss_[kc], yfv[:, kc, 0:n], AF.Square), reads=[YFC[kc], "cpre", CPC[kc]], writes=["cpre", CPC[kc]])
        for kc in range(4):
            P.op("dve", lambda e, kc=kc: e.tensor_scalar(ss_[kc], ss_[kc], 0.044715, 1.0, ALU.mult, ALU.add), reads=[CPC[kc]], writes=[CPC[kc]])
            P.op("dve", lambda e, kc=kc: e.tensor_tensor(ss_[kc], ss_[kc], yfv[:, kc, 0:n], ALU.mult), reads=[CPC[kc], YFC[kc]], writes=[CPC[kc]])
        for kc in range(4):
            P.op("act", lambda e, kc=kc: e.activation(ss_[kc], ss_[kc], AF.Sigmoid, scale=1.5957691216057308), reads=[CPC[kc]], writes=[CPC[kc]])
        for kc in range(4):
            P.op("dve", lambda e, kc=kc: e.tensor_tensor(yfv[:, kc, 0:n], yfv[:, kc, 0:n], ss_[kc], ALU.mult), reads=[CPC[kc], YFC[kc]], writes=[YFC[kc]])
        for kc in range(4):
            P.op("act", lambda e, kc=kc: e.activation(ygbv[:, kc, 0:n], yfv[:, kc, 0:n], AF.Copy), reads=[YFC[kc], "ygb", YGC[kc]], writes=["ygb", YGC[kc]])
        for c in range(4):
            b = proj_fm(Wgluv, 4, c * 128, ygbv, 0, n, ["ygb"] + YGC, "Wglu")
            P.op("act", lambda e, b=b: e.activation(sig[:, 0:n], psb[b][:, 0:n], AF.Sigmoid), reads=[pk[b], "sig"], writes=["sig"])
            P.op("dve", lambda e, c=c: e.tensor_tensor(mixv[:, 4 + c, c0:c0 + n], yfv[:, c, 0:n], sig[:, 0:n], ALU.mult),
                 reads=[YFC[c], "sig", "mix"], writes=["mix"])

    NBLK = T // NB
    TPB = NB // 128
    POOL_BU, POOL_FR, POOL_BACK = [0, 1], [2, 3, 6, 7], [2, 3]

    def prompt_front(blk):
        xt = []
        for tl in range(TPB):
            t_ = blk * TPB + tl
            z = t_ % 2
            xt.append((xin[z], 128, [("xin0" if z == 0 else "gbc")],
                       (lambda t_=t_, z=z: P.dma("sp", xin[z], xp[t_ * 128:(t_ + 1) * 128, :], writes=[("xin0" if z == 0 else "gbc")]))))
        return front_steps(NB, blk * NB, False, blk == NBLK - 1, blk % 2, xt)

    bpool[0] = list(range(8))
    carry_bu = None
    fronts = {}
    run_front(prompt_front(0)[:3] if stop == 20 else prompt_front(0))
    if stop == 21:
        return finish()
    if stop == 20:
        dbg("xn", wk["xn"].bitcast(F32), [128, D], ["xn"])
        dbg("ss", wk["ss"], [128, 2], ["ss"])
        dbg("hT", hT.bitcast(F32), [128, 4 * NB], ["hT"])
        return finish()
    for blk in range(NBLK):
        par = blk % 2
        ubfv = ubfv2[par]
        c0 = blk * NB
        def get_front(k):
            if k >= NBLK:
                return None
            if k not in fronts:
                fronts[k] = prompt_front(k)
            return fronts[k]
        F1, F2 = get_front(blk + 1), get_front(blk + 2)
        slots = [[] for _ in range(8)]
        end_list = []
        if F1 is not None:
            if blk == 0:
                for k in range(8):
                    if k >= 1:
                        slots[k].append(F1[k - 1][1])
                    slots[k].append(F1[k][0])
                end_list.append(F1[7][1])
            else:
                for k in range(5):
                    slots[k].append(F1[k + 2][1])
                    slots[k].append(F1[k + 3][0])
                slots[5].append(F1[7][1])
        if F2 is not None:
            slots[6].append(F2[0][0])
            slots[7].append(F2[1][0])
            end_list.append(F2[2][0])
        yloc = lambda kc: (4 + kc // 2, (kc % 2) * NB)
        zb = blk % 2
        Hcur = Hbv[zb]
        hbk = "Hb0"
        ubi = [ubfv[:, kc, :].rearrange("p (i s) -> p i s", s=4) for kc in range(4)]
        ubmv = ubmv2[par]

        def emit_bu(half, par=par):
            ubmv = ubmv2[par]
            bpool[0] = POOL_BU
            bA, bB = bank(), bank()
            for j in range(8):
                ti = half * 8 + j
                kc, r = ti // 4, ti % 4
                for ri, bb in enumerate((bA, bB)):
                    for s in range(4):
                        P.op("pe", lambda e, ri=ri, kc=kc, r=r, bb=bb, j=j, s=s, ubmv=ubmv: e.matmul(
                            psb[bb][:, j * LD:(j + 1) * LD], Bcv[:, s, ri, kc, :],
                            ubmv[:, r, kc, :].rearrange("p (i s) -> p i s", s=4)[:, :, s], start=(s == 0), stop=(s == 3)),
                            reads=["Bc", "ubm%d" % par], writes=[pk[bb]])
            return bA, bB

        def front_slot(k):
            bpool[0] = POOL_FR
            for f_ in slots[k]:
                f_()

        slot = 0
        nxt = carry_bu if carry_bu is not None else emit_bu(0)
        if stop == 40:
            P.barrier(); return finish()
        for half in range(2):
            bA, bB = nxt
            tsl = slice(half * 8, (half + 1) * 8)
            pre = psb[bA].rearrange("p (t j) -> p t j", t=8)
            pim = psb[bB].rearrange("p (t j) -> p t j", t=8)
            cosT, sinT = TCv[:, tsl, :], TSv[:, tsl, :]
            P.op("dve", lambda e, pre=pre, cosT=cosT: e.tensor_tensor(rtv[0], pre, cosT, ALU.mult), reads=[pk[bA], "TC", "rt0"], writes=["rt0"])
            P.op("dve", lambda e, pim=pim, sinT=sinT: e.tensor_tensor(rtv[1], pim, sinT, ALU.mult), reads=[pk[bB], "TS", "rt1"], writes=["rt1"])
            P.op("dve", lambda e: e.tensor_tensor(bprv, rtv[0], rtv[1], ALU.add), reads=["rt0", "rt1"], writes=["bpr"])
            P.op("dve", lambda e, pim=pim, cosT=cosT: e.tensor_tensor(rtv[2], pim, cosT, ALU.mult), reads=[pk[bB], "TC", "rt0"], writes=["rt0"])
            P.op("dve", lambda e, pre=pre, sinT=sinT: e.tensor_tensor(rtv[3], pre, sinT, ALU.mult), reads=[pk[bA], "TS", "rt1"], writes=["rt1"])
            P.op("dve", lambda e: e.tensor_tensor(bpiv, rtv[2], rtv[3], ALU.subtract), reads=["rt0", "rt1"], writes=["bpi"])
            if stop == 41:
                P.barrier(); return finish()
            if half == 0:
                nxt = emit_bu(1)
            front_slot(slot); slot += 1
            front_slot(slot); slot += 1
            if stop == 42:
                P.barrier(); return finish()
            for j in range(8):
                ti = half * 8 + j
                rb = pwv[:, 4, ti:ti + 1].to_broadcast([128, LD])
                P.op("dve", lambda e, ti=ti, j=j, rb=rb: e.tensor_tensor_scan(grev[:, j, :], rb, bprv[:, j, :], Hv[:, 0, ti:ti + 1], ALU.mult, ALU.add),
                     reads=["pw", "bpr", "H", "gre"], writes=["gre"])
                P.op("dve", lambda e, ti=ti, j=j, rb=rb: e.tensor_tensor_scan(gimv[:, j, :], rb, bpiv[:, j, :], Hv[:, 1, ti:ti + 1], ALU.mult, ALU.add),
                     reads=["pw", "bpi", "H", "gim"], writes=["gim"])
            P.op("act", lambda e, tsl=tsl: e.activation(glv[:, 0, tsl], grev[:, :, LD - 1], AF.Copy), reads=["gre", "gl"], writes=["gl"])
            P.op("act", lambda e, tsl=tsl: e.activation(glv[:, 1, tsl], gimv[:, :, LD - 1], AF.Copy), reads=["gim", "gl"], writes=["gl"])
            P.op("dve", lambda e, cosT=cosT: e.tensor_tensor(rtv[0], grev, cosT, ALU.mult), reads=["gre", "TC"], writes=["rt0"])
            P.op("dve", lambda e, sinT=sinT: e.tensor_tensor(rtv[1], gimv, sinT, ALU.mult), reads=["gim", "TS"], writes=["rt1"])
            P.op("dve", lambda e, tsl=tsl, Hcur=Hcur: e.tensor_tensor(Hcur[:, 0, tsl, 1:LD + 1], rtv[0], rtv[1], ALU.subtract), reads=["rt0", "rt1", hbk], writes=[hbk])
            P.op("dve", lambda e, sinT=sinT: e.tensor_tensor(rtv[2], grev, sinT, ALU.mult), reads=["gre", "TS"], writes=["rt0"])
            P.op("dve", lambda e, cosT=cosT: e.tensor_tensor(rtv[3], gimv, cosT, ALU.mult), reads=["gim", "TC"], writes=["rt1"])
            P.op("dve", lambda e, tsl=tsl, Hcur=Hcur: e.tensor_tensor(Hcur[:, 1, tsl, 1:LD + 1], rtv[2], rtv[3], ALU.add), reads=["rt0", "rt1", hbk], writes=[hbk])
            front_slot(slot); slot += 1
            front_slot(slot); slot += 1
        if stop == 43:
            P.barrier(); return finish()
        cl, sl = TCv[:, :, LD - 1], TSv[:, :, LD - 1]
        P.op("dve", lambda e: e.tensor_tensor(hlv[:, 0, :], glv[:, 0, :], cl, ALU.mult), reads=["gl", "TC", "hl"], writes=["hl"])
        P.op("dve", lambda e: e.tensor_tensor(hlv[:, 1, :], glv[:, 1, :], sl, ALU.mult), reads=["gl", "TS", "hl"], writes=["hl"])
        P.op("dve", lambda e: e.tensor_tensor(hlv[:, 2, :], glv[:, 0, :], sl, ALU.mult), reads=["gl", "TS", "hl"], writes=["hl"])
        P.op("dve", lambda e: e.tensor_tensor(hlv[:, 3, :], glv[:, 1, :], cl, ALU.mult), reads=["gl", "TC", "hl"], writes=["hl"])
        P.op("dve", lambda e: e.tensor_tensor(Hv[:, 0, :], hlv[:, 0, :], hlv[:, 1, :], ALU.subtract), reads=["hl", "H"], writes=["H"])
        P.op("dve", lambda e: e.tensor_tensor(Hv[:, 1, :], hlv[:, 2, :], hlv[:, 3, :], ALU.add), reads=["hl", "H"], writes=["H"])
        if stop == 44:
            P.barrier(); return finish()
        for kc in range(4):
            b, off = yloc(kc)
            for sp in range(4):
                reg = slice(off + sp * LD, off + (sp + 1) * LD)
                if sp < 3:
                    for s in range(sp + 1):
                        P.op("pe", lambda e, b=b, reg=reg, sp=sp, s=s, kc=kc, ubi=ubi: e.matmul(psb[b][:, reg], KTv[:, sp - s, kc, :], ubi[kc][:, :, s],
                                                                                              start=(s == 0), stop=False),
                             reads=["KTw", "ubf%d" % par], writes=[pk[b]])
                for r in range(4):
                    ti = kc * 4 + r
                    for ri in range(2):
                        if sp < 3:
                            lhs_, rhs_, st_f = Ccv[:, sp + 1, ri, ti, :], Hcur[:, ri, ti, 0:LD], False
                        else:
                            lhs_, rhs_, st_f = Ccv[:, 0, ri, ti, :], Hcur[:, ri, ti, 1:LD + 1], (ri == 0)
                        P.op("pe", lambda e, b=b, reg=reg, r=r, lhs_=lhs_, rhs_=rhs_, st_f=st_f, ri=ri: e.matmul(
                            psb[b][32 * r:32 * r + 32, reg], lhs_, rhs_, start=st_f, stop=(ri == 1), tile_position=(0, 32 * r)),
                            reads=["Cc", hbk], writes=[pk[b]])
        P.op("act", lambda e, zb=zb: e.activation(Hbv[1 - zb][:, :, :, 0], Hv, AF.Copy), reads=["H", "Hb0"], writes=["Hb0"])
        while slot < 8:
            front_slot(slot); slot += 1
        carry_bu = emit_bu(0, par=(blk + 1) % 2) if blk + 1 < NBLK else None
        bpool[0] = POOL_FR
        for f_ in end_list:
            f_()
        if stop == 22:
            return finish()
        bpool[0] = POOL_BACK
        s5_back(NB, c0, yloc, par, deint=True)
        if stop == 23:
            return finish()
    bpool[0] = list(range(8))

    b = bank()
    for ri in range(2):
        P.op("pe", lambda e, ri=ri, b=b: e.transpose(psb[b][:16, ri * 128:(ri + 1) * 128], Hv[:, ri, :], idf), reads=["H", "idf"], writes=[pk[b]])
    P.op("act", lambda e, b=b: e.activation(hout[:16, :], psb[b][:16, 0:256], AF.Copy), reads=[pk[b], "rt0"], writes=["rt0"])
    P.dma("sp", hpr, hout[:16, 0:128], key="o_hp", reads=["rt0"])
    P.dma("sp", hpi, hout[:16, 128:256], key="o_hp", reads=["rt0"])
    b = bank()
    for c in range(4):
        P.op("pe", lambda e, c=c, b=b: e.transpose(psb[b][:32, c * 128:(c + 1) * 128], vtailv[:, c, :], idf), reads=["vtail", "idf"], writes=[pk[b]])
    P.op("act", lambda e, b=b: e.activation(cpo[:32, :], psb[b][:32, :], AF.Copy), reads=[pk[b], "rt1"], writes=["rt1"])
    P.dma("sp", cp, cpo[2:32, :], key="o_cp", reads=["rt1"])
    dbg("mixp", mix, [128, 8 * (T + TS)], ["mix"])
    P.barrier()
    pos[0] = p1_shared_mark
    if stop == 1:
        return finish()

    vsf = carve(4 * TS); vsfv = vsf.rearrange("p (c t) -> p c t", c=4)
    vexts = carve(4 * NS * 34, BF16); vextsv = vexts.rearrange("p (c s t) -> p c s t", c=4, s=NS)
    scq = [carve(512), carve(512)]
    cso = carve(512)
    h0q = scq
    hc = [carve(16 * 32), carve(16 * 32)]
    hcv = [x.rearrange("p (t r s) -> p t r s", t=16, r=2) for x in hc]
    bus = ubm2[1].bitcast(F32)[:, 0:2 * 16 * TS]; busv = bus.rearrange("p (r t s k) -> p r t s k", r=2, t=16, s=NS)
    st = [ubf2[1].bitcast(F32)[:, 0:16 * NS], ubf2[1].bitcast(F32)[:, 16 * NS:32 * NS]]
    stv = [x.rearrange("p (t s) -> p t s", t=16) for x in st]
    hsb = uf2[1].bitcast(BF16)[:, 0:2 * 16 * TS]; hsbv = hsb.rearrange("p (r t s k) -> p r t s k", r=2, t=16, s=NS)
    hsoq = scq
    for q in range(4):
        z = q % 2
        P.dma("sp", scq[z][:120, :], sconv[q * 4:(q + 1) * 4, :, :].rearrange("s j c -> (s j) c"), writes=["scq%d" % z])
        b = bank()
        for c in range(4):
            P.op("pe", lambda e, z=z, c=c, b=b: e.transpose(psb[b][:, c * 120:(c + 1) * 120], scq[z][:120, c * 128:(c + 1) * 128], idf[:120, :120]),
                 reads=["scq%d" % z, "idf"], writes=[pk[b]])
        for c in range(4):
            P.op("act", lambda e, q=q, c=c, b=b: e.activation(vextsv[:, c, q * 4:(q + 1) * 4, 0:30],
                                                             psb[b][:, c * 120:(c + 1) * 120].rearrange("p (s j) -> p s j", s=4), AF.Copy),
                 reads=[pk[b], "vexts"], writes=["vexts"])
    P.dma("sp", cs[:, 0:26, :], sconv[:, 4:30, :], key="o_cs0")
    P.dma("sp", xresv[:TS, 16, :], xs, writes=["xr16"])
    run_front(front_steps(TS, T, True, False, 0, [(xresv[:TS, 16, :], TS, ["xr16"], None)]))
    b = bank()
    for c in range(4):
        P.op("pe", lambda e, c=c, b=b: e.transpose(psb[b][:TS, c * 128:(c + 1) * 128], vsfv[:, c, :], idf), reads=["vsf", "idf"], writes=[pk[b]])
    P.op("act", lambda e, b=b: e.activation(cso[:TS, :], psb[b][:TS, :], AF.Copy), reads=[pk[b]], writes=["cso"])
    for s in range(NS):
        P.dma("sp", cs[s, 26:30, :], cso[s * 4:(s + 1) * 4, :], key="o_cs1", reads=["cso"])
    for q in range(4):
        z = q % 2
        P.dma("sp", h0q[z][0:NS, :], sre[:, q * 512:(q + 1) * 512], writes=["scq%d" % z])
        P.dma("sp", h0q[z][NS:2 * NS, :], sim[:, q * 512:(q + 1) * 512], reads=["scq%d" % z], writes=["scq%d" % z])
        b = bank()
        for j in range(4):
            P.op("pe", lambda e, z=z, j=j, b=b: e.transpose(psb[b][:, j * 32:(j + 1) * 32], h0q[z][:32, j * 128:(j + 1) * 128], idf[:32, :32]),
                 reads=["scq%d" % z, "idf"], writes=[pk[b]])
        P.op("act", lambda e, q=q, b=b: e.activation(hcv[0][:, q * 4:(q + 1) * 4, :, :].rearrange("p t r s -> p t (r s)"),
                                                     psb[b][:, 0:128].rearrange("p (t x) -> p t x", t=4), AF.Copy),
             reads=[pk[b], "hc0"], writes=["hc0"])
    for q in range(4):
        bb2 = [bank(), bank()]
        for j in range(4):
            ti = q * 4 + j
            b = bb2[j % 2]
            for ri in range(2):
                c_ = ((j // 2) * 2 + ri) * TS
                P.op("pe", lambda e, ri=ri, ti=ti, b=b, c_=c_: e.matmul(psb[b][:, c_:c_ + TS],
                                                                      Bcv[:, 3, ri, ti // 4, :],
                                                                      ubmv2[0][:, ti % 4, ti // 4, 0:TS], start=True, stop=True),
                     reads=["Bc", "ubm0"], writes=[pk[b]])
        for par_ in range(2):
            b = bb2[par_]
            for ri in range(2):
                P.op("act", lambda e, ri=ri, q=q, b=b, par_=par_: e.activation(
                    busv[:, ri, q * 4 + par_:q * 4 + 4:2, :, :].rearrange("p t s k -> p t (s k)"),
                    psb[b][:, 0:4 * TS].rearrange("p (j r x) -> p j r x", j=2, r=2)[:, :, ri, :], AF.Copy),
                    reads=[pk[b], "bus"], writes=["bus"])
    arb = spv[:, 1, :].unsqueeze(2).to_broadcast([128, 16, NS])
    aib = spv[:, 2, :].unsqueeze(2).to_broadcast([128, 16, NS])
    for k in range(4):
        zc, zn = k % 2, (k + 1) % 2
        hr, hi = hcv[zc][:, :, 0, :], hcv[zc][:, :, 1, :]
        KH = ["hc0", "hc1", "st", "sp", "bus"]
        P.op("dve", lambda e, hr=hr: e.tensor_tensor(stv[0], hr, arb, ALU.mult), reads=KH, writes=["st"])
        P.op("dve", lambda e, hi=hi: e.tensor_tensor(stv[1], hi, aib, ALU.mult), reads=KH, writes=["st"])
        P.op("dve", lambda e: e.tensor_tensor(stv[0], stv[0], stv[1], ALU.subtract), reads=KH, writes=["st"])
        P.op("dve", lambda e, k=k, zn=zn: e.tensor_tensor(hcv[zn][:, :, 0, :], stv[0], busv[:, 0, :, :, k], ALU.add), reads=KH, writes=["hc%d" % zn])
        P.op("dve", lambda e, hr=hr: e.tensor_tensor(stv[0], hr, aib, ALU.mult), reads=KH, writes=["st"])
        P.op("dve", lambda e, hi=hi: e.tensor_tensor(stv[1], hi, arb, ALU.mult), reads=KH, writes=["st"])
        P.op("dve", lambda e: e.tensor_tensor(stv[0], stv[0], stv[1], ALU.add), reads=KH, writes=["st"])
        P.op("dve", lambda e, k=k, zn=zn: e.tensor_tensor(hcv[zn][:, :, 1, :], stv[0], busv[:, 1, :, :, k], ALU.add), reads=KH, writes=["hc%d" % zn])
        for ri in range(2):
            P.op("act", lambda e, ri=ri, k=k, zn=zn: e.activation(hsbv[:, ri, :, :, k], hcv[zn][:, :, ri, :], AF.Copy),
                 reads=["hc%d" % zn, "hsb"], writes=["hsb"])
    ysb = {}
    for kc in range(4):
        b = bank(); ysb[kc] = b
        for r in range(4):
            ti = kc * 4 + r
            for ri in range(2):
                P.op("pe", lambda e, ri=ri, ti=ti, r=r, b=b: e.matmul(psb[b][32 * r:32 * r + 32, 0:TS], Ccv[:, 0, ri, ti, :],
                                                                      hsbv[:, ri, ti, :, :].rearrange("p s k -> p (s k)"),
                                                                      start=(ri == 0), stop=(ri == 1), tile_position=(0, 32 * r)),
                     reads=["Cc", "hsb"], writes=[pk[b]])
    s5_back(TS, T, lambda kc: (ysb[kc], 0), 0)
    for q in range(4):
        z = q % 2
        b = bank()
        for j in range(4):
            ti = q * 4 + j
            P.op("pe", lambda e, ti=ti, j=j, b=b: e.transpose(psb[b][:32, j * 128:(j + 1) * 128], hcv[0][:, ti, :, :].rearrange("p r s -> p (r s)"), idf),
                 reads=["hc0", "idf"], writes=[pk[b]])
        P.op("act", lambda e, z=z, b=b: e.activation(hsoq[z][:32, :], psb[b][:32, :], AF.Copy), reads=[pk[b], "scq%d" % z], writes=["scq%d" % z])
        P.dma("sp", hsr[:, q * 512:(q + 1) * 512], hsoq[z][0:NS, :], key="o_hs%d" % z, reads=["scq%d" % z])
        P.dma("sp", hsi[:, q * 512:(q + 1) * 512], hsoq[z][NS:2 * NS, :], key="o_hs%d" % z, reads=["scq%d" % z])
    dbg("mix", mix, [128, 8 * (T + TS)], ["mix"])
    P.barrier()
    if stop == 2:
        return finish()
    pos[0] = p1b_mark
    load_gbc(1)
    Wo1 = carve(8 * D, BF16); Wo1v = Wo1.rearrange("p (k n) -> p k n", k=8)
    load_w(Wo1v, w_out, 8, "Wout")
    PRE0 = 45000
    Wq, _ = carve_at(PRE0, 8 * D, BF16); Wqv = Wq.rearrange("p (k n) -> p k n", k=8)
    Wkv, _ = carve_at(PRE0 + 4096, 8 * D, BF16); Wkvv = Wkv.rearrange("p (k n) -> p k n", k=8)
    load_w(Wqv, w_q, 8, "Wq")
    load_w(Wkvv, w_k, 8, "Wkv")
    wk = dict(ss=carve(1), xn=carve(D, BF16), ss2=[carve(4), carve(4)], tmp=[carve(D), carve(D)], pjunk=[carve(D, BF16), carve(D, BF16)]); wk["junk"] = wk["xn"]
    for q in range(4):
        P.dma("sp", xresv[:, q * 4:(q + 1) * 4, :], xp[q * 512:(q + 1) * 512, :].rearrange("(t p) n -> p t n", p=128),
              writes=["xr%d" % (q * 4 + i) for i in range(4)], key="k_xrq%d" % q)
    for t_ in range(17):
        n = 128 if t_ < 16 else TS
        proj_tm_postnorm_residual(mixv, 8, t_ * 128, n, Wo1v, "Wout", ["mix"], 0, xresv[:n, t_, :], "xr%d" % t_, wk)
    assert pos[0] <= PRE0, pos[0]
    dbg("x1", xres, [128, 17 * D], ["xr%d" % i for i in range(17)])
    P.barrier()
    pos[0] = persist_mark
    if stop == 3:
        return finish()

    load_gbc(3)
    Wo = carve(8 * D, BF16); Wov = Wo.rearrange("p (k n) -> p k n", k=8)
    load_w(Wov, w_v, 8, "Wo")
    wk = dict(ss=carve(1), xn=carve(D, BF16), ss2=[carve(4), carve(4)], tmp=[carve(D), carve(D)]); wk["junk"] = wk["xn"]
    pj2_ = carve(D, BF16); wk["pjunk"] = [pj2_, pj2_]
    kT = carve(8 * 256, BF16); kTv = kT.rearrange("p (k t) -> p k t", k=8)
    vb = carve(2 * D, BF16); vbv = vb.rearrange("p (m n) -> p m n", m=2)
    p2_mark = pos[0]
    memt = [carve(D), carve(D)]
    mT = carve(8 * 256, BF16); mTv = mT.rearrange("p (k t) -> p k t", k=8)
    kvo = carve(D)
    assert pos[0] <= PRE0, pos[0]
    for mt in range(2):
        P.dma("sp", memt[mt], mem[mt * 128:(mt + 1) * 128, :], writes=["memt%d" % mt])
        wk["xkeys"] = ["memt%d" % mt]; wk["hkeys"] = ["mT"]
        norm_to_hT(memt[mt], 128, 3, mTv, mt * 128, wk)
    for c in range(8):
        b = proj_fm(Wkvv, 8, c * 128, mTv, 0, 256, ["mT"], "Wkv")
        P.op("act", lambda e, c=c, b=b: e.activation(kTv[:, c, :], psb[b][:, 0:256], AF.Copy), reads=[pk[b], "kT"], writes=["kT"])

    def mem_tm(out_dram, okey, also_bf=None, Wv_=None, wkey_="Wkv"):
        Wv_ = Wkvv if Wv_ is None else Wv_
        for mt in range(2):
            for h in range(2):
                b = bank()
                for kt in range(8):
                    P.op("pe", lambda e, kt=kt, b=b, h=h, mt=mt: e.matmul(psb[b], mTv[:, kt, mt * 128:(mt + 1) * 128],
                                                                          Wv_[:, kt, h * 512:(h + 1) * 512], start=(kt == 0), stop=(kt == 7)),
                         reads=["mT", wkey_], writes=[pk[b]])
                P.op("act", lambda e, b=b, h=h: e.activation(kvo[:, h * 512:(h + 1) * 512], psb[b], AF.Copy), reads=[pk[b], "kvo"], writes=["kvo"])
                if also_bf is not None:
                    P.op("dve", lambda e, b=b, h=h, mt=mt: e.tensor_copy(also_bf[:, mt, h * 512:(h + 1) * 512], psb[b]), reads=[pk[b], "vb"], writes=["vb"])
            P.dma("sp", out_dram[mt * 128:(mt + 1) * 128, :], kvo, key=okey, reads=["kvo"])

    mem_tm(kp, "o_kv")
    mem_tm(vp, "o_kv", also_bf=vbv, Wv_=Wov, wkey_="Wo")
    load_w(Wov, w_o, 8, "Wo")
    P.barrier()
    pos[0] = p2_mark
    if stop == 4:
        return finish()
    hT = carve(8 * 512, BF16); hTv = hT.rearrange("p (k t) -> p k t", k=8)
    qT = carve(8 * 512, BF16); qTv = qT.rearrange("p (k t) -> p k t", k=8)
    oT = carve(8 * 512, BF16); oTv = oT.rearrange("p (k t) -> p k t", k=8)
    pT = carve(2 * 512, BF16); pTv = pT.rearrange("p (m t) -> p m t", m=2)
    rec = carve(512)

    def attn_tail(tiles):
        for i, (t_, n) in enumerate(tiles):
            proj_tm_postnorm_residual(oTv, 8, i * 128, n, Wov, "Wo", ["oT"], 1, xresv[:n, t_, :], "xr%d" % t_, wk)

    def q_proj(n):
        for c in range(8):
            b = proj_fm(Wqv, 8, c * 128, hTv, 0, n, ["hT"], "Wq")
            P.op("act", lambda e, c=c, b=b: e.activation(qTv[:, c, 0:n], psb[b][:, 0:n], AF.Copy, scale=1.0 / 16.0),
                 reads=[pk[b], "qT"], writes=["qT"])

    pT2 = [pT, carve(2 * 512, BF16)]
    pTv2 = [x.rearrange("p (m t) -> p m t", m=2) for x in pT2]
    rec2 = [rec, rec]
    hT2 = [hT, carve(8 * 512, BF16)]
    hTv2 = [x.rearrange("p (k t) -> p k t", k=8) for x in hT2]

    def norm_tile(blk, tl):
        t_ = blk * 4 + tl
        wk["xkeys"] = ["xr%d" % t_]; wk["hkeys"] = ["hTa%d" % (blk % 2)]
        norm_to_hT(xresv[:, t_, :], 128, 1, hTv2[blk % 2], tl * 128, wk)

    def q_proj2(n, hv, hk):
        for c in range(8):
            b = proj_fm(Wqv, 8, c * 128, hv, 0, n, [hk], "Wq")
            P.op("act", lambda e, c=c, b=b: e.activation(qTv[:, c, 0:n], psb[b][:, 0:n], AF.Copy, scale=1.0 / 16.0),
                 reads=[pk[b], "qT"], writes=["qT"])

    def head_scores(h, z):
        bs = [bank(), bank()]
        for mc in range(2):
            for dc in range(2):
                P.op("pe", lambda e, h=h, mc=mc, dc=dc, b=bs[mc]: e.matmul(psb[b], kTv[:, 2 * h + dc, mc * 128:(mc + 1) * 128],
                                                                           qTv[:, 2 * h + dc, :], start=(dc == 0), stop=(dc == 1)),
                     reads=["kT", "qT"], writes=[pk[bs[mc]]])
            P.op("act", lambda e, mc=mc, b=bs[mc], z=z: e.activation(pTv2[z][:, mc, :], psb[b], AF.Exp),
                 reads=[pk[bs[mc]], "pT%d_%d" % (z, mc)], writes=["pT%d_%d" % (z, mc)])

    def head_finish(h, z):
        bc = bank()
        for mc in range(2):
            P.op("pe", lambda e, mc=mc, bc=bc, z=z: e.matmul(psb[bc], onesb, pTv2[z][:, mc, :], start=(mc == 0), stop=(mc == 1)),
                 reads=["onesb", "pT%d_%d" % (z, mc)], writes=[pk[bc]])
        P.op("act", lambda e, bc=bc, z=z: e.activation(rec2[z], psb[bc], AF.Ln), reads=[pk[bc], "rec"], writes=["rec"])
        P.op("act", lambda e, z=z: e.activation(rec2[z], rec2[z], AF.Exp, scale=-1.0), reads=["rec"], writes=["rec"])
        for dc in range(2):
            bo = bank()
            for mc in range(2):
                P.op("pe", lambda e, h=h, mc=mc, dc=dc, bo=bo, z=z: e.matmul(psb[bo], vbv[:, mc, (2 * h + dc) * 128:(2 * h + dc + 1) * 128],
                                                                             pTv2[z][:, mc, :], start=(mc == 0), stop=(mc == 1)),
                     reads=["vb", "pT%d_%d" % (z, mc)], writes=[pk[bo]])
            P.op("dve", lambda e, h=h, dc=dc, bo=bo, z=z: e.tensor_tensor(oTv[:, 2 * h + dc, :], psb[bo], rec2[z], ALU.mult),
                 reads=[pk[bo], "rec", "oT"], writes=["oT"])

    hTs = carve(8 * TS, BF16); hTsv = hTs.rearrange("p (k t) -> p k t", k=8)
    qTs = carve(8 * TS, BF16); qTsv = qTs.rearrange("p (k t) -> p k t", k=8)
    kcb = [carve(2 * D, BF16), carve(2 * D, BF16)]
    vcb = [carve(2 * D, BF16), carve(2 * D, BF16)]
    kcbv = [x.rearrange("p (m n) -> p m n", m=2) for x in kcb]
    vcbv = [x.rearrange("p (m n) -> p m n", m=2) for x in vcb]
    kTs = [carve(8 * 256, BF16), carve(8 * 256, BF16)]
    kTsv = [x.rearrange("p (k t) -> p k t", k=8) for x in kTs]
    pTs = carve(512, BF16); pTsv = pTs.rearrange("p (s m h k) -> p s m h k", s=NS, m=2, h=4)
    assert pos[0] <= PRE0, pos[0]
    bsc, bos = 6, 7
    bpool[0] = [0, 1, 2, 3, 4, 5]
    wk["xkeys"] = ["xr16"]; wk["hkeys"] = ["hTs"]
    norm_to_hT(xresv[:TS, 16, :], TS, 1, hTsv, 0, wk)
    for c in range(8):
        b_ = proj_fm(Wqv, 8, c * 128, hTsv, 0, TS, ["hTs"], "Wq")
        P.op("act", lambda e, c=c, b_=b_: e.activation(qTsv[:, c, :], psb[b_][:, 0:TS], AF.Copy, scale=1.0 / 16.0),
             reads=[pk[b_], "qTs"], writes=["qTs"])

    def load_seq(s):
        z = s % 2
        P.dma("pool", kcbv[z], ck[s].rearrange("(m p) n -> p m n", p=128), writes=["kcb%d" % z])
        P.dma("pool", vcbv[z], cv[s].rearrange("(m p) n -> p m n", p=128), writes=["vcb%d" % z])

    def sample_seq(s):
        z = s % 2
        for half in range(2):
            b = bank()
            pt = psb[b].bitcast(BF16)
            for j in range(8):
                idx = half * 8 + j
                dcn, mc = idx // 2, idx % 2
                P.op("pe", lambda e, z=z, dcn=dcn, mc=mc, j=j, pt=pt: e.transpose(pt[:, j * 128:(j + 1) * 128], kcbv[z][:, mc, dcn * 128:(dcn + 1) * 128], idb),
                     reads=["kcb%d" % z, "idb"], writes=[pk[b]])
            if half == 0:
                P.op("act", lambda e, z=z, half=half, pt=pt: e.activation(kTsv[z][:, half * 4:(half + 1) * 4, :], pt.rearrange("p (d x) -> p d x", d=4), AF.Copy),
                     reads=[pk[b], "kTs%d" % z], writes=["kTs%d" % z])
            else:
                P.op("dve", lambda e, z=z, half=half, pt=pt: e.tensor_copy(kTsv[z][:, half * 4:(half + 1) * 4, :], pt.rearrange("p (d x) -> p d x", d=4)),
                     reads=[pk[b], "kTs%d" % z], writes=["kTs%d" % z])
        for mc in range(2):
            for h in range(4):
                col = ((s * 2 + mc) * 4 + h) * 4
                for dc in range(2):
                    P.op("pe", lambda e, z=z, h=h, mc=mc, dc=dc, col=col, s=s: e.matmul(psb[bsc][:, col:col + 4], kTsv[z][:, 2 * h + dc, mc * 128:(mc + 1) * 128],
                                                                                      qTsv[:, 2 * h + dc, s * 4:(s + 1) * 4], start=(dc == 0), stop=(dc == 1)),
                         reads=["kTs%d" % z, "qTs"], writes=[pk[bsc]])
        P.op("act", lambda e, s=s: e.activation(pTs[:, s * 32:(s + 1) * 32], psb[bsc][:, s * 32:(s + 1) * 32], AF.Exp),
             reads=[pk[bsc], "pTs"], writes=["pTs"])
        for h in range(4):
            for dc in range(2):
                col = (2 * h + dc) * TS + s * 4
                for mc in range(2):
                    P.op("pe", lambda e, z=z, h=h, mc=mc, dc=dc, col=col, s=s: e.matmul(psb[bos][:, col:col + 4], vcbv[z][:, mc, (2 * h + dc) * 128:(2 * h + dc + 1) * 128],
                                                                                      pTsv[:, s, mc, h, :], start=(mc == 0), stop=(mc == 1)),
                         reads=["vcb%d" % z, "pTs"], writes=[pk[bos]])

    load_seq(0)
    load_seq(1)
    for tl in range(4):
        norm_tile(0, tl)
    for blk in range(4):
        q_proj2(512, hTv2[blk % 2], "hTa%d" % (blk % 2))
        pend = None
        for h in range(4):
            head_scores(h, h % 2)
            if pend is not None:
                head_finish(*pend)
            pend = (h, h % 2)
            if blk + 1 < 4:
                norm_tile(blk + 1, h)
            s_ = blk * 4 + h
            sample_seq(s_)
            if s_ + 2 < NS:
                load_seq(s_ + 2)
        head_finish(*pend)
        attn_tail([(blk * 4 + tl, 128) for tl in range(4)])

    bc = bank()
    for mc in range(2):
        P.op("pe", lambda e, mc=mc, bc=bc: e.matmul(psb[bc][:, 0:256], onesb, pTsv[:, :, mc, :, :], start=(mc == 0), stop=(mc == 1)),
             reads=["onesb", "pTs"], writes=[pk[bc]])
    P.op("dve", lambda e, bc=bc: e.reciprocal(rec[:, 0:256], psb[bc][:, 0:256]), reads=[pk[bc]], writes=["rec"])
    recv = rec[:, 0:256].rearrange("p (s h k) -> p s h k", s=NS, h=4)
    osv = psb[bos].rearrange("p (c s k) -> p c s k", c=8, s=NS)
    for h in range(4):
        for dc in range(2):
            P.op("dve", lambda e, h=h, dc=dc: e.tensor_tensor(oTv[:, 2 * h + dc, 0:TS].rearrange("p (s k) -> p s k", s=NS),
                                                              osv[:, 2 * h + dc, :, :], recv[:, :, h, :], ALU.mult),
                 reads=[pk[bos], "rec", "oT"], writes=["oT"])
    bpool[0] = list(range(8))
    attn_tail([(16, TS)])
    dbg("x2", xres, [128, 17 * D], ["xr%d" % i for i in range(17)])
    P.barrier()
    pos[0] = persist_mark
    if stop == 5:
        return finish()

    load_gbc(5)
    Wd = carve(NFF * D, BF16); Wdv = Wd.rearrange("p (k n) -> p k n", k=NFF)
    wk = dict(ss=carve(1), xn=carve(D, BF16), ss2=[carve(4), carve(4)], tmp=[carve(D), carve(D)]); wk["junk"] = wk["xn"]
    pj_ = carve(D, BF16); wk["pjunk"] = [pj_, pj_]
    NH = 768
    JW = 256
    hT2 = [carve(8 * NH, BF16), carve(8 * NH, BF16)]
    hTv2 = [x.rearrange("p (k t) -> p k t", k=8) for x in hT2]
    act = carve(NFF * NH, BF16); actv = act.rearrange("p (k t) -> p k t", k=NFF)
    wg = [carve(8 * JW, BF16), carve(8 * JW, BF16)]
    wu = [carve(8 * JW, BF16), carve(8 * JW, BF16)]
    wgv = [x.rearrange("p (k n) -> p k n", k=8) for x in wg]
    wuv = [x.rearrange("p (k n) -> p k n", k=8) for x in wu]
    sgt = carve(512); gt = carve(512)
    groups = [[(i, 128) for i in range(0, 6)], [(i, 128) for i in range(6, 12)], [(i, 128) for i in range(12, 16)] + [(16, TS)]]
    wstep = 0
    NJJ = NFF * 128 // JW
    loaded = set()

    def load_step(s):
        if s in loaded or s >= NJJ * len(groups):
            return
        loaded.add(s)
        z_, jj_ = s % 2, s % NJJ
        P.dma("pool", wgv[z_], w_gate[:, jj_ * JW:(jj_ + 1) * JW].rearrange("(k p) n -> p k n", p=128), writes=["wg%d" % z_])
        P.dma("pool", wuv[z_], w_up[:, jj_ * JW:(jj_ + 1) * JW].rearrange("(k p) n -> p k n", p=128), writes=["wu%d" % z_])
        if 1 <= s <= 8:
            k0 = (s - 1) * 3
            k1 = min(k0 + 3, NFF)
            P.dma("pool", Wdv[:, k0:k1, :], w_down[k0 * 128:k1 * 128, :].rearrange("(k p) n -> p k n", p=128),
                  reads=["Wd"], writes=["Wd"])

    load_step(0)
    for i, (t_, n) in enumerate(groups[0]):
        wk["xkeys"] = ["xr%d" % t_]; wk["hkeys"] = ["hTf0"]
        norm_to_hT(xresv[:n, t_, :], n, 2, hTv2[0], i * 128, wk)
    for gi_, tiles in enumerate(groups):
        zg = gi_ % 2
        hTv_, hk_ = hTv2[zg], "hTf%d" % zg
        ntok = sum(n for _, n in tiles)
        blocks = [(c0, min(512, ntok - c0)) for c0 in range(0, ntok, 512)]
        for jj in range(NFF * 128 // JW):
            z = wstep % 2
            load_step(wstep)
            load_step(wstep + 1)
            wstep += 1
            for sub in range(JW // 128):
                j = jj * (JW // 128) + sub
                for (c0, n) in blocks:
                    bg = proj_fm(wgv[z], 8, sub * 128, hTv_, c0, n, [hk_], "wg%d" % z)
                    bu = proj_fm(wuv[z], 8, sub * 128, hTv_, c0, n, [hk_], "wu%d" % z)
                    P.op("act", lambda e, bg=bg, n=n: e.activation(sgt[:, 0:n], psb[bg][:, 0:n], AF.Sigmoid), reads=[pk[bg], "sgt"], writes=["sgt"])
                    P.op("dve", lambda e, bg=bg, n=n: e.tensor_tensor(gt[:, 0:n], psb[bg][:, 0:n], sgt[:, 0:n], ALU.mult), reads=[pk[bg], "sgt", "gt"], writes=["gt"])
                    P.op("dve", lambda e, bu=bu, n=n, j=j, c0=c0: e.tensor_tensor(actv[:, j, c0:c0 + n], gt[:, 0:n], psb[bu][:, 0:n], ALU.mult),
                         reads=[pk[bu], "gt", "act"], writes=["act"])
        nxt_tiles = groups[gi_ + 1] if gi_ + 1 < len(groups) else []
        zn = (gi_ + 1) % 2
        for i, (t_, n) in enumerate(tiles):
            od = yp[t_ * 128:(t_ + 1) * 128, :] if t_ < 16 else ys
            hook = None
            if i < len(nxt_tiles):
                t2, n2 = nxt_tiles[i]
                wk["xkeys"] = ["xr%d" % t2]; wk["hkeys"] = ["hTf%d" % zn]
                norm_pre(xresv[:n2, t2, :], n2, wk)
                hook = (lambda i=i, n2=n2, zn=zn: norm_T(n2, 2, hTv2[zn], i * 128, wk))
            proj_tm_postnorm_residual(actv, NFF, i * 128, n, Wdv, "Wd", ["act"], 2, xresv[:n, t_, :], "xr%d" % t_, wk,
                                      out_dram=od, okey="o_y%d" % (t_ % 4), mid_hook=hook)
        for i in range(len(tiles), len(nxt_tiles)):
            t2, n2 = nxt_tiles[i]
            wk["xkeys"] = ["xr%d" % t2]; wk["hkeys"] = ["hTf%d" % zn]
            norm_to_hT(xresv[:n2, t2, :], n2, 2, hTv2[zn], i * 128, wk)
    return finish()


_NC_CACHE = {}


def _shard_inputs(inp):
    f = lambda a: np.ascontiguousarray(a, dtype=np.float32)
    shared = {
        "norm_g": f(inp["norm_g"][0]), "mem_g": f(inp["mem_norm_g"][0]), "w_in": f(inp["w_in"][0]),
        "w_dw": f(inp["w_dw"][0]), "b_dw": f(inp["b_dw"][0]), "ln_g": f(inp["ln_g"][0]), "ln_b": f(inp["ln_b"][0]),
        "lam_re": f(inp["lam_re"][0]), "lam_im": f(inp["lam_im"][0]), "log_dt": f(inp["log_dt"][0]),
        "b_re": f(inp["b_re"][0]), "b_im": f(inp["b_im"][0]), "c_re": f(inp["c_re"][0]), "c_im": f(inp["c_im"][0]),
        "d_skip": f(inp["d_skip"][0]), "w_glu": f(inp["w_glu"][0]), "w_out": f(inp["w_out"][0]),
        "w_q": f(inp["w_q"][0]), "w_k": f(inp["w_k"][0]), "w_v": f(inp["w_v"][0]), "w_o": f(inp["w_o"][0]),
        "w_gate": f(inp["w_gate"][0]), "w_up": f(inp["w_up"][0]), "w_down": f(inp["w_down"][0]),
    }
    maps = []
    for c in range(NCORES):
        s0, s1 = c * NS, (c + 1) * NS
        m = dict(shared)
        m["xp"] = f(inp["x_prompt"][c])
        m["xs"] = f(inp["x_sample"][s0:s1]).reshape(TS, D)
        m["mem"] = f(inp["mem_prompt"][c])
        m["ck"] = f(inp["cache_mem_k"][0, s0:s1]).reshape(NS, 256, D)
        m["cv"] = f(inp["cache_mem_v"][0, s0:s1]).reshape(NS, 256, D)
        m["sconv"] = f(inp["state_conv"][0, s0:s1])
        m["sre"] = f(inp["state_ssm_re"][0, s0:s1]).reshape(NS, 2048)
        m["sim"] = f(inp["state_ssm_im"][0, s0:s1]).reshape(NS, 2048)
        maps.append(m)
    return maps


def kernel(**inputs):
    inp = {k: np.asarray(v) for k, v in inputs.items()}
    if "nc" not in _NC_CACHE:
        import os
        _NC_CACHE["nc"] = build(stop=int(os.environ.get("KSTOP", "99")))
    nc = _NC_CACHE["nc"]
    maps = _shard_inputs(inp)
    res = run_bass_kernel_spmd(nc, maps, core_ids=list(range(NCORES)))
    R = res.results
    cat = lambda k: np.stack([np.asarray(R[c][k], dtype=np.float32) for c in range(NCORES)], axis=0)
    y_p = cat("yp")
    y_s = cat("ys").reshape(128, 4, D)
    k_p = cat("kp").reshape(1, 8, 256, 4, 256)
    v_p = cat("vp").reshape(1, 8, 256, 4, 256)
    c_p = cat("cp").reshape(1, 8, 30, 512)
    h_pr = cat("hpr").reshape(1, 8, 32, 64)
    h_pi = cat("hpi").reshape(1, 8, 32, 64)
    c_s = cat("cs").reshape(1, 128, 30, 512)
    h_sr = cat("hsr").reshape(1, 128, 32, 64)
    h_si = cat("hsi").reshape(1, 128, 32, 64)
    return (y_p, y_s, k_p, v_p, c_p, h_pr, h_pi, c_s, h_sr, h_si)
```
